# Optimizing a Trainium2 kernel written in Bass

```python
import math
import jax, jax.numpy as jnp
from jax import lax
import numpy as np

D_MODEL = 2048
BATCH = 8
SEQ = 2048
DEPTH = 2

DA_HEADS = 8
DA_HEAD_DIM = 64
DA_WIDTH = DA_HEADS * 2 * DA_HEAD_DIM
DA_Q_BLOCK = 128
SSM_HEADS = 16
SSM_HEAD_DIM = 64
SSM_INNER = SSM_HEADS * SSM_HEAD_DIM
SSM_GROUPS = 2
SSM_STATE = 128
SSM_CONV = 4
SSM_CHUNK = 128
SSM_CONV_CH = SSM_INNER + 2 * SSM_GROUPS * SSM_STATE
GDN_HEADS = 8
GDN_HEAD_DIM = 128
GDN_WIDTH = GDN_HEADS * GDN_HEAD_DIM
GDN_CONV = 4
GDN_CHUNK = 64
D_FF = 5632
FFN_CONV = 3
N_BRANCH = 3
RMS_EPS = 1e-6

IN_SIZES = (DA_WIDTH, DA_WIDTH, DA_WIDTH,
            SSM_INNER, SSM_CONV_CH, SSM_HEADS,
            3 * GDN_WIDTH, GDN_WIDTH, GDN_HEADS, GDN_HEADS,
            N_BRANCH * D_MODEL)
IN_SPLITS = tuple(sum(IN_SIZES[:i + 1]) for i in range(len(IN_SIZES) - 1))
IN_COLS = sum(IN_SIZES)
BRANCH_ROWS = DA_WIDTH + SSM_INNER + GDN_WIDTH

kernel_name = "hybrid_diffattn_mamba2_gdn_convffn"


def rms_norm(x, w, eps=RMS_EPS):
    xf = x.astype(jnp.float32)
    y = xf * lax.rsqrt(jnp.mean(xf * xf, axis=-1, keepdims=True) + eps)
    return (y * w.astype(jnp.float32)).astype(x.dtype)


def l2_normalize(x, eps=1e-6):
    xf = x.astype(jnp.float32)
    return xf * lax.rsqrt(jnp.sum(xf * xf, axis=-1, keepdims=True) + eps)


def causal_dwconv(x, w, b=None):
    K, C = w.shape
    y = lax.conv_general_dilated(x, w[:, None, :].astype(x.dtype), window_strides=(1,),
                                 padding=[(K - 1, 0)], dimension_numbers=('NWC', 'WIO', 'NWC'),
                                 feature_group_count=C)
    if b is not None:
        y = y + b.astype(x.dtype)
    return y


def alibi_slopes(n_heads):
    return jnp.asarray([2.0 ** (-8.0 * (h + 1) / n_heads) for h in range(n_heads)], dtype=jnp.float32)


def diff_attention(q_raw, k_raw, v_raw, lam_params, subln_w, lambda_init):
    Bsz, L, _ = q_raw.shape
    H, d = DA_HEADS, DA_HEAD_DIM
    q = q_raw.reshape(Bsz, L, H, 2, d) * (d ** -0.5)
    k = k_raw.reshape(Bsz, L, H, 2, d)
    v = v_raw.reshape(Bsz, L, H, 2 * d)
    lp = lam_params.astype(jnp.float32)
    lam = jnp.exp(jnp.sum(lp[0] * lp[1])) - jnp.exp(jnp.sum(lp[2] * lp[3])) + lambda_init
    slopes = alibi_slopes(H)
    kpos = jnp.arange(L, dtype=jnp.int32)
    nb = L // DA_Q_BLOCK
    q_blocks = jnp.moveaxis(q.reshape(Bsz, nb, DA_Q_BLOCK, H, 2, d), 1, 0)
    starts = jnp.arange(nb, dtype=jnp.int32) * DA_Q_BLOCK

    def block(args):
        qb, start = args
        qpos = start + jnp.arange(DA_Q_BLOCK, dtype=jnp.int32)
        dist = (qpos[:, None] - kpos[None, :]).astype(jnp.float32)
        s = jnp.einsum('bqhid,bkhid->bhiqk', qb, k).astype(jnp.float32)
        s = s - slopes[None, :, None, None, None] * dist
        s = jnp.where(dist >= 0, s, -jnp.inf)
        p = jax.nn.softmax(s, axis=-1)
        a = (p[:, :, 0] - lam * p[:, :, 1]).astype(v.dtype)
        return jnp.einsum('bhqk,bkhe->bqhe', a, v)

    o = lax.map(block, (q_blocks, starts))
    o = jnp.moveaxis(o, 0, 1).reshape(Bsz, L, H, 2 * d)
    o = rms_norm(o, subln_w) * (1.0 - lambda_init)
    return o.reshape(Bsz, L, DA_WIDTH)


def mamba2_ssd(xbc_raw, z, dt_raw, conv_w, conv_b, dt_bias, a_log, d_skip, norm_w):
    Bsz, L, _ = xbc_raw.shape
    f32 = jnp.float32
    Q, H, P, G, N = SSM_CHUNK, SSM_HEADS, SSM_HEAD_DIM, SSM_GROUPS, SSM_STATE
    nc = L // Q
    xbc = jax.nn.silu(causal_dwconv(xbc_raw, conv_w, conv_b)).astype(f32)
    xs, Bm, Cm = jnp.split(xbc, [SSM_INNER, SSM_INNER + G * N], axis=-1)
    x = xs.reshape(Bsz, nc, Q, H, P)
    Bm = jnp.repeat(Bm.reshape(Bsz, nc, Q, G, N), H // G, axis=3)
    Cm = jnp.repeat(Cm.reshape(Bsz, nc, Q, G, N), H // G, axis=3)
    dt = jax.nn.softplus(dt_raw.astype(f32) + dt_bias.astype(f32)).reshape(Bsz, nc, Q, H)
    A = -jnp.exp(a_log.astype(f32))
    a_cs = jnp.cumsum(dt * A, axis=2)
    xdt = x * dt[..., None]
    causal = jnp.tril(jnp.ones((Q, Q), dtype=bool))
    seg = a_cs[:, :, :, None, :] - a_cs[:, :, None, :, :]
    decay = jnp.exp(jnp.where(causal[None, None, :, :, None], seg, -jnp.inf))
    scores = jnp.einsum('bclhn,bcshn->bclsh', Cm, Bm) * decay
    y_diag = jnp.einsum('bclsh,bcshp->bclhp', scores, xdt)
    decay_to_end = jnp.exp(a_cs[:, :, -1:, :] - a_cs)
    states = jnp.einsum('bclhn,bclh,bclhp->bchpn', Bm, decay_to_end, xdt)
    chunk_decay = jnp.exp(a_cs[:, :, -1, :])

    def step(S, inp):
        st, dec = inp
        return S * dec[:, :, None, None] + st, S

    _, prev = lax.scan(step, jnp.zeros((Bsz, H, P, N), f32),
                       (jnp.moveaxis(states, 1, 0), jnp.moveaxis(chunk_decay, 1, 0)))
    prev = jnp.moveaxis(prev, 0, 1)
    y_off = jnp.einsum('bclhn,bchpn,bclh->bclhp', Cm, prev, jnp.exp(a_cs))
    y = y_diag + y_off + x * d_skip.astype(f32)[:, None]
    y = y.reshape(Bsz, L, SSM_INNER) * jax.nn.silu(z.astype(f32))
    y = rms_norm(y.reshape(Bsz, L, G, SSM_INNER // G),
                 norm_w.reshape(G, SSM_INNER // G)).reshape(Bsz, L, SSM_INNER)
    return y.astype(xbc_raw.dtype)


def gated_deltanet(qkv_raw, z, b_raw, a_raw, conv_w, dt_bias, a_log, norm_w):
    Bsz, L, _ = qkv_raw.shape
    f32 = jnp.float32
    C, H, D = GDN_CHUNK, GDN_HEADS, GDN_HEAD_DIM
    nc = L // C
    qkv = jax.nn.silu(causal_dwconv(qkv_raw, conv_w))
    q, k, v = jnp.split(qkv, 3, axis=-1)
    q = l2_normalize(q.reshape(Bsz, L, H, D)) * (D ** -0.5)
    k = l2_normalize(k.reshape(Bsz, L, H, D))
    v = v.reshape(Bsz, L, H, D).astype(f32)
    beta = jax.nn.sigmoid(b_raw.astype(f32))
    g = -jnp.exp(a_log.astype(f32)) * jax.nn.softplus(a_raw.astype(f32) + dt_bias.astype(f32))

    def chunked(t):
        t = t.reshape((Bsz, nc, C, H) + t.shape[3:])
        return jnp.moveaxis(t, 3, 1)

    q, k, v, beta, g = chunked(q), chunked(k), chunked(v), chunked(beta), chunked(g)
    g = jnp.cumsum(g, axis=-1)
    k_beta = k * beta[..., None]
    v_beta = v * beta[..., None]
    incl = jnp.tril(jnp.ones((C, C), dtype=bool))
    strict = jnp.tril(jnp.ones((C, C), dtype=bool), k=-1)
    gdiff = g[..., :, None] - g[..., None, :]
    decay = jnp.where(incl, jnp.exp(jnp.where(incl, gdiff, 0.0)), 0.0)
    A = jnp.where(strict, jnp.einsum('bhcld,bhcsd->bhcls', k_beta, k) * decay, 0.0)
    lhs = A + jnp.eye(C, dtype=f32)
    rhs = jnp.concatenate([v_beta, k_beta * jnp.exp(g)[..., None]], axis=-1)
    sol = lax.linalg.triangular_solve(lhs, rhs, left_side=True, lower=True, unit_diagonal=True)
    u, w = jnp.split(sol, 2, axis=-1)

    def step(S, inp):
        qc, kc, uc, wc, gc, dc = inp
        attn = jnp.einsum('bhld,bhsd->bhls', qc, kc) * dc
        v_new = uc - jnp.einsum('bhld,bhdv->bhlv', wc, S)
        o = (jnp.einsum('bhld,bhdv->bhlv', qc * jnp.exp(gc)[..., None], S)
             + jnp.einsum('bhls,bhsv->bhlv', attn, v_new))
        g_last = gc[..., -1]
        S = (S * jnp.exp(g_last)[..., None, None]
             + jnp.einsum('bhld,bhlv->bhdv', kc * jnp.exp(g_last[..., None] - gc)[..., None], v_new))
        return S, o

    xs = tuple(jnp.moveaxis(t, 2, 0) for t in (q, k, u, w, g, decay))
    _, o = lax.scan(step, jnp.zeros((Bsz, H, D, D), f32), xs)
    o = jnp.transpose(o, (1, 0, 3, 2, 4)).reshape(Bsz, L, H, D)
    o = rms_norm(o, norm_w) * jax.nn.silu(z.reshape(Bsz, L, H, D).astype(f32))
    return o.reshape(Bsz, L, GDN_WIDTH).astype(qkv_raw.dtype)


def setup_inputs(seed: int = 0) -> dict:
    key = jax.random.key(seed)
    ks = jax.random.split(key, 24)
    f32 = jnp.float32

    def nrm(k, shape, scale):
        return jax.random.normal(k, shape, f32) * scale

    def gain(k, shape):
        return 1.0 + 0.01 * jax.random.normal(k, shape, f32)

    def dt_bias_init(k, n):
        dt = jnp.exp(jax.random.uniform(k, (DEPTH, n), f32, math.log(1e-3), math.log(1e-1)))
        return dt + jnp.log(-jnp.expm1(-dt))

    def a_log_init(k, n):
        return jnp.log(jax.random.uniform(k, (DEPTH, n), f32, 1.0, 16.0))

    return {
        "x": nrm(ks[0], (BATCH, SEQ, D_MODEL), 1.0),
        "norm_mix": gain(ks[1], (DEPTH, D_MODEL)),
        "w_in": nrm(ks[2], (DEPTH, D_MODEL, IN_COLS), D_MODEL ** -0.5),
        "da_lambda": nrm(ks[3], (DEPTH, 4, DA_HEAD_DIM), 0.1),
        "da_subln": gain(ks[4], (DEPTH, 2 * DA_HEAD_DIM)),
        "ssm_conv_w": nrm(ks[5], (DEPTH, SSM_CONV, SSM_CONV_CH), SSM_CONV ** -0.5),
        "ssm_conv_b": nrm(ks[6], (DEPTH, SSM_CONV_CH), 0.02),
        "ssm_dt_bias": dt_bias_init(ks[7], SSM_HEADS),
        "ssm_a_log": a_log_init(ks[8], SSM_HEADS),
        "ssm_d": gain(ks[9], (DEPTH, SSM_HEADS)),
        "ssm_norm": gain(ks[10], (DEPTH, SSM_INNER)),
        "gdn_conv_w": nrm(ks[11], (DEPTH, GDN_CONV, 3 * GDN_WIDTH), GDN_CONV ** -0.5),
        "gdn_dt_bias": dt_bias_init(ks[12], GDN_HEADS),
        "gdn_a_log": a_log_init(ks[13], GDN_HEADS),
        "gdn_norm": gain(ks[14], (DEPTH, GDN_HEAD_DIM)),
        "w_branch": nrm(ks[15], (DEPTH, BRANCH_ROWS, D_MODEL), DA_WIDTH ** -0.5),
        "w_out": nrm(ks[16], (DEPTH, D_MODEL, D_MODEL), D_MODEL ** -0.5),
        "norm_ffn": gain(ks[17], (DEPTH, D_MODEL)),
        "ffn_up": nrm(ks[18], (DEPTH, D_MODEL, 2 * D_FF), D_MODEL ** -0.5),
        "ffn_conv_w": nrm(ks[19], (DEPTH, FFN_CONV, 2 * D_FF), FFN_CONV ** -0.5),
        "ffn_conv_b": nrm(ks[20], (DEPTH, 2 * D_FF), 0.02),
        "ffn_down": nrm(ks[21], (DEPTH, D_FF, D_MODEL), D_FF ** -0.5),
        "norm_final": gain(ks[22], (D_MODEL,)),
    }


def reference(x, norm_mix, w_in, da_lambda, da_subln, ssm_conv_w, ssm_conv_b, ssm_dt_bias,
              ssm_a_log, ssm_d, ssm_norm, gdn_conv_w, gdn_dt_bias, gdn_a_log, gdn_norm,
              w_branch, w_out, norm_ffn, ffn_up, ffn_conv_w, ffn_conv_b, ffn_down, norm_final):
    Bsz, L, _ = x.shape
    h = x
    for l in range(DEPTH):
        xn = rms_norm(h, norm_mix[l])
        (da_q, da_k, da_v, ssm_z, ssm_xbc, ssm_dt, gdn_qkv, gdn_z, gdn_b, gdn_a,
         gates) = jnp.split(xn @ w_in[l], IN_SPLITS, axis=-1)
        lambda_init = 0.8 - 0.6 * math.exp(-0.3 * l)
        o_da = diff_attention(da_q, da_k, da_v, da_lambda[l], da_subln[l], lambda_init)
        o_ssm = mamba2_ssd(ssm_xbc, ssm_z, ssm_dt, ssm_conv_w[l], ssm_conv_b[l], ssm_dt_bias[l],
                           ssm_a_log[l], ssm_d[l], ssm_norm[l])
        o_gdn = gated_deltanet(gdn_qkv, gdn_z, gdn_b, gdn_a, gdn_conv_w[l], gdn_dt_bias[l],
                               gdn_a_log[l], gdn_norm[l])
        wb = w_branch[l]
        gate = jax.nn.sigmoid(gates.reshape(Bsz, L, N_BRANCH, D_MODEL).astype(jnp.float32)).astype(h.dtype)
        merged = (gate[:, :, 0] * (o_da @ wb[:DA_WIDTH])
                  + gate[:, :, 1] * (o_ssm @ wb[DA_WIDTH:DA_WIDTH + SSM_INNER])
                  + gate[:, :, 2] * (o_gdn @ wb[DA_WIDTH + SSM_INNER:]))
        h = h + merged @ w_out[l]
        hn = rms_norm(h, norm_ffn[l])
        u = causal_dwconv(hn @ ffn_up[l], ffn_conv_w[l], ffn_conv_b[l])
        u_gate, u_val = jnp.split(u, 2, axis=-1)
        h = h + (jax.nn.silu(u_gate) * u_val) @ ffn_down[l]
    return rms_norm(h, norm_final)
```

```python
import numpy as np
import ml_dtypes
import concourse.bass as bass
import concourse.mybir as mybir
from concourse.bass_utils import run_bass_kernel_spmd

F32 = mybir.dt.float32
BF16 = mybir.dt.bfloat16
AF = mybir.ActivationFunctionType
ALU = mybir.AluOpType
AX = mybir.AxisListType

ENGS = ['pe', 'act', 'dve', 'pool', 'sp']
NSEM_DMA = 14


class Tok:
    __slots__ = ('name', 'writers', 'readers', 'prev_readers')

    def __init__(self, name=''):
        self.name = name
        self.writers = []
        self.readers = []
        self.prev_readers = []


class _Op:
    __slots__ = ('eng', 'fn', 'waits', 'idx', 'dma', 'dma_k', 'target', 'val')

    def __init__(self, eng, fn, idx, dma):
        self.eng = eng
        self.fn = fn
        self.idx = idx
        self.dma = dma
        self.dma_k = None
        self.waits = []
        self.target = False
        self.val = None


class Sched:
    def __init__(self, nc):
        self.nc = nc
        self.ops = {e: [] for e in ENGS}
        self.ndma = {e: 0 for e in ENGS}
        self.dma_ops = {e: [] for e in ENGS}
        self.wc = {e: {} for e in ENGS}
        self.wd = {e: set() for e in ENGS}
        self.pending = {e: [] for e in ENGS}

    def barrier(self):
        evs = []
        for e in ENGS:
            comp = [o for o in self.ops[e] if not o.dma]
            if comp:
                evs.append(('c', e, comp[-1].idx))
            for o in self.dma_ops[e][-NSEM_DMA:]:
                evs.append(('d', e, o.dma_k))
        for e in ENGS:
            self.pending[e] = list(evs)

    def _add_wait(self, op, ev):
        e = op.eng
        if ev[0] == 'c':
            src, idx = ev[1], ev[2]
            if src == e:
                if e == 'pe':
                    return
                if op.dma:
                    pass
                elif op.idx - idx > 3:
                    return
            if self.wc[e].get(src, -1) >= idx:
                return
            self.wc[e][src] = idx
            op.waits.append(ev)
        else:
            if ev in self.wd[e]:
                return
            self.wd[e].add(ev)
            op.waits.append(ev)

    def op(self, eng, fn, r=(), w=(), pw=(), dma=False):
        lst = self.ops[eng]
        o = _Op(eng, fn, len(lst), dma)
        if dma:
            k = self.ndma[eng]
            self.ndma[eng] += 1
            o.dma_k = k
            if k >= NSEM_DMA:
                self._add_wait(o, ('d', eng, k - NSEM_DMA))
            ev = ('d', eng, k)
            self.dma_ops[eng].append(o)
        else:
            ev = ('c', eng, o.idx)
        if self.pending[eng]:
            for pe_ in self.pending[eng]:
                self._add_wait(o, pe_)
            self.pending[eng] = []
        for t in r:
            for we in t.writers:
                self._add_wait(o, we)
        for t in w:
            for we in t.writers:
                self._add_wait(o, we)
            for re_ in t.readers:
                self._add_wait(o, re_)
            for re_ in t.prev_readers:
                self._add_wait(o, re_)
        for t in pw:
            if t.readers:
                t.prev_readers = t.readers
                t.readers = []
                t.writers = []
            for re_ in t.prev_readers:
                self._add_wait(o, re_)
        for t in r:
            t.readers.append(ev)
            if len(t.readers) > 64:
                t.readers = _compact(t.readers)
        for t in w:
            t.writers = [ev]
            t.readers = []
            t.prev_readers = []
        for t in pw:
            t.writers.append(ev)
            if len(t.writers) > 64:
                t.writers = _compact(t.writers)
        lst.append(o)
        return o

    def finalize_and_emit(self, final_waits=()):
        nc = self.nc
        for e in ENGS:
            for o in self.ops[e]:
                for ev in o.waits:
                    if ev[0] == 'c':
                        self.ops[ev[1]][ev[2]].target = True
        fin = []
        for e in ENGS:
            comp = [o for o in self.ops[e] if not o.dma]
            if comp:
                comp[-1].target = True
                fin.append(('c', e, comp[-1].idx))
            for o in self.dma_ops[e][-NSEM_DMA:]:
                fin.append(('d', e, o.dma_k))
        for e in ENGS:
            c = 0
            for o in self.ops[e]:
                if o.dma:
                    continue
                if o.target:
                    c += 1
                o.val = c
        sems = {}
        dsems = {}
        import contextlib
        with contextlib.ExitStack() as st:
            for e in ENGS:
                sems[e] = st.enter_context(nc.semaphore('s_' + e))
                if self.ndma[e]:
                    dsems[e] = [st.enter_context(nc.semaphore('d_%s_%d' % (e, i))) for i in range(NSEM_DMA)]
            block = st.enter_context(nc.Block())

            def wait_ev(engh, ev):
                if ev[0] == 'c':
                    engh.wait_ge(sems[ev[1]], self.ops[ev[1]][ev[2]].val)
                else:
                    k = ev[2]
                    engh.wait_ge(dsems[ev[1]][k % NSEM_DMA], 16 * (k // NSEM_DMA + 1))

            def emit(e, engh):
                for o in self.ops[e]:
                    for ev in o.waits:
                        wait_ev(engh, ev)
                    ins = o.fn(engh)
                    if o.dma:
                        ins.then_inc(dsems[e][o.dma_k % NSEM_DMA], 16)
                    elif o.target:
                        ins.then_inc(sems[e], 1)
                if e == 'sp':
                    for ev in fin:
                        wait_ev(engh, ev)

            @block.tensor
            def _(h):
                emit('pe', h)

            @block.scalar
            def _(h):
                emit('act', h)

            @block.vector
            def _(h):
                emit('dve', h)

            @block.gpsimd
            def _(h):
                emit('pool', h)

            @block.sync
            def _(h):
                emit('sp', h)
        return {e: len(self.ops[e]) for e in ENGS}


def _compact(evs):
    best = {}
    out = []
    for ev in evs:
        if ev[0] == 'c':
            if best.get(ev[1], -1) < ev[2]:
                best[ev[1]] = ev[2]
        else:
            out.append(ev)
    return [('c', e, i) for e, i in best.items()] + out


class KB:
    def __init__(self, nc):
        self.nc = nc
        self.s = Sched(nc)
        self._n = 0

    def sb(self, shape, dt, name=None):
        self._n += 1
        return self.nc.alloc_sbuf_tensor(name or ('t%d' % self._n), list(shape), dt)

    def ps(self, shape, dt, name=None):
        self._n += 1
        return self.nc.alloc_psum_tensor(name or ('p%d' % self._n), list(shape), dt)

    def dram(self, name, shape, dt, kind="Internal"):
        return self.nc.dram_tensor(name, list(shape), dt, kind=kind).ap()

    def dma(self, q, out, in_, r=(), w=(), pw=()):
        return self.s.op(q, lambda e: e.dma_start(out=out, in_=in_), r=r, w=w, pw=pw, dma=True)

    def mm(self, out, lhsT, rhs, start=True, stop=True, r=(), w=(), pw=()):
        return self.s.op('pe', lambda e: e.matmul(out, lhsT, rhs, start=start, stop=stop), r=r, w=w, pw=pw)

    def tr(self, out, in_, ident, r=(), w=(), pw=()):
        return self.s.op('pe', lambda e: e.transpose(out, in_, ident), r=r, w=w, pw=pw)

    def act(self, out, in_, func, bias=None, scale=None, accum_out=None, r=(), w=(), pw=(), eng='act'):
        kw = {}
        if bias is not None:
            kw['bias'] = bias
        if scale is not None:
            kw['scale'] = scale
        if accum_out is not None:
            kw['accum_out'] = accum_out
        return self.s.op('act', lambda e: e.activation(out, in_, func, **kw), r=r, w=w, pw=pw)

    def copy(self, eng, out, in_, r=(), w=(), pw=()):
        if eng == 'act':
            return self.s.op('act', lambda e: e.copy(out, in_), r=r, w=w, pw=pw)
        return self.s.op(eng, lambda e: e.tensor_copy(out, in_), r=r, w=w, pw=pw)

    def tt(self, eng, out, in0, in1, op, r=(), w=(), pw=()):
        return self.s.op(eng, lambda e: e.tensor_tensor(out, in0, in1, op), r=r, w=w, pw=pw)

    def ts(self, eng, out, in0, s1, s2, op0, op1=None, accum_out=None, r=(), w=(), pw=()):
        def f(e):
            kw = {}
            if accum_out is not None:
                kw['accum_out'] = accum_out
            if op1 is None:
                return e.tensor_scalar(out, in0, s1, None, op0, **kw)
            return e.tensor_scalar(out, in0, s1, s2, op0, op1, **kw)
        return self.s.op(eng, f, r=r, w=w, pw=pw)

    def stt(self, eng, out, in0, scalar, in1, op0, op1, accum_out=None, r=(), w=(), pw=()):
        def f(e):
            kw = {}
            if accum_out is not None:
                kw['accum_out'] = accum_out
            return e.scalar_tensor_tensor(out, in0, scalar, in1, op0, op1, **kw)
        return self.s.op(eng, f, r=r, w=w, pw=pw)

    def rsqrt_eps(self, out, in_, t_in, t_out, eps=1e-6):
        self.s.op('act', lambda e: e.activation(out, in_, AF.Ln, bias=float(eps)), r=[t_in], w=[t_out])
        self.s.op('act', lambda e: e.activation(out, out, AF.Exp, scale=-0.5), r=[t_out], w=[t_out])

    def memset(self, eng, ap, val, r=(), w=(), pw=()):
        return self.s.op(eng, lambda e: e.memset(ap, val), r=r, w=w, pw=pw)


class Arena:
    def __init__(self, kb, base, limit):
        self.kb = kb
        self.base = base
        self.off = base
        self.limit = limit
        self.n = 0

    def reset(self):
        self.off = self.base

    def alloc(self, shape, dt, name=None):
        nb = int(np.prod(shape[1:])) * (4 if dt == F32 else 2)
        nb = (nb + 31) // 32 * 32
        self.n += 1
        h = self.kb.nc.alloc_sbuf_tensor_at(name or ('a%d' % self.n), list(shape), dt, offset=self.off)
        self.off += nb
        assert self.off <= self.limit, ('SBUF arena overflow', self.off, self.limit)
        return h


class GemmRes:
    def __init__(self, kb, arena, next_bank, elems=16 * 512):
        self.elems = elems
        self.wst = [arena.alloc([128, elems], F32) for _ in range(2)]
        self.wbf = [arena.alloc([128, elems], BF16) for _ in range(2)]
        self.t_wst = [Tok('wst%d' % i) for i in range(2)]
        self.t_wbf = [Tok('wbf%d' % i) for i in range(2)]
        self.next_bank = next_bank


def run_gemms(kb, G, xT, t_xT, L, tiles):
    n = len(tiles)

    def view(buf, KC, wd):
        return buf[:, 0:KC * wd].rearrange("p (k c) -> p k c", k=KC)

    def load(j):
        tl = tiles[j]
        KC, wd = tl['KC'], tl['width']
        assert KC * wd <= G.elems
        buf, tk = view(G.wst[j % 2], KC, wd), G.t_wst[j % 2]
        segs = tl.get('segs') or [(tl['c0'], wd)]
        step = 4
        first = True
        off = 0
        for (c0, w_) in segs:
            for a in range(0, KC, step):
                b = min(KC, a + step)
                src = tl['W'][tl['k0'] + a * 128: tl['k0'] + b * 128, c0:c0 + w_].rearrange("(kc p) c -> p kc c", p=128)
                if first:
                    kb.dma('sp', buf[:, a:b, off:off + w_], src, w=[tk])
                    first = False
                else:
                    kb.dma('sp', buf[:, a:b, off:off + w_], src, pw=[tk])
            off += w_

    def cast(j):
        tl = tiles[j]
        KC, wd = tl['KC'], tl['width']
        src, dst = view(G.wst[j % 2], KC, wd), view(G.wbf[j % 2], KC, wd)
        ce = tl.get('cast', 'pool')
        if tl.get('nscale') is not None:
            ns = tl['nscale']
            if ce == 'act':
                for kc in range(KC):
                    kb.act(dst[:, kc, :], src[:, kc, :], AF.Copy, scale=ns[:, kc:kc + 1], r=[G.t_wst[j % 2], tl['t_nscale']],
                           **({'w': [G.t_wbf[j % 2]]} if kc == 0 else {'pw': [G.t_wbf[j % 2]]}))
            else:
                kb.tt('pool', dst, src, ns[:, 0:KC].unsqueeze(2).to_broadcast([128, KC, wd]), ALU.mult,
                      r=[G.t_wst[j % 2], tl['t_nscale']], w=[G.t_wbf[j % 2]])
        else:
            kb.copy(ce, dst, src, r=[G.t_wst[j % 2]], w=[G.t_wbf[j % 2]])

    load(0)
    if n > 1:
        load(1)
    cast(0)
    for j in range(n):
        tl = tiles[j]
        KC, wd = tl['KC'], tl['width']
        if tl.get('pre') is not None:
            tl['pre']()
        if j + 1 < n:
            cast(j + 1)
        if j + 2 < n:
            load(j + 2)
        wb, twb = view(G.wbf[j % 2], KC, wd), G.t_wbf[j % 2]
        kx = tl.get('kx0', 0)
        x_, tx_ = tl.get('xT', xT), tl.get('t_xT', t_xT)
        if tl['mode'] == 'tm':
            for t in range(L // 128):
                ps, tps = G.next_bank()
                for kc in range(KC):
                    kb.mm(ps[:, 0:wd], x_[:, kx + kc, t * 128:(t + 1) * 128], wb[:, kc, 0:wd],
                          start=(kc == 0), stop=(kc == KC - 1), r=[tx_, twb], pw=[tps])
                tl['consume'](ps, tps, tl, 0, t)
        else:
            for sub in range(wd // 128):
                for tb in range(L // 512):
                    ps, tps = G.next_bank()
                    for kc in range(KC):
                        kb.mm(ps[:, 0:512], wb[:, kc, sub * 128:(sub + 1) * 128], x_[:, kx + kc, tb * 512:(tb + 1) * 512],
                              start=(kc == 0), stop=(kc == KC - 1), r=[tx_, twb], pw=[tps])
                    tl['consume'](ps, tps, tl, sub, tb)


D = 2048
DFF = 5632
SEG = dict(da_q=(0, 1024), da_k=(1024, 1024), da_v=(2048, 1024), ssm_z=(3072, 1024), ssm_xbc=(4096, 1536),
           ssm_dt=(5632, 16), gdn_qkv=(5648, 3072), gdn_z=(8720, 1024), gdn_ba=(9744, 16), gates=(9760, 6144))
EPS = 1e-6


class Evac:
    def __init__(self, kb, arena, n=4):
        self.kb = kb
        self.f = [arena.alloc([128, 512], F32) for _ in range(n)]
        self.tf = [Tok('evf%d' % i) for i in range(n)]
        self.b = [arena.alloc([128, 512], BF16) for _ in range(n)]
        self.tb = [Tok('evb%d' % i) for i in range(n)]
        self.i = 0
        self.n = n

    def get(self, dt):
        i = self.i % self.n
        self.i += 1
        if dt == F32:
            return self.f[i], self.tf[i]
        return self.b[i], self.tb[i]

    def eng(self):
        return 'act' if (self.i % 2 == 0) else 'dve'


def store_consumer(kb, EV, dst, dt, func=None, q='sp'):
    def consume(ps, tps, tl, sub, tb):
        wd = tl['width']
        st, tst = EV.get(dt)
        if tl['mode'] == 'tm':
            n = wd
            d = dst[tb * 128:(tb + 1) * 128, tl['doff']:tl['doff'] + wd]
        else:
            n = 512
            f0 = tl['doff'] + sub * 128
            d = dst[f0:f0 + 128, tb * 512:(tb + 1) * 512]
        if func is not None:
            kb.act(st[:, 0:n], ps[:, 0:n], func, r=[tps], w=[tst])
        else:
            kb.copy(EV.eng(), st[:, 0:n], ps[:, 0:n], r=[tps], w=[tst])
        kb.dma(q, d, st[:, 0:n], r=[tst])
    return consume


def phase_norm_T(kb, A, P, h_dram, xT, t_xT, L):
    ld = [A.alloc([128, D], F32) for _ in range(2)]
    t_ld = [Tok() for _ in range(2)]
    xn = [A.alloc([128, D], BF16) for _ in range(2)]
    t_xn = [Tok() for _ in range(2)]
    junk = A.alloc([128, D], BF16)
    t_junk = Tok()
    ssq = [A.alloc([128, 1], F32) for _ in range(2)]
    t_ssq = [Tok() for _ in range(2)]
    rstd = [A.alloc([128, 1], F32) for _ in range(2)]
    t_rstd = [Tok() for _ in range(2)]
    for t in range(L // 128):
        b = t % 2
        kb.dma('sp', ld[b][:, :], h_dram[t * 128:(t + 1) * 128, :], w=[t_ld[b]])
        kb.act(junk[:, :], ld[b][:, :], AF.Square, scale=float(D ** -0.5), accum_out=ssq[b][:, :], r=[t_ld[b]], w=[t_junk, t_ssq[b]])
        kb.rsqrt_eps(rstd[b][:, :], ssq[b][:, :], t_ssq[b], t_rstd[b])
        kb.ts('dve', xn[b][:, :], ld[b][:, :], rstd[b][:, :], None, ALU.mult, r=[t_ld[b], t_rstd[b]], w=[t_xn[b]])
        for g in range(2):
            ps, tps = P.next_bank()
            psb = ps[:, :].bitcast(BF16)
            for i in range(8):
                c = (g * 8 + i) * 128
                kb.tr(psb[:, i * 128:(i + 1) * 128], xn[b][:, c:c + 128], P.ident[:, :], r=[t_xn[b], P.t_const], pw=[tps])
            eng = 'dve' if g == 0 else 'act'
            kb.copy(eng, xT[:, g * 8:(g + 1) * 8, t * 128:(t + 1) * 128],
                    psb[:, 0:1024].rearrange("p (a b) -> p a b", a=8), r=[tps], pw=[t_xT])


class Persist:
    def __init__(self, kb, consts):
        self.kb = kb
        nc = kb.nc
        self.banks = []
        for i in range(8):
            self.banks.append((kb.ps([128, 512], F32, 'bank%d' % i), Tok('bank%d' % i)))
        self.bi = 0
        self.t_const = Tok('const')
        self.consts = consts

    def next_bank(self):
        b = self.banks[self.bi % 8]
        self.bi += 1
        return b


def phase_inproj(kb, A, P, xT, t_xT, L, w_in, nscale, t_nscale, scr):
    G = GemmRes(kb, A, P.next_bank)
    EV = Evac(kb, A)
    tiles = []

    def add(seg, mode, dst, dt, dbase=0, func=None):
        c0, n = SEG[seg]
        cons = store_consumer(kb, EV, dst, dt, func)
        for a in range(0, n, 512):
            wd = min(512, n - a)
            tiles.append(dict(W=w_in, k0=0, KC=16, c0=c0 + a, width=wd, mode=mode, nscale=nscale,
                              t_nscale=t_nscale, consume=cons, doff=dbase + a))
    add('da_q', 'fm', scr['qkT'], BF16, 0)
    add('da_k', 'fm', scr['qkT'], BF16, 1024)
    add('da_v', 'tm', scr['v_tm'], BF16)
    add('ssm_z', 'tm', scr['sz'], F32)
    add('ssm_xbc', 'fm', scr['xbcT'], F32)
    add('ssm_dt', 'tm', scr['sdt'], F32)
    add('gdn_qkv', 'fm', scr['gqkvT'], F32)
    add('gdn_z', 'tm', scr['gz'], F32)
    add('gdn_ba', 'tm', scr['gba'], F32)
    add('gates', 'fm', scr['gatesT'], BF16, 0, AF.Sigmoid)
    run_gemms(kb, G, xT, t_xT, L, tiles)


def make_scratch(kb, L, kind="Internal"):
    scr = {}
    scr['qkT'] = kb.dram('scr_qkT', [2048, L], BF16, kind)
    scr['v_tm'] = kb.dram('scr_v', [L, 1024], BF16, kind)
    scr['sz'] = kb.dram('scr_sz', [L, 1024], F32, kind)
    scr['xbcT'] = kb.dram('scr_xbcT', [1536, L], F32, kind)
    scr['sdt'] = kb.dram('scr_sdt', [L, 16], F32, kind)
    scr['gqkvT'] = kb.dram('scr_gqkvT', [3072, L], F32, kind)
    scr['gz'] = kb.dram('scr_gz', [L, 1024], F32, kind)
    scr['gba'] = kb.dram('scr_gba', [L, 16], F32, kind)
    scr['gatesT'] = kb.dram('scr_gatesT', [6144, L], BF16, kind)
    scr['obT'] = kb.dram('scr_obT', [3072, L], BF16, kind)
    return scr


DA_SLOPES = [2.0 ** (-(h + 1)) for h in range(8)]


def host_da_aug(L):
    t = np.arange(L)
    out = np.zeros((8, 2, 3, L), np.float32)
    for h in range(8):
        sl = DA_SLOPES[h]
        qr = t % 512
        kr = t % 128
        out[h, 0, 0] = -8.0 * sl * (2 * (qr // 2))
        out[h, 0, 1] = -8.0 * sl * (qr % 2)
        out[h, 0, 2] = 1.0
        out[h, 1, 0] = 1.0
        out[h, 1, 1] = 1.0
        out[h, 1, 2] = 8.0 * sl * kr
    return out.astype(ml_dtypes.bfloat16)


def phase_da(kb, A, P, L, scr, c_aug, lam_neg, sw, t_par, lambda_init):
    NQ = L // 512
    QT = [[A.alloc([67, L], BF16) for _ in range(2)] for _ in range(2)]
    KT = [[A.alloc([67, L], BF16) for _ in range(2)] for _ in range(2)]
    V = [A.alloc([128, L // 128, 128], BF16) for _ in range(2)]
    t_qkv = [Tok('qkv%d' % i) for i in range(2)]
    NPT = 6
    PT = [A.alloc([128, 512], BF16) for _ in range(NPT)]
    t_PT = [Tok('pt%d' % i) for i in range(NPT)]
    rl = [A.alloc([128, 512], F32) for _ in range(2)]
    t_rl = [Tok(), Tok()]
    on = [A.alloc([128, 512], F32) for _ in range(2)]
    t_on = [Tok(), Tok()]
    o = A.alloc([128, 512], F32)
    t_o = Tok()
    sq = A.alloc([128, 512], F32)
    t_sq = Tok()
    o2 = A.alloc([128, 512], F32)
    t_o2 = Tok()
    rs = A.alloc([128, 512], F32)
    t_rs = Tok()
    ob = [A.alloc([128, 512], BF16) for _ in range(2)]
    t_ob = [Tok(), Tok()]
    S_b = [P.banks[0], P.banks[1], P.banks[7]]
    O_b = [P.banks[2], P.banks[3]]
    L_b = [P.banks[4], P.banks[5]]
    N_b = P.banks[6]
    st = {'pti': 0, 'si': 0, 'dq_evac': [], 'dq_tail': []}

    def load_head(h):
        s = h % 2
        first = True
        for i in range(2):
            for (dst, base) in ((QT[s][i], 0), (KT[s][i], 1024)):
                r0 = base + h * 128 + i * 64
                if first:
                    kb.dma('sp', dst[0:64, :], scr['qkT'][r0:r0 + 64, :], w=[t_qkv[s]])
                    first = False
                else:
                    kb.dma('sp', dst[0:64, :], scr['qkT'][r0:r0 + 64, :], pw=[t_qkv[s]])
            kb.dma('sp', QT[s][i][64:67, :], c_aug[h, 0, :, :], pw=[t_qkv[s]])
            kb.dma('sp', KT[s][i][64:67, :], c_aug[h, 1, :, :], pw=[t_qkv[s]])
        kb.dma('sp', V[s][:, :, :], scr['v_tm'][:, h * 128:(h + 1) * 128].rearrange("(t p) e -> p t e", p=128),
               pw=[t_qkv[s]])

    load_head(0)
    for h in range(8):
        s = h % 2
        if h + 1 < 8:
            load_head(h + 1)
        sl = DA_SLOPES[h]
        for j in range(NQ):
            nk = 4 * (j + 1)
            for i in range(2):
                Ob, tO = O_b[i]
                Lb, tL = L_b[i]
                pend = {}

                def emit_S(kt, j=j, i=i, s=s, sl=sl, pend=pend):
                    c = kt - 4 * j
                    c0 = 128 * c if c > 0 else 0
                    Sb, tS = S_b[st['si'] % 3]
                    st['si'] += 1
                    kb.mm(Sb[:, c0:512], KT[s][i][0:67, kt * 128:(kt + 1) * 128], QT[s][i][0:67, j * 512 + c0:(j + 1) * 512],
                          start=True, stop=(c < 0), r=[t_qkv[s]], w=[tS])
                    if c >= 0:
                        kb.mm(Sb[:, c0:c0 + 128], P.ident[:, :], P.negtri_bf[:, :], start=False, stop=True, r=[P.t_const], pw=[tS])
                    pt, tpt = PT[st['pti'] % NPT], t_PT[st['pti'] % NPT]
                    st['pti'] += 1
                    kb.act(pt[:, c0:512], Sb[:, c0:512], AF.Exp, bias=float(sl * (kt * 128 - j * 512)), scale=0.125,
                           r=[tS], w=[tpt])
                    pend[kt] = (pt, tpt, c0)

                def emit_AV(kt, nk=nk, s=s, Ob=Ob, tO=tO, Lb=Lb, tL=tL, pend=pend):
                    pt, tpt, c0 = pend.pop(kt)
                    kb.mm(Ob[:, c0:512], V[s][:, kt, :], pt[:, c0:512], start=(kt == 0), stop=(kt == nk - 1),
                          r=[tpt, t_qkv[s]], pw=[tO])
                    kb.mm(Lb[:, c0:512], P.ones_bf[:, :], pt[:, c0:512], start=(kt == 0), stop=(kt == nk - 1),
                          r=[tpt, P.t_const], pw=[tL])

                emit_S(0)
                emit_S(1)
                for kt in range(nk):
                    if kt + 2 < nk:
                        emit_S(kt + 2)
                    emit_AV(kt)
                    if kt == 0:
                        for f in st['dq_evac']:
                            f()
                        st['dq_evac'] = []
                    if kt == 2:
                        for f in st['dq_tail']:
                            f()
                        st['dq_tail'] = []

                def evac(i=i, Ob=Ob, tO=tO, Lb=Lb, tL=tL):
                    kb.act(rl[i][:, :], Lb[:, :], AF.Ln, r=[tL], w=[t_rl[i]])
                    kb.act(rl[i][:, :], rl[i][:, :], AF.Exp, scale=-1.0, r=[t_rl[i]], w=[t_rl[i]])
                    kb.tt('dve', on[i][:, :], Ob[:, :], rl[i][:, :], ALU.mult, r=[tO, t_rl[i]], w=[t_on[i]])
                st['dq_evac'].append(evac)

            def tail(h=h, j=j):
                kb.stt('dve', o[:, :], on[1][:, :], lam_neg, on[0][:, :], ALU.mult, ALU.add, r=[t_on[0], t_on[1], t_par], w=[t_o])
                kb.act(sq[:, :], o[:, :], AF.Square, r=[t_o], w=[t_sq])
                Nb, tN = N_b
                kb.mm(Nb[:, :], P.ones_f32[:, :], sq[:, :], r=[t_sq, P.t_const], w=[tN])
                kb.act(rs[:, :], Nb[:, :], AF.Ln, scale=1.0 / 128, bias=float(EPS), r=[tN], w=[t_rs])
                kb.act(rs[:, :], rs[:, :], AF.Exp, scale=-0.5, r=[t_rs], w=[t_rs])
                b = (h * NQ + j) % 2
                kb.stt('dve', ob[b][:, :], o[:, :], sw, rs[:, :], ALU.mult, ALU.mult, r=[t_o, t_rs, t_par], w=[t_ob[b]])
                kb.dma('sp', scr['obT'][h * 128:(h + 1) * 128, j * 512:(j + 1) * 512], ob[b][:, :], r=[t_ob[b]])
            st['dq_tail'].append(tail)
    for f in st['dq_evac'] + st['dq_tail']:
        f()


def host_consts():
    c = np.zeros((128, 1152), np.float32)
    i = np.arange(128)
    c[:, 0:128] = np.eye(128)
    c[:, 128:256] = (i[:, None] <= i[None, :])
    c[:, 256:384] = 1.0
    c[:, 384:512] = (i[:, None] > i[None, :])
    blk = (i[:, None] // 64) == (i[None, :] // 64)
    c[:, 512:640] = (i[:, None] <= i[None, :]) & blk
    c[:, 640:768] = (i[:, None] > i[None, :]) & blk
    c[:, 768:896] = (i[:, None] < i[None, :]) & blk
    c[:, 896:1024] = blk
    c[:, 1024:1152] = -30000.0 * (i[:, None] > i[None, :])
    return c


def setup_consts(kb, A, P, cd):
    cst = A.alloc([128, 1152], F32)
    kb.dma('sp', cst[:, :], cd[:, :], w=[P.t_const])
    P.cst = cst
    P.ident = A.alloc([128, 128], BF16)
    P.tri_bf = A.alloc([128, 128], BF16)
    P.ones_bf = A.alloc([128, 128], BF16)
    kb.copy('dve', P.ident[:, :], cst[:, 0:128], r=[P.t_const], pw=[P.t_const])
    kb.copy('dve', P.tri_bf[:, :], cst[:, 128:256], r=[P.t_const], pw=[P.t_const])
    kb.copy('dve', P.ones_bf[:, :], cst[:, 256:384], r=[P.t_const], pw=[P.t_const])
    P.negtri_bf = A.alloc([128, 128], BF16)
    kb.copy('dve', P.negtri_bf[:, :], cst[:, 1024:1152], r=[P.t_const], pw=[P.t_const])
    P.ident_f32 = cst[:, 0:128]
    P.tri_f32 = cst[:, 128:256]
    P.ones_f32 = cst[:, 256:384]
    P.lstrict_f32 = cst[:, 384:512]
    P.u2_f32 = cst[:, 512:640]
    P.l2_f32 = cst[:, 640:768]
    P.su2_f32 = cst[:, 768:896]
    P.blk_f32 = cst[:, 896:1024]


def prep_da_params(kb, A, pks, t_pk, par, t_par, lambda_init):
    tmp = A.alloc([128, 64], F32)
    s12 = A.alloc([128, 2], F32)
    t_tmp = Tok()
    for i in range(2):
        kb.tt('dve', tmp[:, :], pks[:, i * 128:i * 128 + 64], pks[:, i * 128 + 64:i * 128 + 128], ALU.mult, r=[t_pk], w=[t_tmp])
        kb.s.op('dve', lambda e, i=i: e.reduce_sum(s12[:, i:i + 1], tmp[:, :], axis=AX.X), r=[t_tmp], pw=[t_par])
    kb.act(s12[:, :], s12[:, :], AF.Exp, r=[t_par], w=[t_par])
    kb.tt('dve', par[:, 0:1], s12[:, 1:2], s12[:, 0:1], ALU.subtract, r=[t_par], pw=[t_par])
    kb.ts('dve', par[:, 0:1], par[:, 0:1], -float(lambda_init), None, ALU.add, r=[t_par], pw=[t_par])
    kb.ts('dve', par[:, 1:2], pks[:, 256:257], float(1.0 - lambda_init), None, ALU.mult, r=[t_pk, t_par], pw=[t_par])


def conv_silu_chunk(kb, xin, t_xin, acc, t_acc, src_rows, L, K, wcols, bcol, t_par, out_ap, t_out, out_w=True):
    pad = K - 1
    kb.dma('sp', xin[:, pad:pad + L], src_rows, pw=[t_xin])
    if bcol is not None:
        kb.ts('dve', acc[:, 0:L], xin[:, pad:pad + L], wcols[K - 1], bcol, ALU.mult, ALU.add, r=[t_xin, t_par], w=[t_acc])
    else:
        kb.ts('dve', acc[:, 0:L], xin[:, pad:pad + L], wcols[K - 1], None, ALU.mult, r=[t_xin, t_par], w=[t_acc])
    for j in range(K - 1):
        kb.stt('dve', acc[:, 0:L], xin[:, j:j + L], wcols[j], acc[:, 0:L], ALU.mult, ALU.add, r=[t_xin, t_par, t_acc], w=[t_acc])
    if out_w:
        kb.act(out_ap, acc[:, 0:L], AF.Silu, r=[t_acc], w=[t_out])
    else:
        kb.act(out_ap, acc[:, 0:L], AF.Silu, r=[t_acc], pw=[t_out])


def softplus_tm(kb, A, out, in_, bias_bc, shape, t_in, t_out, t_par):
    xb = A.alloc(shape, F32)
    ab = A.alloc(shape, F32)
    t_x = Tok()
    t_a = Tok()
    sl = tuple([slice(None)] * len(shape))
    kb.tt('dve', xb[sl], in_, bias_bc, ALU.add, r=[t_in, t_par], w=[t_x])
    kb.ts('dve', ab[sl], xb[sl], -1.0, None, ALU.mult, r=[t_x], w=[t_a])
    kb.tt('dve', ab[sl], ab[sl], xb[sl], ALU.max, r=[t_x, t_a], w=[t_a])
    kb.act(ab[sl], ab[sl], AF.Exp, scale=-1.0, r=[t_a], w=[t_a])
    kb.act(ab[sl], ab[sl], AF.Ln, bias=1.0, r=[t_a], w=[t_a])
    kb.ts('dve', xb[sl], xb[sl], 0.0, None, ALU.max, r=[t_x], w=[t_x])
    kb.tt('dve', out, xb[sl], ab[sl], ALU.add, r=[t_x, t_a], w=[t_out])


def phase_ssd(kb, A, P, L, scr, pks, t_pk):
    T = L // 128
    x_tm = A.alloc([128, T, 1024], BF16)
    t_xtm = Tok('x_tm')
    BT = A.alloc([128, 2, L], BF16)
    CT = A.alloc([128, 2, L], BF16)
    t_BC = Tok('BCT')
    B_tm = A.alloc([128, T, 256], BF16)
    t_Btm = Tok('B_tm')
    xin = [A.alloc([128, 3 + L], F32) for _ in range(2)]
    t_xin = [Tok(), Tok()]
    acc = A.alloc([128, L], F32)
    t_acc = Tok()
    xs = [A.alloc([128, L], BF16) for _ in range(2)]
    t_xs = [Tok(), Tok()]
    for b in range(2):
        kb.memset('dve', xin[b][:, 0:3], 0.0, w=[t_xin[b]])
    for cc in range(12):
        b = cc % 2
        wcols = [pks[:, cc * 4 + j:cc * 4 + j + 1] for j in range(4)]
        bcol = pks[:, 48 + cc:49 + cc]
        src = scr['xbcT'][cc * 128:(cc + 1) * 128, :]
        if cc < 8:
            conv_silu_chunk(kb, xin[b], t_xin[b], acc, t_acc, src, L, 4, wcols, bcol, t_pk, xs[b][:, :], t_xs[b])
            for t0 in range(0, T, 8):
                ps, tps = P.next_bank()
                psb = ps[:, :].bitcast(BF16)
                nt = min(8, T - t0)
                for i in range(nt):
                    kb.tr(psb[:, i * 128:(i + 1) * 128], xs[b][:, (t0 + i) * 128:(t0 + i + 1) * 128], P.ident[:, :],
                          r=[t_xs[b], P.t_const], pw=[tps])
                kb.copy('act' if (t0 // 8) % 2 else 'dve', x_tm[:, t0:t0 + nt, cc * 128:(cc + 1) * 128],
                        psb[:, 0:nt * 128].rearrange("p (a b) -> p a b", a=nt), r=[tps], pw=[t_xtm])
        elif cc < 10:
            g = cc - 8
            conv_silu_chunk(kb, xin[b], t_xin[b], acc, t_acc, src, L, 4, wcols, bcol, t_pk, BT[:, g, :], t_BC, out_w=False)
            for t0 in range(0, T, 8):
                ps, tps = P.next_bank()
                psb = ps[:, :].bitcast(BF16)
                nt = min(8, T - t0)
                for i in range(nt):
                    kb.tr(psb[:, i * 128:(i + 1) * 128], BT[:, g, (t0 + i) * 128:(t0 + i + 1) * 128], P.ident[:, :],
                          r=[t_BC, P.t_const], pw=[tps])
                kb.copy('dve', B_tm[:, t0:t0 + nt, g * 128:(g + 1) * 128],
                        psb[:, 0:nt * 128].rearrange("p (a b) -> p a b", a=nt), r=[tps], pw=[t_Btm])
        else:
            g = cc - 10
            conv_silu_chunk(kb, xin[b], t_xin[b], acc, t_acc, src, L, 4, wcols, bcol, t_pk, CT[:, g, :], t_BC, out_w=False)
    dtr = A.alloc([128, T, 16], F32)
    t_dtr = Tok()
    kb.dma('sp', dtr[:, :, :], scr['sdt'].rearrange("(t p) h -> p t h", p=128), w=[t_dtr])
    dt = A.alloc([128, T, 16], F32)
    t_dt = Tok()
    softplus_tm(kb, A, dt[:, :, :], dtr[:, :, :], pks[:, 64:80].unsqueeze(1).to_broadcast([128, T, 16]), [128, T, 16], t_dtr, t_dt, t_pk)
    aneg = A.alloc([128, 16], F32)
    t_an = Tok()
    kb.act(aneg[:, :], pks[:, 80:96], AF.Exp, r=[t_pk], w=[t_an])
    kb.ts('dve', aneg[:, :], aneg[:, :], -1.0, None, ALU.mult, r=[t_an], w=[t_an])
    a_all = A.alloc([128, T, 16], F32)
    t_a = Tok()
    kb.tt('dve', a_all[:, :, :], dt[:, :, :], aneg[:, :].unsqueeze(1).to_broadcast([128, T, 16]), ALU.mult, r=[t_dt, t_an], w=[t_a])
    S = A.alloc([128, 1024], F32)
    Sbf = A.alloc([128, 1024], BF16)
    t_S = Tok('S')
    t_Sbf = Tok('Sbf')
    kb.memset('dve', S[:, :], 0.0, w=[t_S])
    kb.memset('dve', Sbf[:, :], 0.0, w=[t_Sbf])
    pre = A.alloc([128, 48], F32)
    t_pre = Tok()
    E3 = A.alloc([128, 48], F32)
    t_E3 = Tok()
    rhsA = A.alloc([128, 16, 128], F32)
    t_rhsA = Tok()
    ET = A.alloc([128, 16, 128], F32)
    t_ET = Tok()
    Gm = A.alloc([128, 2, 128], F32)
    t_Gm = Tok()
    M = A.alloc([128, 16, 128], BF16)
    t_M = Tok()
    xdt = A.alloc([128, 16, 64], BF16)
    t_xdt = Tok()
    xdtd = A.alloc([128, 16, 64], BF16)
    t_xdtd = Tok()
    t1 = A.alloc([128, 1024], F32)
    t_t1 = Tok()
    y = A.alloc([128, 1024], F32)
    t_y = Tok()
    zt = A.alloc([128, 1024], F32)
    t_zt = Tok()
    junk = A.alloc([128, 512], BF16)
    t_junk = Tok()
    ssq = A.alloc([128, 2], F32)
    t_ssq = Tok()
    yn = A.alloc([128, 1024], BF16)
    t_yn = Tok()
    oT = A.alloc([128, 8, 128], BF16)
    t_oT = Tok()
    Drep = pks[:, 96:1120]
    bk = P.banks
    E3s = [E3, A.alloc([128, 48], F32)]
    t_E3s = [t_E3, Tok()]
    xdtds = [xdtd, A.alloc([128, 16, 64], BF16)]
    t_xdtds = [t_xdtd, Tok()]
    yds = [A.alloc([128, 1024], F32) for _ in range(2)]
    t_yds = [Tok(), Tok()]

    def gen_pre(c):
        p = c % 2
        E3_, tE3_ = E3s[p], t_E3s[p]
        a_c = a_all[:, c, :]
        tok = slice(c * 128, (c + 1) * 128)
        b0, tb0 = bk[0]
        kb.mm(b0[:, 0:16], P.tri_f32, a_c, r=[t_a, P.t_const], w=[tb0])
        kb.mm(b0[:, 16:32], P.ones_f32, a_c, r=[t_a, P.t_const], pw=[tb0])
        for g in range(2):
            kb.mm(b0[:, 128 + g * 128:256 + g * 128], BT[:, g, tok], CT[:, g, tok], r=[t_BC], pw=[tb0])
        kb.copy('dve', pre[:, 0:16], b0[:, 0:16], r=[tb0], w=[t_pre])
        kb.copy('dve', pre[:, 32:48], b0[:, 16:32], r=[tb0], pw=[t_pre])
        kb.tt('dve', pre[:, 16:32], pre[:, 32:48], pre[:, 0:16], ALU.subtract, r=[t_pre], pw=[t_pre])
        kb.act(E3_[:, :], pre[:, :], AF.Exp, r=[t_pre], w=[tE3_])
        kb.tt('dve', Gm[:, :, :], b0[:, 128:384].rearrange("p (g l) -> p g l", g=2),
              P.tri_f32.unsqueeze(1).to_broadcast([128, 2, 128]), ALU.mult, r=[tb0, P.t_const], w=[t_Gm])
        yield
        kb.tt('dve', rhsA[:, :, :], P.tri_f32.unsqueeze(1).to_broadcast([128, 16, 128]),
              a_c.unsqueeze(2).to_broadcast([128, 16, 128]), ALU.mult, r=[t_a, P.t_const], w=[t_rhsA])
        kb.tt('pool', xdt[:, :, :], x_tm[:, c, :].rearrange("p (h d) -> p h d", h=16),
              dt[:, c, :].unsqueeze(2).to_broadcast([128, 16, 64]), ALU.mult, r=[t_xtm, t_dt], w=[t_xdt])
        kb.tt('pool', xdtds[p][:, :, :], xdt[:, :, :], E3_[:, 16:32].unsqueeze(2).to_broadcast([128, 16, 64]), ALU.mult,
              r=[t_xdt, tE3_], w=[t_xdtds[p]])
        for hb in range(4):
            bs, tbs = bk[1 + hb % 2]
            kb.mm(bs[:, :], P.lstrict_f32, rhsA[:, hb * 4:(hb + 1) * 4, :].rearrange("p a b -> p (a b)"),
                  r=[t_rhsA, P.t_const], w=[tbs])
            kb.act(ET[:, hb * 4:(hb + 1) * 4, :].rearrange("p a b -> p (a b)"), bs[:, :], AF.Exp, r=[tbs],
                   **({'w': [t_ET]} if hb == 0 else {'pw': [t_ET]}))
            if hb % 2 == 1:
                yield
        for g in range(2):
            kb.tt('dve', M[:, g * 8:(g + 1) * 8, :], ET[:, g * 8:(g + 1) * 8, :],
                  Gm[:, g:g + 1, :].to_broadcast([128, 8, 128]), ALU.mult, r=[t_ET, t_Gm],
                  **({'w': [t_M]} if g == 0 else {'pw': [t_M]}))
        yield
        for h in range(16):
            by, tby = bk[3 + h // 8]
            kb.mm(by[:, (h % 8) * 64:(h % 8 + 1) * 64], M[:, h, :], xdt[:, h, :], r=[t_M, t_xdt],
                  **({'w': [tby]} if h % 8 == 0 else {'pw': [tby]}))
        for g in range(2):
            by, tby = bk[3 + g]
            kb.copy('act', yds[p][:, g * 512:(g + 1) * 512], by[:, :], r=[tby],
                    **({'w': [t_yds[p]]} if g == 0 else {'pw': [t_yds[p]]}))
        yield

    def gen_rec(c):
        p = c % 2
        E3_, tE3_ = E3s[p], t_E3s[p]
        tok = slice(c * 128, (c + 1) * 128)
        for g in range(2):
            bo, tbo = bk[5 + g]
            kb.mm(bo[:, :], CT[:, g, tok], Sbf[:, g * 512:(g + 1) * 512], r=[t_BC, t_Sbf], w=[tbo])
        kb.dma('sp', zt[:, :], scr['sz'][c * 128:(c + 1) * 128, :], w=[t_zt])
        kb.act(zt[:, :], zt[:, :], AF.Silu, r=[t_zt], w=[t_zt])
        for g in range(2):
            hs = slice(g * 512, (g + 1) * 512)
            bo, tbo = bk[5 + g]
            kb.tt('dve', t1[:, hs].rearrange("p (h d) -> p h d", h=8), bo[:, :].rearrange("p (h d) -> p h d", h=8),
                  E3_[:, g * 8:(g + 1) * 8].unsqueeze(2).to_broadcast([128, 8, 64]), ALU.mult, r=[tbo, tE3_],
                  **({'w': [t_t1]} if g == 0 else {'pw': [t_t1]}))
        yield
        if c + 1 < T:
            for g in range(2):
                hs = slice(g * 512, (g + 1) * 512)
                bo, tbo = bk[5 + g]
                kb.mm(bo[:, :], B_tm[:, c, g * 128:(g + 1) * 128], xdtds[p][:, g * 8:(g + 1) * 8, :].rearrange("p a b -> p (a b)"),
                      r=[t_Btm, t_xdtds[p]], w=[tbo])
                kb.tt('dve', S[:, hs].rearrange("p (h d) -> p h d", h=8), S[:, hs].rearrange("p (h d) -> p h d", h=8),
                      E3_[:, 32 + g * 8:32 + (g + 1) * 8].unsqueeze(2).to_broadcast([128, 8, 64]), ALU.mult,
                      r=[tE3_, t_S], pw=[t_S])
                kb.tt('dve', S[:, hs], S[:, hs], bo[:, :], ALU.add, r=[tbo, t_S], pw=[t_S])
                kb.copy('act', Sbf[:, hs], S[:, hs], r=[t_S], **({'w': [t_Sbf]} if g == 0 else {'pw': [t_Sbf]}))
            yield
        kb.tt('dve', t1[:, :], t1[:, :], yds[p][:, :], ALU.add, r=[t_yds[p], t_t1], w=[t_t1])
        for g in range(2):
            hs = slice(g * 512, (g + 1) * 512)
            kb.tt('dve', y[:, hs], x_tm[:, c, hs], Drep[:, hs], ALU.mult, r=[t_xtm, t_pk],
                  **({'w': [t_y]} if g == 0 else {'pw': [t_y]}))
        kb.tt('dve', y[:, :], y[:, :], t1[:, :], ALU.add, r=[t_y, t_t1], w=[t_y])
        kb.tt('dve', y[:, :], y[:, :], zt[:, :], ALU.mult, r=[t_y, t_zt], w=[t_y])
        yield
        for g in range(2):
            hs = slice(g * 512, (g + 1) * 512)
            kb.act(junk[:, :], y[:, hs], AF.Square, scale=float(512 ** -0.5), accum_out=ssq[:, g:g + 1], r=[t_y],
                   **({'w': [t_junk, t_ssq]} if g == 0 else {'w': [t_junk], 'pw': [t_ssq]}))
        kb.rsqrt_eps(ssq[:, :], ssq[:, :], t_ssq, t_ssq)
        for g in range(2):
            hs = slice(g * 512, (g + 1) * 512)
            kb.act(yn[:, hs], y[:, hs], AF.Copy, scale=ssq[:, g:g + 1], r=[t_y, t_ssq],
                   **({'w': [t_yn]} if g == 0 else {'pw': [t_yn]}))
        yield
        bt, tbt = bk[7]
        btb = bt[:, :].bitcast(BF16)
        for j in range(8):
            kb.tr(btb[:, j * 128:(j + 1) * 128], yn[:, j * 128:(j + 1) * 128], P.ident[:, :], r=[t_yn, P.t_const],
                  **({'w': [tbt]} if j == 0 else {'pw': [tbt]}))
        kb.copy('act', oT[:, :, :], btb[:, 0:1024].rearrange("p (a b) -> p a b", a=8), r=[tbt], w=[t_oT])
        kb.dma('sp', scr['obT'][1024:2048, c * 128:(c + 1) * 128].rearrange("(j p) t -> p j t", p=128), oT[:, :, :], r=[t_oT])
        yield

    for _ in gen_pre(0):
        pass
    for c in range(T):
        gr = gen_rec(c)
        gp = gen_pre(c + 1) if c + 1 < T else iter(())
        alive = [True, True]
        while alive[0] or alive[1]:
            if alive[1]:
                try:
                    next(gp)
                except StopIteration:
                    alive[1] = False
            if alive[0]:
                try:
                    next(gr)
                except StopIteration:
                    alive[0] = False


def phase_gdn(kb, A, P, L, scr, pks, t_pk):
    T = L // 128
    NB = L // 512
    bk = P.banks
    base0 = A.off
    bar = A.alloc([128, T, 16], F32)
    t_bar = Tok()
    kb.dma('sp', bar[:, :, :], scr['gba'].rearrange("(t p) c -> p t c", p=128), w=[t_bar])
    beta = A.alloc([128, T, 8], F32)
    negb = A.alloc([128, T, 8], F32)
    t_beta = Tok()
    kb.act(beta[:, :, :], bar[:, :, 0:8], AF.Sigmoid, r=[t_bar], w=[t_beta])
    kb.ts('dve', negb[:, :, :], beta[:, :, :], -1.0, None, ALU.mult, r=[t_beta], pw=[t_beta])
    sp = A.alloc([128, T, 8], F32)
    t_sp = Tok()
    softplus_tm(kb, A, sp[:, :, :], bar[:, :, 8:16], pks[:, 96:104].unsqueeze(1).to_broadcast([128, T, 8]), [128, T, 8], t_bar, t_sp, t_pk)
    aneg = A.alloc([128, 8], F32)
    t_an = Tok()
    kb.act(aneg[:, :], pks[:, 104:112], AF.Exp, r=[t_pk], w=[t_an])
    kb.ts('dve', aneg[:, :], aneg[:, :], -1.0, None, ALU.mult, r=[t_an], w=[t_an])
    g_all = A.alloc([128, T, 8], F32)
    t_g = Tok()
    kb.tt('dve', g_all[:, :, :], sp[:, :, :], aneg[:, :].unsqueeze(1).to_broadcast([128, T, 8]), ALU.mult, r=[t_sp, t_an], w=[t_g])
    base1 = A.off
    for hb in range(2):
        kb.s.barrier()
        A.off = base1
        qT = A.alloc([128, 4, L], BF16)
        kT = A.alloc([128, 4, L], BF16)
        t_qk = Tok('qkT')
        k_tm = A.alloc([128, T, 4, 128], BF16)
        v_tm = A.alloc([128, T, 4, 128], BF16)
        t_kv = Tok('kv_tm')
        xin = [A.alloc([128, 3 + L], F32) for _ in range(2)]
        t_xin = [Tok(), Tok()]
        acc2 = [A.alloc([128, L], F32) for _ in range(2)]
        t_acc2 = [Tok(), Tok()]
        ks2 = [A.alloc([128, L], F32) for _ in range(2)]
        t_ks2 = [Tok(), Tok()]
        sq2 = [A.alloc([128, L], F32) for _ in range(2)]
        t_sq2 = [Tok(), Tok()]
        rn2 = [A.alloc([128, 512], F32) for _ in range(2)]
        t_rn2 = [Tok(), Tok()]
        vs2 = [A.alloc([128, L], BF16) for _ in range(2)]
        t_vs2 = [Tok(), Tok()]
        for b in range(2):
            kb.memset('dve', xin[b][:, 0:3], 0.0, w=[t_xin[b]])
        ci = 0
        def partA(kind, hh, b):
            h = hb * 4 + hh
            cc = kind * 8 + h
            wcols = [pks[:, cc * 4 + j:cc * 4 + j + 1] for j in range(4)]
            src = scr['gqkvT'][cc * 128:(cc + 1) * 128, :]
            if kind < 2:
                conv_silu_chunk(kb, xin[b], t_xin[b], acc2[b], t_acc2[b], src, L, 4, wcols, None, t_pk, ks2[b][:, :], t_ks2[b])
                kb.act(sq2[b][:, :], ks2[b][:, :], AF.Square, r=[t_ks2[b]], w=[t_sq2[b]])
            else:
                conv_silu_chunk(kb, xin[b], t_xin[b], acc2[b], t_acc2[b], src, L, 4, wcols, None, t_pk, vs2[b][:, :], t_vs2[b])

        def partB(kind, hh, b):
            ks, t_ks, sq, t_sq, vs, t_vs = ks2[b], t_ks2[b], sq2[b], t_sq2[b], vs2[b], t_vs2[b]
            if kind < 2:
                dstT = qT if kind == 0 else kT
                scale = float(128 ** -0.5) if kind == 0 else 1.0
                for tb in range(NB):
                    ps, tps = P.next_bank()
                    cs_ = slice(tb * 512, (tb + 1) * 512)
                    rn, t_rn = rn2[tb % 2], t_rn2[tb % 2]
                    kb.mm(ps[:, :], P.ones_f32, sq[:, cs_], r=[t_sq, P.t_const], w=[tps])
                    kb.act(rn[:, :], ps[:, :], AF.Ln, bias=1e-6, r=[tps], w=[t_rn])
                    kb.act(rn[:, :], rn[:, :], AF.Exp, scale=-0.5, r=[t_rn], w=[t_rn])
                    kb.stt('dve', dstT[:, hh, cs_], ks[:, cs_], scale, rn[:, :], ALU.mult, ALU.mult, r=[t_ks, t_rn], pw=[t_qk])
                if kind == 1:
                    for t0 in range(0, T, 8):
                        ps, tps = P.next_bank()
                        psb = ps[:, :].bitcast(BF16)
                        nt = min(8, T - t0)
                        for i in range(nt):
                            kb.tr(psb[:, i * 128:(i + 1) * 128], kT[:, hh, (t0 + i) * 128:(t0 + i + 1) * 128], P.ident[:, :],
                                  r=[t_qk, P.t_const], pw=[tps])
                        kb.copy('act', k_tm[:, t0:t0 + nt, hh, :], psb[:, 0:nt * 128].rearrange("p (a b) -> p a b", a=nt),
                                r=[tps], pw=[t_kv])
            else:
                for t0 in range(0, T, 8):
                    ps, tps = P.next_bank()
                    psb = ps[:, :].bitcast(BF16)
                    nt = min(8, T - t0)
                    for i in range(nt):
                        kb.tr(psb[:, i * 128:(i + 1) * 128], vs[:, (t0 + i) * 128:(t0 + i + 1) * 128], P.ident[:, :],
                              r=[t_vs, P.t_const], pw=[tps])
                    kb.copy('act', v_tm[:, t0:t0 + nt, hh, :], psb[:, 0:nt * 128].rearrange("p (a b) -> p a b", a=nt),
                            r=[tps], pw=[t_kv])

        chunks = [(kind, hh, i % 2) for i, (kind, hh) in enumerate([(k_, h_) for k_ in range(3) for h_ in range(4)])]
        partA(*chunks[0])
        for i in range(len(chunks)):
            if i + 1 < len(chunks):
                partA(*chunks[i + 1])
            partB(*chunks[i])
        S = A.alloc([128, 4, 128], F32)
        Sbf = A.alloc([128, 4, 128], BF16)
        t_S = Tok('S')
        t_Sbf = Tok('Sbf')
        kb.memset('dve', S[:, :, :], 0.0, w=[t_S])
        kb.memset('dve', Sbf[:, :, :], 0.0, w=[t_Sbf])
        gm = A.alloc([128, 2, 4], F32)
        t_gm = Tok()
        pre = A.alloc([128, 16], F32)
        t_pre = Tok()
        E = A.alloc([128, 16], F32)
        t_E = Tok()
        rhsG = A.alloc([128, 4, 128], F32)
        t_rhsG = Tok()
        DT = A.alloc([128, 4, 128], F32)
        t_DT = Tok()
        tmp = A.alloc([128, 4, 128], F32)
        t_tmp = Tok()
        tmp2 = A.alloc([128, 4, 128], F32)
        t_tmp2 = Tok()
        attnT = A.alloc([128, 4, 128], BF16)
        t_attn = Tok()
        Xs = [A.alloc([128, 4, 128], BF16) for _ in range(2)]
        Ys = [A.alloc([128, 4, 128], BF16) for _ in range(2)]
        t_X = [Tok(), Tok()]
        t_Y = [Tok(), Tok()]
        Rs = [A.alloc([128, 4, 128], BF16) for _ in range(2)]
        t_R = [Tok(), Tok()]
        u0b = A.alloc([128, 4, 128], F32)
        t_u0b = Tok()
        w0T = A.alloc([128, 4, 128], BF16)
        t_w0T = Tok()
        keg = A.alloc([128, 4, 128], BF16)
        t_keg = Tok()
        kd = A.alloc([128, 4, 128], BF16)
        t_kd = Tok()
        vn = A.alloc([128, 4, 128], BF16)
        t_vn = Tok()
        kb.memset('dve', vn[:, :, :], 0.0, w=[t_vn])
        tq = A.alloc([128, 4, 128], F32)
        t_tq = Tok()
        o = A.alloc([128, 4, 128], F32)
        t_o = Tok()
        zt = A.alloc([128, 512], F32)
        t_zt = Tok()
        junk = A.alloc([128, 128], BF16)
        t_junk = Tok()
        ssq = A.alloc([128, 4], F32)
        t_ssq = Tok()
        onb = A.alloc([128, 512], BF16)
        t_onb = Tok()
        oT = A.alloc([128, 4, 128], BF16)
        t_oT = Tok()
        hs = slice(hb * 4, hb * 4 + 4)

        def bc3(ap2, n=128):
            return ap2.unsqueeze(2).to_broadcast([ap2.shape[0], 4, n])

        def m4(ap2):
            return ap2.unsqueeze(1).to_broadcast([128, 4, 128])

        def f2(ap3):
            return ap3.rearrange("p a b -> p (a b)")

        E2 = [E, A.alloc([128, 16], F32)]
        t_E2 = [t_E, Tok()]
        attn2 = [attnT, A.alloc([128, 4, 128], BF16)]
        t_attn2 = [t_attn, Tok()]
        u0b2 = [u0b, A.alloc([128, 4, 128], F32)]
        t_u0b2 = [t_u0b, Tok()]
        w0T2 = [w0T, A.alloc([128, 4, 128], BF16)]
        t_w0T2 = [t_w0T, Tok()]
        kd2 = [kd, A.alloc([128, 4, 128], BF16)]
        t_kd2 = [t_kd, Tok()]

        def gen_pre(t):
            p = t % 2
            E_, tE_ = E2[p], t_E2[p]
            tok = slice(t * 128, (t + 1) * 128)
            g_t = g_all[:, t, hs]
            for j in range(2):
                kb.ts('dve', gm[:, j, :], g_t, P.blk_f32[:, 64 * j:64 * j + 1], None, ALU.mult, r=[t_g, P.t_const],
                      **({'w': [t_gm]} if j == 0 else {'pw': [t_gm]}))
            b0, tb0 = bk[0]
            kb.mm(b0[:, 0:4], P.u2_f32, g_t, r=[t_g, P.t_const], w=[tb0])
            kb.mm(b0[:, 4:8], P.blk_f32, g_t, r=[t_g, P.t_const], pw=[tb0])
            kb.mm(b0[:, 8:16], P.ones_f32, gm[:, :, :].rearrange("p a b -> p (a b)"), r=[t_gm, P.t_const], pw=[tb0])
            kb.copy('dve', pre[:, :], b0[:, 0:16], r=[tb0], w=[t_pre])
            kb.tt('dve', pre[:, 4:8], pre[:, 4:8], pre[:, 0:4], ALU.subtract, r=[t_pre], w=[t_pre])
            kb.act(E_[:, :], pre[:, :], AF.Exp, r=[t_pre], w=[tE_])
            yield
            kb.tt('dve', rhsG[:, :, :], m4(P.u2_f32), bc3(g_t), ALU.mult, r=[t_g, P.t_const], w=[t_rhsG])
            b1, tb1 = bk[1]
            kb.mm(b1[:, :], P.l2_f32, f2(rhsG[:, :, :]), r=[t_rhsG, P.t_const], w=[tb1])
            kb.act(f2(DT[:, :, :]), b1[:, :], AF.Exp, r=[tb1], w=[t_DT])
            yield
            b2, tb2 = bk[2]
            b3, tb3 = bk[3]
            for hh in range(4):
                kb.mm(b2[:, hh * 128:(hh + 1) * 128], kT[:, hh, tok], kT[:, hh, tok], r=[t_qk],
                      **({'w': [tb2]} if hh == 0 else {'pw': [tb2]}))
            for hh in range(4):
                kb.mm(b3[:, hh * 128:(hh + 1) * 128], kT[:, hh, tok], qT[:, hh, tok], r=[t_qk],
                      **({'w': [tb3]} if hh == 0 else {'pw': [tb3]}))
            kb.tt('dve', f2(tmp[:, :, :]), b2[:, :], f2(DT[:, :, :]), ALU.mult, r=[tb2, t_DT], w=[t_tmp])
            kb.tt('dve', tmp[:, :, :], tmp[:, :, :], m4(P.su2_f32), ALU.mult, r=[t_tmp, P.t_const], w=[t_tmp])
            kb.tt('dve', Ys[0][:, :, :], tmp[:, :, :], bc3(negb[:, t, hs]), ALU.mult, r=[t_tmp, t_beta], w=[t_Y[0]])
            yield
            kb.tt('dve', f2(tmp2[:, :, :]), b3[:, :], f2(DT[:, :, :]), ALU.mult, r=[tb3, t_DT], w=[t_tmp2])
            kb.tt('dve', attn2[p][:, :, :], tmp2[:, :, :], m4(P.u2_f32), ALU.mult, r=[t_tmp2, P.t_const], w=[t_attn2[p]])
            b4, tb4 = bk[0]
            b4b = b4[:, :].bitcast(BF16)
            for hh in range(4):
                kb.tr(b4b[:, hh * 128:(hh + 1) * 128], Ys[0][:, hh, :], P.ident[:, :], r=[t_Y[0], P.t_const],
                      **({'w': [tb4]} if hh == 0 else {'pw': [tb4]}))
            kb.copy('act', f2(Xs[0][:, :, :]), b4b[:, 0:512], r=[tb4], w=[t_X[0]])
            kb.tt('dve', Rs[0][:, :, :], Ys[0][:, :, :], m4(P.ident_f32), ALU.add, r=[t_Y[0], P.t_const], w=[t_R[0]])
            yield
            for lv in range(5):
                a, n_ = lv % 2, (lv + 1) % 2
                bx, tbx = bk[1]
                by, tby = bk[2]
                br, tbr = bk[3]
                for hh in range(4):
                    kb.mm(bx[:, hh * 128:(hh + 1) * 128], Ys[a][:, hh, :], Xs[a][:, hh, :], r=[t_X[a], t_Y[a]],
                          **({'w': [tbx]} if hh == 0 else {'pw': [tbx]}))
                kb.copy('act', f2(Xs[n_][:, :, :]), bx[:, :], r=[tbx], w=[t_X[n_]])
                if lv < 4:
                    for hh in range(4):
                        kb.mm(by[:, hh * 128:(hh + 1) * 128], Xs[a][:, hh, :], Ys[a][:, hh, :], r=[t_X[a], t_Y[a]],
                              **({'w': [tby]} if hh == 0 else {'pw': [tby]}))
                    kb.copy('dve', f2(Ys[n_][:, :, :]), by[:, :], r=[tby], w=[t_Y[n_]])
                yield
                for hh in range(4):
                    kb.mm(br[:, hh * 128:(hh + 1) * 128], Xs[n_][:, hh, :], Rs[a][:, hh, :], r=[t_X[n_], t_R[a]],
                          **({'w': [tbr]} if hh == 0 else {'pw': [tbr]}))
                kb.tt('dve', f2(Rs[n_][:, :, :]), br[:, :], f2(Rs[a][:, :, :]), ALU.add, r=[tbr, t_R[a]], w=[t_R[n_]])
                yield
            R, tR = Rs[1], t_R[1]
            kb.tt('dve', keg[:, :, :], k_tm[:, t, :, :], bc3(E_[:, 0:4]), ALU.mult, r=[t_kv, tE_], w=[t_keg])
            kb.tt('dve', kd2[p][:, :, :], k_tm[:, t, :, :], bc3(E_[:, 4:8]), ALU.mult, r=[t_kv, tE_], w=[t_kd2[p]])
            b2, tb2 = bk[0]
            b3, tb3 = bk[1]
            for hh in range(4):
                kb.mm(b2[:, hh * 128:(hh + 1) * 128], R[:, hh, :], v_tm[:, t, hh, :], r=[tR, t_kv],
                      **({'w': [tb2]} if hh == 0 else {'pw': [tb2]}))
            kb.tt('dve', u0b2[p][:, :, :], b2[:, :].rearrange("p (a b) -> p a b", a=4), bc3(beta[:, t, hs]), ALU.mult,
                  r=[tb2, t_beta], w=[t_u0b2[p]])
            yield
            for hh in range(4):
                kb.mm(b3[:, hh * 128:(hh + 1) * 128], keg[:, hh, :], R[:, hh, :], r=[tR, t_keg],
                      **({'w': [tb3]} if hh == 0 else {'pw': [tb3]}))
            kb.copy('act', f2(w0T2[p][:, :, :]), b3[:, :], r=[tb3], w=[t_w0T2[p]])
            yield

        def gen_rec(t):
            p = t % 2
            E_, tE_ = E2[p], t_E2[p]
            tok = slice(t * 128, (t + 1) * 128)
            for j in range(2):
                rows = slice(64 * j, 64 * j + 64)
                ba_, tba = bk[4]
                bq, tbq = bk[5]
                bo, tbo = bk[6]
                bs, tbs = bk[7]
                for hh in range(4):
                    kb.mm(ba_[:, hh * 128:(hh + 1) * 128], w0T2[p][:, hh, :], Sbf[:, hh, :], r=[t_w0T2[p], t_Sbf],
                          **({'w': [tba]} if hh == 0 else {'pw': [tba]}))
                for hh in range(4):
                    kb.mm(bq[:, hh * 128:(hh + 1) * 128], qT[:, hh, tok], Sbf[:, hh, :], r=[t_qk, t_Sbf],
                          **({'w': [tbq]} if hh == 0 else {'pw': [tbq]}))
                kb.tt('dve', tq[rows, :, :], ba_[rows, :].rearrange("p (a b) -> p a b", a=4), bc3(negb[rows, t, hs]), ALU.mult,
                      r=[tba, t_beta], w=[t_tq])
                kb.tt('dve', vn[rows, :, :], tq[rows, :, :], u0b2[p][rows, :, :], ALU.add, r=[t_tq, t_u0b2[p]], w=[t_vn])
                yield
                for hh in range(4):
                    kb.mm(bo[:, hh * 128:(hh + 1) * 128], attn2[p][rows, hh, :], vn[rows, hh, :], r=[t_attn2[p], t_vn],
                          **({'w': [tbo]} if hh == 0 else {'pw': [tbo]}))
                for hh in range(4):
                    kb.mm(bs[:, hh * 128:(hh + 1) * 128], kd2[p][rows, hh, :], vn[rows, hh, :], r=[t_kd2[p], t_vn],
                          **({'w': [tbs]} if hh == 0 else {'pw': [tbs]}))
                kb.tt('dve', S[:, :, :], S[:, :, :], bc3(E_[:, 8 + 4 * j:12 + 4 * j]), ALU.mult, r=[t_S, tE_], w=[t_S])
                kb.tt('dve', f2(S[:, :, :]), f2(S[:, :, :]), bs[:, :], ALU.add, r=[t_S, tbs], w=[t_S])
                kb.copy('act', Sbf[:, :, :], S[:, :, :], r=[t_S], w=[t_Sbf])
                yield
                kb.tt('dve', tq[rows, :, :], bq[rows, :].rearrange("p (a b) -> p a b", a=4), bc3(E_[rows, 0:4]), ALU.mult,
                      r=[tbq, tE_], w=[t_tq])
                kb.tt('dve', o[rows, :, :], tq[rows, :, :], bo[rows, :].rearrange("p (a b) -> p a b", a=4), ALU.add,
                      r=[t_tq, tbo], **({'w': [t_o]} if j == 0 else {'pw': [t_o]}))
                yield
            kb.dma('sp', zt[:, :], scr['gz'][tok, hb * 512:(hb + 1) * 512], w=[t_zt])
            kb.act(zt[:, :], zt[:, :], AF.Silu, r=[t_zt], w=[t_zt])
            for hh in range(4):
                kb.act(junk[:, :], o[:, hh, :], AF.Square, scale=float(128 ** -0.5), accum_out=ssq[:, hh:hh + 1], r=[t_o],
                       **({'w': [t_junk, t_ssq]} if hh == 0 else {'w': [t_junk], 'pw': [t_ssq]}))
            kb.rsqrt_eps(ssq[:, :], ssq[:, :], t_ssq, t_ssq)
            yield
            kb.tt('dve', o[:, :, :], o[:, :, :], bc3(ssq[:, 0:4]), ALU.mult, r=[t_o, t_ssq], w=[t_o])
            kb.tt('dve', onb[:, :], f2(o[:, :, :]), zt[:, :], ALU.mult, r=[t_o, t_zt], w=[t_onb])
            b1, tb1 = bk[4]
            b1b = b1[:, :].bitcast(BF16)
            for hh in range(4):
                kb.tr(b1b[:, hh * 128:(hh + 1) * 128], onb[:, hh * 128:(hh + 1) * 128], P.ident[:, :], r=[t_onb, P.t_const],
                      **({'w': [tb1]} if hh == 0 else {'pw': [tb1]}))
            kb.copy('act', f2(oT[:, :, :]), b1b[:, 0:512], r=[tb1], w=[t_oT])
            r0 = 2048 + hb * 512
            kb.dma('sp', scr['obT'][r0:r0 + 512, tok].rearrange("(j p) t -> p j t", p=128), oT[:, :, :], r=[t_oT])
            yield

        for _ in gen_pre(0):
            pass
        for t in range(T):
            gr = gen_rec(t)
            gp = gen_pre(t + 1) if t + 1 < T else iter(())
            alive = [True, True]
            while alive[0] or alive[1]:
                for k_ in range(2):
                    if alive[1]:
                        try:
                            next(gp)
                        except StopIteration:
                            alive[1] = False
                if alive[0]:
                    try:
                        next(gr)
                    except StopIteration:
                        alive[0] = False
    kb.s.barrier()
    A.off = base0


PK_NMIX, PK_DALAM, PK_NSB, PK_NFFN, PK_SSD, PK_GDN, PK_FFNC, PK_W = 0, 16, 280, 304, 320, 1472, 1600, 2048


def phase_merge(kb, A, P, L, scr, w_branch, nsb, t_pk, mT, t_mT):
    HL = 1024 if L >= 1024 else 512
    ob = A.alloc([128, 24, HL], BF16)
    t_ob = Tok('ob')
    G = GemmRes(kb, A, P.next_bank, elems=8 * 512)
    gt = [A.alloc([128, 4, HL], BF16) for _ in range(2)]
    t_gt = [Tok(), Tok()]
    acc = A.alloc([128, 4, HL], F32)
    t_acc = [Tok() for _ in range(4)]
    tmp = [A.alloc([128, 512], F32)] * 2
    t_tmp = [Tok()] * 2
    st = {'gi': 0, 'ti': 0}

    def cons(ps, tps, tl, sub, tb):
        ft, b, th = tl['ft'], tl['b'], tl['th']
        if sub == 0 and tb == 0:
            st['gi'] += 1
            gi = st['gi'] % 2
            r0 = b * 2048 + ft * 512
            kb.dma('sp', gt[gi][:, :, :], scr['gatesT'][r0:r0 + 512, th * HL:(th + 1) * HL].rearrange("(s p) t -> p s t", p=128),
                   w=[t_gt[gi]])
        gi = st['gi'] % 2
        cs = slice(tb * 512, (tb + 1) * 512)
        if b == 0:
            kb.tt('dve', acc[:, sub, cs], ps[:, :], gt[gi][:, sub, cs], ALU.mult, r=[tps, t_gt[gi]], w=[t_acc[sub]])
        else:
            st['ti'] += 1
            ti = st['ti'] % 2
            kb.tt('dve', tmp[ti][:, :], ps[:, :], gt[gi][:, sub, cs], ALU.mult, r=[tps, t_gt[gi]], w=[t_tmp[ti]])
            if b == 1:
                kb.tt('dve', acc[:, sub, cs], acc[:, sub, cs], tmp[ti][:, :], ALU.add, r=[t_tmp[ti], t_acc[sub]], w=[t_acc[sub]])
            else:
                kb.tt('dve', mT[:, ft * 4 + sub, th * HL + tb * 512:th * HL + (tb + 1) * 512], acc[:, sub, cs], tmp[ti][:, :],
                      ALU.add, r=[t_tmp[ti], t_acc[sub]], pw=[t_mT])

    def load_ob(th):
        def f():
            for b in range(3):
                kb.dma('sp', ob[:, b * 8:(b + 1) * 8, :],
                       scr['obT'][b * 1024:(b + 1) * 1024, th * HL:(th + 1) * HL].rearrange("(kc p) t -> p kc t", p=128),
                       **({'w': [t_ob]} if b == 0 else {'pw': [t_ob]}))
        return f

    tiles = []
    for th in range(L // HL):
        for ft in range(4):
            for b in range(3):
                tiles.append(dict(W=w_branch, k0=b * 1024, KC=8, c0=ft * 512, width=512, mode='fm', nscale=nsb[:, b * 8:(b + 1) * 8],
                                  t_nscale=t_pk, consume=cons, kx0=b * 8, ft=ft, b=b, th=th,
                                  cast=('act' if (ft * 3 + b) % 2 else 'pool'),
                                  pre=(load_ob(th) if (ft == 0 and b == 0) else None)))
    run_gemms(kb, G, ob, t_ob, HL, tiles)


def accum_consumer(kb, A, h_dst, t_h, wd):
    hx = [A.alloc([128, wd], F32) for _ in range(4)]
    t_hx = [Tok() for _ in range(4)]
    st = {'i': 0}

    def cons(ps, tps, tl, sub, t):
        i = st['i'] % 4
        st['i'] += 1
        j = tl['j']
        key = (t, j * wd)
        if key not in t_h:
            t_h[key] = Tok()
        rs, cs = slice(t * 128, (t + 1) * 128), slice(j * wd, (j + 1) * wd)
        kb.copy('dve' if i % 2 == 0 else 'act', hx[i][:, :], ps[:, 0:wd], r=[tps], w=[t_hx[i]])
        kb.s.op('pool', lambda e, i=i, rs=rs, cs=cs: e.dma_start(out=h_dst[rs, cs], in_=hx[i][:, :], accum_op=ALU.add),
                r=[t_hx[i]], w=[t_h[key]], dma=True)
    return cons


def phase_outproj(kb, A, P, L, w_out, mT, t_mT, h_src, h_dst):
    G = GemmRes(kb, A, P.next_bank, elems=16 * 512)
    t_h = {}
    cons = accum_consumer(kb, A, h_dst, t_h, 512)
    tiles = [dict(W=w_out, k0=0, KC=16, c0=j * 512, width=512, mode='tm', nscale=None, consume=cons, j=j, cast='act') for j in range(4)]
    run_gemms(kb, G, mT, t_mT, L, tiles)


def phase_ffn(kb, A, P, L, ffn_up, ffn_down, xT, t_xT, nffn, fc, t_pk, h):
    NCP = 8
    aT = A.alloc([128, NCP, L], BF16)
    t_aT = Tok('aT')
    G = GemmRes(kb, A, P.next_bank, elems=16 * 256)
    ub = [A.alloc([128, 2 + L], F32) for _ in range(2)]
    t_ub = [Tok(), Tok()]
    yv = [A.alloc([128, L], F32) for _ in range(2)]
    t_yv = [Tok(), Tok()]
    t_h = {}
    cons_down = accum_consumer(kb, A, h, t_h, 512)
    for b in range(2):
        kb.memset('dve', ub[b][:, 0:2], 0.0, w=[t_ub[b]])

    def cons_up(ps, tps, tl, sub, tb):
        kb.copy('act', ub[sub][:, 2 + tb * 512:2 + (tb + 1) * 512], ps[:, :], r=[tps], pw=[t_ub[sub]])
        if tb == L // 512 - 1:
            c = tl['chunk'] + (44 if sub == 1 else 0)
            w0, w1, w2, bb = [fc[:, c * 4 + j:c * 4 + j + 1] for j in range(4)]
            kb.ts('dve', yv[sub][:, :], ub[sub][:, 2:2 + L], w2, bb, ALU.mult, ALU.add, r=[t_ub[sub], t_pk], w=[t_yv[sub]])
            kb.stt('dve', yv[sub][:, :], ub[sub][:, 1:1 + L], w1, yv[sub][:, :], ALU.mult, ALU.add, r=[t_ub[sub], t_pk, t_yv[sub]], w=[t_yv[sub]])
            kb.stt('dve', yv[sub][:, :], ub[sub][:, 0:L], w0, yv[sub][:, :], ALU.mult, ALU.add, r=[t_ub[sub], t_pk, t_yv[sub]], w=[t_yv[sub]])
            if sub == 0:
                kb.act(yv[0][:, :], yv[0][:, :], AF.Silu, r=[t_yv[0]], w=[t_yv[0]])
            else:
                kb.tt('dve', aT[:, tl['cl'], :], yv[0][:, :], yv[1][:, :], ALU.mult, r=[t_yv[0], t_yv[1]], pw=[t_aT])

    tiles = []
    for c0 in range(0, 44, NCP):
        ncl = min(NCP, 44 - c0)
        for cl in range(ncl):
            ch = c0 + cl
            tiles.append(dict(W=ffn_up, k0=0, KC=16, c0=0, width=256, segs=[(ch * 128, 128), (DFF + ch * 128, 128)], mode='fm',
                              nscale=nffn, t_nscale=t_pk, consume=cons_up, chunk=ch, cl=cl))
        tiles += [dict(W=ffn_down, k0=c0 * 128, KC=ncl, c0=j * 512, width=512, mode='tm', nscale=None, consume=cons_down, j=j,
                       cast='act', xT=aT, t_xT=t_aT) for j in range(4)]
    run_gemms(kb, G, xT, t_xT, L, tiles)


def phase_final(kb, A, P, L, h, out, wrep_d):
    wrep = A.alloc([128, D], F32)
    t_w = Tok()
    kb.dma('sp', wrep[:, :], wrep_d[:, :], w=[t_w])
    ld = [A.alloc([128, D], F32) for _ in range(2)]
    t_ld = [Tok(), Tok()]
    junk = A.alloc([128, D], BF16)
    t_junk = Tok()
    ssq = [A.alloc([128, 1], F32) for _ in range(2)]
    t_ssq = [Tok(), Tok()]
    for t in range(L // 128):
        b = t % 2
        kb.dma('sp', ld[b][:, :], h[t * 128:(t + 1) * 128, :], w=[t_ld[b]])
        kb.act(junk[:, :], ld[b][:, :], AF.Square, scale=float(D ** -0.5), accum_out=ssq[b][:, :], r=[t_ld[b]], w=[t_junk, t_ssq[b]])
        kb.rsqrt_eps(ssq[b][:, :], ssq[b][:, :], t_ssq[b], t_ssq[b])
        kb.stt('dve', ld[b][:, :], ld[b][:, :], ssq[b][:, :], wrep[:, :], ALU.mult, ALU.mult, r=[t_ld[b], t_ssq[b], t_w], w=[t_ld[b]])
        kb.dma('sp', out[t * 128:(t + 1) * 128, :], ld[b][:, :], r=[t_ld[b]])


ARENA_BASE, ARENA_LIMIT = 16640, 229376


def pack_layer(l, p):
    pk = np.zeros((128, PK_W), np.float32)
    f = lambda a: np.asarray(a, np.float32)
    pk[:, PK_NMIX:PK_NMIX + 16] = f(p['norm_mix'][l]).reshape(16, 128).T
    pk[:, PK_DALAM:PK_DALAM + 256] = f(p['da_lambda'][l]).reshape(1, 256)
    pk[:, PK_DALAM + 256] = f(p['da_subln'][l])
    nsb = np.ones((24, 128), np.float32)
    nsb[8:16] = f(p['ssm_norm'][l]).reshape(8, 128)
    nsb[16:24] = f(p['gdn_norm'][l])[None, :]
    pk[:, PK_NSB:PK_NSB + 24] = nsb.T
    pk[:, PK_NFFN:PK_NFFN + 16] = f(p['norm_ffn'][l]).reshape(16, 128).T
    cw, cb = f(p['ssm_conv_w'][l]), f(p['ssm_conv_b'][l])
    for cc in range(12):
        pk[:, PK_SSD + cc * 4:PK_SSD + cc * 4 + 4] = cw[:, cc * 128:(cc + 1) * 128].T
        pk[:, PK_SSD + 48 + cc] = cb[cc * 128:(cc + 1) * 128]
    pk[:, PK_SSD + 64:PK_SSD + 80] = f(p['ssm_dt_bias'][l])[None, :]
    pk[:, PK_SSD + 80:PK_SSD + 96] = f(p['ssm_a_log'][l])[None, :]
    pk[:, PK_SSD + 96:PK_SSD + 1120] = np.repeat(f(p['ssm_d'][l]), 64)[None, :]
    gw = f(p['gdn_conv_w'][l])
    for cc in range(24):
        pk[:, PK_GDN + cc * 4:PK_GDN + cc * 4 + 4] = gw[:, cc * 128:(cc + 1) * 128].T
    pk[:, PK_GDN + 96:PK_GDN + 104] = f(p['gdn_dt_bias'][l])[None, :]
    pk[:, PK_GDN + 104:PK_GDN + 112] = f(p['gdn_a_log'][l])[None, :]
    fw, fb = f(p['ffn_conv_w'][l]), f(p['ffn_conv_b'][l])
    for c in range(88):
        pk[:, PK_FFNC + c * 4:PK_FFNC + c * 4 + 3] = fw[:, c * 128:(c + 1) * 128].T
        pk[:, PK_FFNC + c * 4 + 3] = fb[c * 128:(c + 1) * 128]
    return pk


def build_program(L, depth=2, scr_kind="Internal"):
    nc = bass.Bass("TRN2", target_bir_lowering=False)
    kb = KB(nc)
    EI = "ExternalInput"
    x = kb.dram('x', [L, D], F32, EI)
    w_in = kb.dram('w_in', [depth, D, 15904], F32, EI)
    w_branch = kb.dram('w_branch', [depth, 3072, D], F32, EI)
    w_out = kb.dram('w_out', [depth, D, D], F32, EI)
    ffn_up = kb.dram('ffn_up', [depth, D, 2 * DFF], F32, EI)
    ffn_down = kb.dram('ffn_down', [depth, DFF, D], F32, EI)
    pkd = kb.dram('pk', [depth, 128, PK_W], F32, EI)
    cd = kb.dram('consts', [128, 1152], F32, EI)
    augd = kb.dram('c_aug', [8, 2, 3, L], BF16, EI)
    wfin = kb.dram('wfin', [128, D], F32, EI)
    out = kb.dram('out', [L, D], F32, "ExternalOutput")
    h = kb.dram('h_scr', [L, D], F32, scr_kind)
    scr = make_scratch(kb, L, scr_kind)
    P = Persist(kb, None)
    A = Arena(kb, ARENA_BASE, ARENA_LIMIT)
    setup_consts(kb, A, P, cd)
    pks = A.alloc([128, PK_W], F32)
    t_pk = Tok('pk')
    par = A.alloc([128, 8], F32)
    t_par = Tok('par')
    A.base = A.off
    S = kb.s
    for a in range(0, L, 512):
        kb.dma('sp', h[a:a + 512, :], x[a:a + 512, :])
    for l in range(depth):
        lambda_init = 0.8 - 0.6 * float(np.exp(-0.3 * l))
        S.barrier()
        A.reset()
        kb.dma('sp', pks[:, :], pkd[l, :, :], w=[t_pk])
        h_src = h
        xT = A.alloc([128, 16, L], BF16)
        t_xT = Tok('xT')
        mark = A.off
        phase_norm_T(kb, A, P, h_src, xT, t_xT, L)
        S.barrier()
        A.off = mark
        phase_inproj(kb, A, P, xT, t_xT, L, w_in[l], pks[:, PK_NMIX:PK_NMIX + 16], t_pk, scr)
        S.barrier()
        A.reset()
        prep_da_params(kb, A, pks[:, PK_DALAM:PK_DALAM + 257], t_pk, par, t_par, lambda_init)
        phase_da(kb, A, P, L, scr, augd, par[:, 0:1], par[:, 1:2], t_par, lambda_init)
        S.barrier()
        A.reset()
        phase_ssd(kb, A, P, L, scr, pks[:, PK_SSD:PK_SSD + 1152], t_pk)
        S.barrier()
        A.reset()
        phase_gdn(kb, A, P, L, scr, pks[:, PK_GDN:PK_GDN + 128], t_pk)
        S.barrier()
        A.reset()
        mT = A.alloc([128, 16, L], BF16)
        t_mT = Tok('mT')
        mark = A.off
        phase_merge(kb, A, P, L, scr, w_branch[l], pks[:, PK_NSB:PK_NSB + 24], t_pk, mT, t_mT)
        S.barrier()
        A.off = mark
        phase_outproj(kb, A, P, L, w_out[l], mT, t_mT, h_src, h)
        S.barrier()
        A.reset()
        xT = A.alloc([128, 16, L], BF16)
        t_xT = Tok('xT2')
        mark = A.off
        phase_norm_T(kb, A, P, h, xT, t_xT, L)
        S.barrier()
        A.off = mark
        phase_ffn(kb, A, P, L, ffn_up[l], ffn_down[l], xT, t_xT, pks[:, PK_NFFN:PK_NFFN + 16], pks[:, PK_FFNC:PK_FFNC + 352], t_pk, h)
    S.barrier()
    A.reset()
    phase_final(kb, A, P, L, h, out, wfin)
    cnt = kb.s.finalize_and_emit()
    return nc, cnt


_CACHE = {}


def kernel(**inputs):
    p = {k: np.asarray(v) for k, v in inputs.items()}
    x = p['x']
    B, L, _ = x.shape
    depth = p['w_in'].shape[0]
    key = (L, depth)
    if key not in _CACHE:
        _CACHE[key] = build_program(L, depth)
    nc, _ = _CACHE[key]
    pk = np.stack([pack_layer(l, p) for l in range(depth)])
    consts = host_consts()
    aug = host_da_aug(L)
    wfin = np.ascontiguousarray(np.broadcast_to(p['norm_final'].astype(np.float32)[None, :], (128, D)))
    shared = dict(w_in=np.ascontiguousarray(p['w_in'], np.float32), w_branch=np.ascontiguousarray(p['w_branch'], np.float32),
                  w_out=np.ascontiguousarray(p['w_out'], np.float32), ffn_up=np.ascontiguousarray(p['ffn_up'], np.float32),
                  ffn_down=np.ascontiguousarray(p['ffn_down'], np.float32), pk=pk, consts=consts, c_aug=aug, wfin=wfin)
    in_maps = [dict(shared, x=np.ascontiguousarray(x[b], np.float32)) for b in range(B)]
    res = run_bass_kernel_spmd(nc, in_maps, core_ids=list(range(B)))
    return np.stack([np.asarray(r['out'], np.float32) for r in res.results]).astype(np.float32)
```

```python
import numpy as np
import ml_dtypes
import concourse.bass as bass
import concourse.mybir as mybir
from concourse.bass_utils import run_bass_kernel_spmd

F32 = mybir.dt.float32
BF16 = mybir.dt.bfloat16
AF = mybir.ActivationFunctionType
ALU = mybir.AluOpType
AX = mybir.AxisListType

ENGS = ['pe', 'act', 'dve', 'pool', 'sp']
NSEM_DMA = 14


class Tok:
    __slots__ = ('name', 'writers', 'readers', 'prev_readers')

    def __init__(self, name=''):
        self.name = name
        self.writers = []
        self.readers = []
        self.prev_readers = []


class _Op:
    __slots__ = ('eng', 'fn', 'waits', 'idx', 'dma', 'dma_k', 'target', 'val')

    def __init__(self, eng, fn, idx, dma):
        self.eng = eng
        self.fn = fn
        self.idx = idx
        self.dma = dma
        self.dma_k = None
        self.waits = []
        self.target = False
        self.val = None


class Sched:
    def __init__(self, nc):
        self.nc = nc
        self.ops = {e: [] for e in ENGS}
        self.ndma = {e: 0 for e in ENGS}
        self.dma_ops = {e: [] for e in ENGS}
        self.wc = {e: {} for e in ENGS}
        self.wd = {e: set() for e in ENGS}
        self.pending = {e: [] for e in ENGS}

    def barrier(self):
        evs = []
        for e in ENGS:
            comp = [o for o in self.ops[e] if not o.dma]
            if comp:
                evs.append(('c', e, comp[-1].idx))
            for o in self.dma_ops[e][-NSEM_DMA:]:
                evs.append(('d', e, o.dma_k))
        for e in ENGS:
            self.pending[e] = list(evs)

    def _add_wait(self, op, ev):
        e = op.eng
        if ev[0] == 'c':
            src, idx = ev[1], ev[2]
            if src == e:
                if e == 'pe':
                    return
                if op.dma:
                    pass
                elif op.idx - idx > 3:
                    return
            if self.wc[e].get(src, -1) >= idx:
                return
            self.wc[e][src] = idx
            op.waits.append(ev)
        else:
            if ev in self.wd[e]:
                return
            self.wd[e].add(ev)
            op.waits.append(ev)

    def op(self, eng, fn, r=(), w=(), pw=(), dma=False):
        lst = self.ops[eng]
        o = _Op(eng, fn, len(lst), dma)
        if dma:
            k = self.ndma[eng]
            self.ndma[eng] += 1
            o.dma_k = k
            if k >= NSEM_DMA:
                self._add_wait(o, ('d', eng, k - NSEM_DMA))
            ev = ('d', eng, k)
            self.dma_ops[eng].append(o)
        else:
            ev = ('c', eng, o.idx)
        if self.pending[eng]:
            for pe_ in self.pending[eng]:
                self._add_wait(o, pe_)
            self.pending[eng] = []
        for t in r:
            for we in t.writers:
                self._add_wait(o, we)
        for t in w:
            for we in t.writers:
                self._add_wait(o, we)
            for re_ in t.readers:
                self._add_wait(o, re_)
            for re_ in t.prev_readers:
                self._add_wait(o, re_)
        for t in pw:
            if t.readers:
                t.prev_readers = t.readers
                t.readers = []
                t.writers = []
            for re_ in t.prev_readers:
                self._add_wait(o, re_)
        for t in r:
            t.readers.append(ev)
            if len(t.readers) > 64:
                t.readers = _compact(t.readers)
        for t in w:
            t.writers = [ev]
            t.readers = []
            t.prev_readers = []
        for t in pw:
            t.writers.append(ev)
            if len(t.writers) > 64:
                t.writers = _compact(t.writers)
        lst.append(o)
        return o

    def finalize_and_emit(self, final_waits=()):
        nc = self.nc
        for e in ENGS:
            for o in self.ops[e]:
                for ev in o.waits:
                    if ev[0] == 'c':
                        self.ops[ev[1]][ev[2]].target = True
        fin = []
        for e in ENGS:
            comp = [o for o in self.ops[e] if not o.dma]
            if comp:
                comp[-1].target = True
                fin.append(('c', e, comp[-1].idx))
            for o in self.dma_ops[e][-NSEM_DMA:]:
                fin.append(('d', e, o.dma_k))
        for e in ENGS:
            c = 0
            for o in self.ops[e]:
                if o.dma:
                    continue
                if o.target:
                    c += 1
                o.val = c
        sems = {}
        dsems = {}
        import contextlib
        with contextlib.ExitStack() as st:
            for e in ENGS:
                sems[e] = st.enter_context(nc.semaphore('s_' + e))
                if self.ndma[e]:
                    dsems[e] = [st.enter_context(nc.semaphore('d_%s_%d' % (e, i))) for i in range(NSEM_DMA)]
            block = st.enter_context(nc.Block())

            def wait_ev(engh, ev):
                if ev[0] == 'c':
                    engh.wait_ge(sems[ev[1]], self.ops[ev[1]][ev[2]].val)
                else:
                    k = ev[2]
                    engh.wait_ge(dsems[ev[1]][k % NSEM_DMA], 16 * (k // NSEM_DMA + 1))

            def emit(e, engh):
                for o in self.ops[e]:
                    for ev in o.waits:
                        wait_ev(engh, ev)
                    ins = o.fn(engh)
                    if o.dma:
                        ins.then_inc(dsems[e][o.dma_k % NSEM_DMA], 16)
                    elif o.target:
                        ins.then_inc(sems[e], 1)
                if e == 'sp':
                    for ev in fin:
                        wait_ev(engh, ev)

            @block.tensor
            def _(h):
                emit('pe', h)

            @block.scalar
            def _(h):
                emit('act', h)

            @block.vector
            def _(h):
                emit('dve', h)

            @block.gpsimd
            def _(h):
                emit('pool', h)

            @block.sync
            def _(h):
                emit('sp', h)
        return {e: len(self.ops[e]) for e in ENGS}


def _compact(evs):
    best = {}
    out = []
    for ev in evs:
        if ev[0] == 'c':
            if best.get(ev[1], -1) < ev[2]:
                best[ev[1]] = ev[2]
        else:
            out.append(ev)
    return [('c', e, i) for e, i in best.items()] + out


class KB:
    def __init__(self, nc):
        self.nc = nc
        self.s = Sched(nc)
        self._n = 0

    def sb(self, shape, dt, name=None):
        self._n += 1
        return self.nc.alloc_sbuf_tensor(name or ('t%d' % self._n), list(shape), dt)

    def ps(self, shape, dt, name=None):
        self._n += 1
        return self.nc.alloc_psum_tensor(name or ('p%d' % self._n), list(shape), dt)

    def dram(self, name, shape, dt, kind="Internal"):
        return self.nc.dram_tensor(name, list(shape), dt, kind=kind).ap()

    def dma(self, q, out, in_, r=(), w=(), pw=()):
        return self.s.op(q, lambda e: e.dma_start(out=out, in_=in_), r=r, w=w, pw=pw, dma=True)

    def mm(self, out, lhsT, rhs, start=True, stop=True, r=(), w=(), pw=()):
        return self.s.op('pe', lambda e: e.matmul(out, lhsT, rhs, start=start, stop=stop), r=r, w=w, pw=pw)

    def tr(self, out, in_, ident, r=(), w=(), pw=()):
        return self.s.op('pe', lambda e: e.transpose(out, in_, ident), r=r, w=w, pw=pw)

    def act(self, out, in_, func, bias=None, scale=None, accum_out=None, r=(), w=(), pw=(), eng='act'):
        kw = {}
        if bias is not None:
            kw['bias'] = bias
        if scale is not None:
            kw['scale'] = scale
        if accum_out is not None:
            kw['accum_out'] = accum_out
        return self.s.op('act', lambda e: e.activation(out, in_, func, **kw), r=r, w=w, pw=pw)

    def copy(self, eng, out, in_, r=(), w=(), pw=()):
        if eng == 'act':
            return self.s.op('act', lambda e: e.copy(out, in_), r=r, w=w, pw=pw)
        return self.s.op(eng, lambda e: e.tensor_copy(out, in_), r=r, w=w, pw=pw)

    def tt(self, eng, out, in0, in1, op, r=(), w=(), pw=()):
        return self.s.op(eng, lambda e: e.tensor_tensor(out, in0, in1, op), r=r, w=w, pw=pw)

    def ts(self, eng, out, in0, s1, s2, op0, op1=None, accum_out=None, r=(), w=(), pw=()):
        def f(e):
            kw = {}
            if accum_out is not None:
                kw['accum_out'] = accum_out
            if op1 is None:
                return e.tensor_scalar(out, in0, s1, None, op0, **kw)
            return e.tensor_scalar(out, in0, s1, s2, op0, op1, **kw)
        return self.s.op(eng, f, r=r, w=w, pw=pw)

    def stt(self, eng, out, in0, scalar, in1, op0, op1, accum_out=None, r=(), w=(), pw=()):
        def f(e):
            kw = {}
            if accum_out is not None:
                kw['accum_out'] = accum_out
            return e.scalar_tensor_tensor(out, in0, scalar, in1, op0, op1, **kw)
        return self.s.op(eng, f, r=r, w=w, pw=pw)

    def rsqrt_eps(self, out, in_, t_in, t_out, eps=1e-6):
        self.s.op('act', lambda e: e.activation(out, in_, AF.Ln, bias=float(eps)), r=[t_in], w=[t_out])
        self.s.op('act', lambda e: e.activation(out, out, AF.Exp, scale=-0.5), r=[t_out], w=[t_out])

    def memset(self, eng, ap, val, r=(), w=(), pw=()):
        return self.s.op(eng, lambda e: e.memset(ap, val), r=r, w=w, pw=pw)


class Arena:
    def __init__(self, kb, base, limit):
        self.kb = kb
        self.base = base
        self.off = base
        self.limit = limit
        self.n = 0

    def reset(self):
        self.off = self.base

    def alloc(self, shape, dt, name=None):
        nb = int(np.prod(shape[1:])) * (4 if dt == F32 else 2)
        nb = (nb + 31) // 32 * 32
        self.n += 1
        h = self.kb.nc.alloc_sbuf_tensor_at(name or ('a%d' % self.n), list(shape), dt, offset=self.off)
        self.off += nb
        assert self.off <= self.limit, ('SBUF arena overflow', self.off, self.limit)
        return h


class GemmRes:
    def __init__(self, kb, arena, next_bank, elems=16 * 512):
        self.elems = elems
        self.wst = [arena.alloc([128, elems], F32) for _ in range(2)]
        self.wbf = [arena.alloc([128, elems], BF16) for _ in range(2)]
        self.t_wst = [Tok('wst%d' % i) for i in range(2)]
        self.t_wbf = [Tok('wbf%d' % i) for i in range(2)]
        self.next_bank = next_bank


def run_gemms(kb, G, xT, t_xT, L, tiles):
    n = len(tiles)

    def view(buf, KC, wd):
        return buf[:, 0:KC * wd].rearrange("p (k c) -> p k c", k=KC)

    def load(j):
        tl = tiles[j]
        KC, wd = tl['KC'], tl['width']
        assert KC * wd <= G.elems
        buf, tk = view(G.wst[j % 2], KC, wd), G.t_wst[j % 2]
        segs = tl.get('segs') or [(tl['c0'], wd)]
        step = 4
        first = True
        off = 0
        for (c0, w_) in segs:
            for a in range(0, KC, step):
                b = min(KC, a + step)
                src = tl['W'][tl['k0'] + a * 128: tl['k0'] + b * 128, c0:c0 + w_].rearrange("(kc p) c -> p kc c", p=128)
                if first:
                    kb.dma('sp', buf[:, a:b, off:off + w_], src, w=[tk])
                    first = False
                else:
                    kb.dma('sp', buf[:, a:b, off:off + w_], src, pw=[tk])
            off += w_

    def cast(j):
        tl = tiles[j]
        KC, wd = tl['KC'], tl['width']
        src, dst = view(G.wst[j % 2], KC, wd), view(G.wbf[j % 2], KC, wd)
        ce = tl.get('cast', 'pool')
        if tl.get('nscale') is not None:
            ns = tl['nscale']
            if ce == 'act':
                for kc in range(KC):
                    kb.act(dst[:, kc, :], src[:, kc, :], AF.Copy, scale=ns[:, kc:kc + 1], r=[G.t_wst[j % 2], tl['t_nscale']],
                           **({'w': [G.t_wbf[j % 2]]} if kc == 0 else {'pw': [G.t_wbf[j % 2]]}))
            else:
                kb.tt('pool', dst, src, ns[:, 0:KC].unsqueeze(2).to_broadcast([128, KC, wd]), ALU.mult,
                      r=[G.t_wst[j % 2], tl['t_nscale']], w=[G.t_wbf[j % 2]])
        else:
            kb.copy(ce, dst, src, r=[G.t_wst[j % 2]], w=[G.t_wbf[j % 2]])

    load(0)
    if n > 1:
        load(1)
    cast(0)
    for j in range(n):
        tl = tiles[j]
        KC, wd = tl['KC'], tl['width']
        if tl.get('pre') is not None:
            tl['pre']()
        if j + 1 < n:
            cast(j + 1)
        if j + 2 < n:
            load(j + 2)
        wb, twb = view(G.wbf[j % 2], KC, wd), G.t_wbf[j % 2]
        kx = tl.get('kx0', 0)
        x_, tx_ = tl.get('xT', xT), tl.get('t_xT', t_xT)
        if tl['mode'] == 'tm':
            for t in range(L // 128):
                ps, tps = G.next_bank()
                for kc in range(KC):
                    kb.mm(ps[:, 0:wd], x_[:, kx + kc, t * 128:(t + 1) * 128], wb[:, kc, 0:wd],
                          start=(kc == 0), stop=(kc == KC - 1), r=[tx_, twb], pw=[tps])
                tl['consume'](ps, tps, tl, 0, t)
        else:
            for sub in range(wd // 128):
                for tb in range(L // 512):
                    ps, tps = G.next_bank()
                    for kc in range(KC):
                        kb.mm(ps[:, 0:512], wb[:, kc, sub * 128:(sub + 1) * 128], x_[:, kx + kc, tb * 512:(tb + 1) * 512],
                              start=(kc == 0), stop=(kc == KC - 1), r=[tx_, twb], pw=[tps])
                    tl['consume'](ps, tps, tl, sub, tb)


D = 2048
DFF = 5632
SEG = dict(da_q=(0, 1024), da_k=(1024, 1024), da_v=(2048, 1024), ssm_z=(3072, 1024), ssm_xbc=(4096, 1536),
           ssm_dt=(5632, 16), gdn_qkv=(5648, 3072), gdn_z=(8720, 1024), gdn_ba=(9744, 16), gates=(9760, 6144))
EPS = 1e-6


class Evac:
    def __init__(self, kb, arena, n=4):
        self.kb = kb
        self.f = [arena.alloc([128, 512], F32) for _ in range(n)]
        self.tf = [Tok('evf%d' % i) for i in range(n)]
        self.b = [arena.alloc([128, 512], BF16) for _ in range(n)]
        self.tb = [Tok('evb%d' % i) for i in range(n)]
        self.i = 0
        self.n = n

    def get(self, dt):
        i = self.i % self.n
        self.i += 1
        if dt == F32:
            return self.f[i], self.tf[i]
        return self.b[i], self.tb[i]

    def eng(self):
        return 'act' if (self.i % 2 == 0) else 'dve'


def store_consumer(kb, EV, dst, dt, func=None, q='sp'):
    def consume(ps, tps, tl, sub, tb):
        wd = tl['width']
        st, tst = EV.get(dt)
        if tl['mode'] == 'tm':
            n = wd
            d = dst[tb * 128:(tb + 1) * 128, tl['doff']:tl['doff'] + wd]
        else:
            n = 512
            f0 = tl['doff'] + sub * 128
            d = dst[f0:f0 + 128, tb * 512:(tb + 1) * 512]
        if func is not None:
            kb.act(st[:, 0:n], ps[:, 0:n], func, r=[tps], w=[tst])
        else:
            kb.copy(EV.eng(), st[:, 0:n], ps[:, 0:n], r=[tps], w=[tst])
        kb.dma(q, d, st[:, 0:n], r=[tst])
    return consume


def phase_norm_T(kb, A, P, h_dram, xT, t_xT, L):
    ld = [A.alloc([128, D], F32) for _ in range(2)]
    t_ld = [Tok() for _ in range(2)]
    xn = [A.alloc([128, D], BF16) for _ in range(2)]
    t_xn = [Tok() for _ in range(2)]
    junk = A.alloc([128, D], BF16)
    t_junk = Tok()
    ssq = [A.alloc([128, 1], F32) for _ in range(2)]
    t_ssq = [Tok() for _ in range(2)]
    rstd = [A.alloc([128, 1], F32) for _ in range(2)]
    t_rstd = [Tok() for _ in range(2)]
    for t in range(L // 128):
        b = t % 2
        kb.dma('sp', ld[b][:, :], h_dram[t * 128:(t + 1) * 128, :], w=[t_ld[b]])
        kb.act(junk[:, :], ld[b][:, :], AF.Square, scale=float(D ** -0.5), accum_out=ssq[b][:, :], r=[t_ld[b]], w=[t_junk, t_ssq[b]])
        kb.rsqrt_eps(rstd[b][:, :], ssq[b][:, :], t_ssq[b], t_rstd[b])
        kb.ts('dve', xn[b][:, :], ld[b][:, :], rstd[b][:, :], None, ALU.mult, r=[t_ld[b], t_rstd[b]], w=[t_xn[b]])
        for g in range(2):
            ps, tps = P.next_bank()
            psb = ps[:, :].bitcast(BF16)
            for i in range(8):
                c = (g * 8 + i) * 128
                kb.tr(psb[:, i * 128:(i + 1) * 128], xn[b][:, c:c + 128], P.ident[:, :], r=[t_xn[b], P.t_const], pw=[tps])
            eng = 'dve' if g == 0 else 'act'
            kb.copy(eng, xT[:, g * 8:(g + 1) * 8, t * 128:(t + 1) * 128],
                    psb[:, 0:1024].rearrange("p (a b) -> p a b", a=8), r=[tps], pw=[t_xT])


class Persist:
    def __init__(self, kb, consts):
        self.kb = kb
        nc = kb.nc
        self.banks = []
        for i in range(8):
            self.banks.append((kb.ps([128, 512], F32, 'bank%d' % i), Tok('bank%d' % i)))
        self.bi = 0
        self.t_const = Tok('const')
        self.consts = consts

    def next_bank(self):
        b = self.banks[self.bi % 8]
        self.bi += 1
        return b


def phase_inproj(kb, A, P, xT, t_xT, L, w_in, nscale, t_nscale, scr):
    G = GemmRes(kb, A, P.next_bank)
    EV = Evac(kb, A)
    tiles = []

    def add(seg, mode, dst, dt, dbase=0, func=None):
        c0, n = SEG[seg]
        cons = store_consumer(kb, EV, dst, dt, func)
        for a in range(0, n, 512):
            wd = min(512, n - a)
            tiles.append(dict(W=w_in, k0=0, KC=16, c0=c0 + a, width=wd, mode=mode, nscale=nscale,
                              t_nscale=t_nscale, consume=cons, doff=dbase + a))
    add('da_q', 'fm', scr['qkT'], BF16, 0)
    add('da_k', 'fm', scr['qkT'], BF16, 1024)
    add('da_v', 'tm', scr['v_tm'], BF16)
    add('ssm_z', 'tm', scr['sz'], F32)
    add('ssm_xbc', 'fm', scr['xbcT'], F32)
    add('ssm_dt', 'tm', scr['sdt'], F32)
    add('gdn_qkv', 'fm', scr['gqkvT'], F32)
    add('gdn_z', 'tm', scr['gz'], F32)
    add('gdn_ba', 'tm', scr['gba'], F32)
    add('gates', 'fm', scr['gatesT'], BF16, 0, AF.Sigmoid)
    run_gemms(kb, G, xT, t_xT, L, tiles)


def make_scratch(kb, L, kind="Internal"):
    scr = {}
    scr['qkT'] = kb.dram('scr_qkT', [2048, L], BF16, kind)
    scr['v_tm'] = kb.dram('scr_v', [L, 1024], BF16, kind)
    scr['sz'] = kb.dram('scr_sz', [L, 1024], F32, kind)
    scr['xbcT'] = kb.dram('scr_xbcT', [1536, L], F32, kind)
    scr['sdt'] = kb.dram('scr_sdt', [L, 16], F32, kind)
    scr['gqkvT'] = kb.dram('scr_gqkvT', [3072, L], F32, kind)
    scr['gz'] = kb.dram('scr_gz', [L, 1024], F32, kind)
    scr['gba'] = kb.dram('scr_gba', [L, 16], F32, kind)
    scr['gatesT'] = kb.dram('scr_gatesT', [6144, L], BF16, kind)
    scr['obT'] = kb.dram('scr_obT', [3072, L], BF16, kind)
    return scr


DA_SLOPES = [2.0 ** (-(h + 1)) for h in range(8)]


def host_da_aug(L):
    t = np.arange(L)
    out = np.zeros((8, 2, 3, L), np.float32)
    for h in range(8):
        sl = DA_SLOPES[h]
        qr = t % 512
        kr = t % 128
        out[h, 0, 0] = -8.0 * sl * (2 * (qr // 2))
        out[h, 0, 1] = -8.0 * sl * (qr % 2)
        out[h, 0, 2] = 1.0
        out[h, 1, 0] = 1.0
        out[h, 1, 1] = 1.0
        out[h, 1, 2] = 8.0 * sl * kr
    return out.astype(ml_dtypes.bfloat16)


def phase_da(kb, A, P, L, scr, c_aug, lam_neg, sw, t_par, lambda_init):
    NQ = L // 512
    QT = [[A.alloc([67, L], BF16) for _ in range(2)] for _ in range(2)]
    KT = [[A.alloc([67, L], BF16) for _ in range(2)] for _ in range(2)]
    V = [A.alloc([128, L // 128, 128], BF16) for _ in range(2)]
    t_qkv = [Tok('qkv%d' % i) for i in range(2)]
    NPT = 4
    PT = [A.alloc([128, 512], BF16) for _ in range(NPT)]
    t_PT = [Tok('pt%d' % i) for i in range(NPT)]
    rl = [A.alloc([128, 512], F32) for _ in range(2)]
    t_rl = [Tok(), Tok()]
    on = [A.alloc([128, 512], F32) for _ in range(2)]
    t_on = [Tok(), Tok()]
    o = A.alloc([128, 512], F32)
    t_o = Tok()
    sq = A.alloc([128, 512], F32)
    t_sq = Tok()
    o2 = A.alloc([128, 512], F32)
    t_o2 = Tok()
    rs = A.alloc([128, 512], F32)
    t_rs = Tok()
    ob = [A.alloc([128, 512], BF16) for _ in range(2)]
    t_ob = [Tok(), Tok()]
    S_b = [P.banks[0], P.banks[1]]
    O_b = [P.banks[2], P.banks[3]]
    L_b = [P.banks[4], P.banks[5]]
    N_b = P.banks[6]
    st = {'pti': 0, 'si': 0, 'dq_evac': [], 'dq_tail': []}

    def load_head(h):
        s = h % 2
        first = True
        for i in range(2):
            for (dst, base) in ((QT[s][i], 0), (KT[s][i], 1024)):
                r0 = base + h * 128 + i * 64
                if first:
                    kb.dma('sp', dst[0:64, :], scr['qkT'][r0:r0 + 64, :], w=[t_qkv[s]])
                    first = False
                else:
                    kb.dma('sp', dst[0:64, :], scr['qkT'][r0:r0 + 64, :], pw=[t_qkv[s]])
            kb.dma('sp', QT[s][i][64:67, :], c_aug[h, 0, :, :], pw=[t_qkv[s]])
            kb.dma('sp', KT[s][i][64:67, :], c_aug[h, 1, :, :], pw=[t_qkv[s]])
        kb.dma('sp', V[s][:, :, :], scr['v_tm'][:, h * 128:(h + 1) * 128].rearrange("(t p) e -> p t e", p=128),
               pw=[t_qkv[s]])

    load_head(0)
    for h in range(8):
        s = h % 2
        if h + 1 < 8:
            load_head(h + 1)
        sl = DA_SLOPES[h]
        for j in range(NQ):
            nk = 4 * (j + 1)
            for i in range(2):
                Ob, tO = O_b[i]
                Lb, tL = L_b[i]
                pend = {}

                def emit_S(kt, j=j, i=i, s=s, sl=sl, pend=pend):
                    c = kt - 4 * j
                    c0 = 128 * c if c > 0 else 0
                    Sb, tS = S_b[st['si'] % 2]
                    st['si'] += 1
                    kb.mm(Sb[:, c0:512], KT[s][i][0:67, kt * 128:(kt + 1) * 128], QT[s][i][0:67, j * 512 + c0:(j + 1) * 512],
                          start=True, stop=(c < 0), r=[t_qkv[s]], w=[tS])
                    if c >= 0:
                        kb.mm(Sb[:, c0:c0 + 128], P.ident[:, :], P.negtri_bf[:, :], start=False, stop=True, r=[P.t_const], pw=[tS])
                    pt, tpt = PT[st['pti'] % NPT], t_PT[st['pti'] % NPT]
                    st['pti'] += 1
                    kb.act(pt[:, c0:512], Sb[:, c0:512], AF.Exp, bias=float(sl * (kt * 128 - j * 512)), scale=0.125,
                           r=[tS], w=[tpt])
                    pend[kt] = (pt, tpt, c0)

                def emit_AV(kt, nk=nk, s=s, Ob=Ob, tO=tO, Lb=Lb, tL=tL, pend=pend):
                    pt, tpt, c0 = pend.pop(kt)
                    kb.mm(Ob[:, c0:512], V[s][:, kt, :], pt[:, c0:512], start=(kt == 0), stop=(kt == nk - 1),
                          r=[tpt, t_qkv[s]], pw=[tO])
                    kb.mm(Lb[:, c0:512], P.ones_bf[:, :], pt[:, c0:512], start=(kt == 0), stop=(kt == nk - 1),
                          r=[tpt, P.t_const], pw=[tL])

                emit_S(0)
                for kt in range(nk):
                    if kt + 1 < nk:
                        emit_S(kt + 1)
                    emit_AV(kt)
                    if kt == 0:
                        for f in st['dq_evac']:
                            f()
                        st['dq_evac'] = []
                    if kt == 2:
                        for f in st['dq_tail']:
                            f()
                        st['dq_tail'] = []

                def evac(i=i, Ob=Ob, tO=tO, Lb=Lb, tL=tL):
                    kb.act(rl[i][:, :], Lb[:, :], AF.Ln, r=[tL], w=[t_rl[i]])
                    kb.act(rl[i][:, :], rl[i][:, :], AF.Exp, scale=-1.0, r=[t_rl[i]], w=[t_rl[i]])
                    kb.tt('dve', on[i][:, :], Ob[:, :], rl[i][:, :], ALU.mult, r=[tO, t_rl[i]], w=[t_on[i]])
                st['dq_evac'].append(evac)

            def tail(h=h, j=j):
                kb.stt('dve', o[:, :], on[1][:, :], lam_neg, on[0][:, :], ALU.mult, ALU.add, r=[t_on[0], t_on[1], t_par], w=[t_o])
                kb.act(sq[:, :], o[:, :], AF.Square, r=[t_o], w=[t_sq])
                Nb, tN = N_b
                kb.mm(Nb[:, :], P.ones_f32[:, :], sq[:, :], r=[t_sq, P.t_const], w=[tN])
                kb.act(rs[:, :], Nb[:, :], AF.Ln, scale=1.0 / 128, bias=float(EPS), r=[tN], w=[t_rs])
                kb.act(rs[:, :], rs[:, :], AF.Exp, scale=-0.5, r=[t_rs], w=[t_rs])
                b = (h * NQ + j) % 2
                kb.stt('dve', ob[b][:, :], o[:, :], sw, rs[:, :], ALU.mult, ALU.mult, r=[t_o, t_rs, t_par], w=[t_ob[b]])
                kb.dma('sp', scr['obT'][h * 128:(h + 1) * 128, j * 512:(j + 1) * 512], ob[b][:, :], r=[t_ob[b]])
            st['dq_tail'].append(tail)
    for f in st['dq_evac'] + st['dq_tail']:
        f()


def host_consts():
    c = np.zeros((128, 1152), np.float32)
    i = np.arange(128)
    c[:, 0:128] = np.eye(128)
    c[:, 128:256] = (i[:, None] <= i[None, :])
    c[:, 256:384] = 1.0
    c[:, 384:512] = (i[:, None] > i[None, :])
    blk = (i[:, None] // 64) == (i[None, :] // 64)
    c[:, 512:640] = (i[:, None] <= i[None, :]) & blk
    c[:, 640:768] = (i[:, None] > i[None, :]) & blk
    c[:, 768:896] = (i[:, None] < i[None, :]) & blk
    c[:, 896:1024] = blk
    c[:, 1024:1152] = -30000.0 * (i[:, None] > i[None, :])
    return c


def setup_consts(kb, A, P, cd):
    cst = A.alloc([128, 1152], F32)
    kb.dma('sp', cst[:, :], cd[:, :], w=[P.t_const])
    P.cst = cst
    P.ident = A.alloc([128, 128], BF16)
    P.tri_bf = A.alloc([128, 128], BF16)
    P.ones_bf = A.alloc([128, 128], BF16)
    kb.copy('dve', P.ident[:, :], cst[:, 0:128], r=[P.t_const], pw=[P.t_const])
    kb.copy('dve', P.tri_bf[:, :], cst[:, 128:256], r=[P.t_const], pw=[P.t_const])
    kb.copy('dve', P.ones_bf[:, :], cst[:, 256:384], r=[P.t_const], pw=[P.t_const])
    P.negtri_bf = A.alloc([128, 128], BF16)
    kb.copy('dve', P.negtri_bf[:, :], cst[:, 1024:1152], r=[P.t_const], pw=[P.t_const])
    P.ident_f32 = cst[:, 0:128]
    P.tri_f32 = cst[:, 128:256]
    P.ones_f32 = cst[:, 256:384]
    P.lstrict_f32 = cst[:, 384:512]
    P.u2_f32 = cst[:, 512:640]
    P.l2_f32 = cst[:, 640:768]
    P.su2_f32 = cst[:, 768:896]
    P.blk_f32 = cst[:, 896:1024]


def prep_da_params(kb, A, pks, t_pk, par, t_par, lambda_init):
    tmp = A.alloc([128, 64], F32)
    s12 = A.alloc([128, 2], F32)
    t_tmp = Tok()
    for i in range(2):
        kb.tt('dve', tmp[:, :], pks[:, i * 128:i * 128 + 64], pks[:, i * 128 + 64:i * 128 + 128], ALU.mult, r=[t_pk], w=[t_tmp])
        kb.s.op('dve', lambda e, i=i: e.reduce_sum(s12[:, i:i + 1], tmp[:, :], axis=AX.X), r=[t_tmp], pw=[t_par])
    kb.act(s12[:, :], s12[:, :], AF.Exp, r=[t_par], w=[t_par])
    kb.tt('dve', par[:, 0:1], s12[:, 1:2], s12[:, 0:1], ALU.subtract, r=[t_par], pw=[t_par])
    kb.ts('dve', par[:, 0:1], par[:, 0:1], -float(lambda_init), None, ALU.add, r=[t_par], pw=[t_par])
    kb.ts('dve', par[:, 1:2], pks[:, 256:257], float(1.0 - lambda_init), None, ALU.mult, r=[t_pk, t_par], pw=[t_par])


def conv_silu_chunk(kb, xin, t_xin, acc, t_acc, src_rows, L, K, wcols, bcol, t_par, out_ap, t_out, out_w=True):
    pad = K - 1
    kb.dma('sp', xin[:, pad:pad + L], src_rows, pw=[t_xin])
    if bcol is not None:
        kb.ts('dve', acc[:, 0:L], xin[:, pad:pad + L], wcols[K - 1], bcol, ALU.mult, ALU.add, r=[t_xin, t_par], w=[t_acc])
    else:
        kb.ts('dve', acc[:, 0:L], xin[:, pad:pad + L], wcols[K - 1], None, ALU.mult, r=[t_xin, t_par], w=[t_acc])
    for j in range(K - 1):
        kb.stt('dve', acc[:, 0:L], xin[:, j:j + L], wcols[j], acc[:, 0:L], ALU.mult, ALU.add, r=[t_xin, t_par, t_acc], w=[t_acc])
    if out_w:
        kb.act(out_ap, acc[:, 0:L], AF.Silu, r=[t_acc], w=[t_out])
    else:
        kb.act(out_ap, acc[:, 0:L], AF.Silu, r=[t_acc], pw=[t_out])


def softplus_tm(kb, A, out, in_, bias_bc, shape, t_in, t_out, t_par):
    xb = A.alloc(shape, F32)
    ab = A.alloc(shape, F32)
    t_x = Tok()
    t_a = Tok()
    sl = tuple([slice(None)] * len(shape))
    kb.tt('dve', xb[sl], in_, bias_bc, ALU.add, r=[t_in, t_par], w=[t_x])
    kb.ts('dve', ab[sl], xb[sl], -1.0, None, ALU.mult, r=[t_x], w=[t_a])
    kb.tt('dve', ab[sl], ab[sl], xb[sl], ALU.max, r=[t_x, t_a], w=[t_a])
    kb.act(ab[sl], ab[sl], AF.Exp, scale=-1.0, r=[t_a], w=[t_a])
    kb.act(ab[sl], ab[sl], AF.Ln, bias=1.0, r=[t_a], w=[t_a])
    kb.ts('dve', xb[sl], xb[sl], 0.0, None, ALU.max, r=[t_x], w=[t_x])
    kb.tt('dve', out, xb[sl], ab[sl], ALU.add, r=[t_x, t_a], w=[t_out])


def phase_ssd(kb, A, P, L, scr, pks, t_pk):
    T = L // 128
    x_tm = A.alloc([128, T, 1024], BF16)
    t_xtm = Tok('x_tm')
    BT = A.alloc([128, 2, L], BF16)
    CT = A.alloc([128, 2, L], BF16)
    t_BC = Tok('BCT')
    B_tm = A.alloc([128, T, 256], BF16)
    t_Btm = Tok('B_tm')
    xin = [A.alloc([128, 3 + L], F32) for _ in range(2)]
    t_xin = [Tok(), Tok()]
    acc = A.alloc([128, L], F32)
    t_acc = Tok()
    xs = [A.alloc([128, L], BF16) for _ in range(2)]
    t_xs = [Tok(), Tok()]
    for b in range(2):
        kb.memset('dve', xin[b][:, 0:3], 0.0, w=[t_xin[b]])
    acc2 = [acc, A.alloc([128, L], F32)]
    t_acc2 = [t_acc, Tok()]

    def partA(cc):
        b = cc % 2
        wcols = [pks[:, cc * 4 + j:cc * 4 + j + 1] for j in range(4)]
        bcol = pks[:, 48 + cc:49 + cc]
        src = scr['xbcT'][cc * 128:(cc + 1) * 128, :]
        if cc < 8:
            conv_silu_chunk(kb, xin[b], t_xin[b], acc2[b], t_acc2[b], src, L, 4, wcols, bcol, t_pk, xs[b][:, :], t_xs[b])
        elif cc < 10:
            conv_silu_chunk(kb, xin[b], t_xin[b], acc2[b], t_acc2[b], src, L, 4, wcols, bcol, t_pk, BT[:, cc - 8, :], t_BC, out_w=False)
        else:
            conv_silu_chunk(kb, xin[b], t_xin[b], acc2[b], t_acc2[b], src, L, 4, wcols, bcol, t_pk, CT[:, cc - 10, :], t_BC, out_w=False)

    def partB(cc):
        b = cc % 2
        if cc < 8:
            for t0 in range(0, T, 8):
                ps, tps = P.next_bank()
                psb = ps[:, :].bitcast(BF16)
                nt = min(8, T - t0)
                for i in range(nt):
                    kb.tr(psb[:, i * 128:(i + 1) * 128], xs[b][:, (t0 + i) * 128:(t0 + i + 1) * 128], P.ident[:, :],
                          r=[t_xs[b], P.t_const], pw=[tps])
                kb.copy('act' if (t0 // 8) % 2 else 'dve', x_tm[:, t0:t0 + nt, cc * 128:(cc + 1) * 128],
                        psb[:, 0:nt * 128].rearrange("p (a b) -> p a b", a=nt), r=[tps], pw=[t_xtm])
        elif cc < 10:
            g = cc - 8
            for t0 in range(0, T, 8):
                ps, tps = P.next_bank()
                psb = ps[:, :].bitcast(BF16)
                nt = min(8, T - t0)
                for i in range(nt):
                    kb.tr(psb[:, i * 128:(i + 1) * 128], BT[:, g, (t0 + i) * 128:(t0 + i + 1) * 128], P.ident[:, :],
                          r=[t_BC, P.t_const], pw=[tps])
                kb.copy('dve', B_tm[:, t0:t0 + nt, g * 128:(g + 1) * 128],
                        psb[:, 0:nt * 128].rearrange("p (a b) -> p a b", a=nt), r=[tps], pw=[t_Btm])

    partA(0)
    for cc in range(12):
        if cc + 1 < 12:
            partA(cc + 1)
        partB(cc)
    dtr = A.alloc([128, T, 16], F32)
    t_dtr = Tok()
    kb.dma('sp', dtr[:, :, :], scr['sdt'].rearrange("(t p) h -> p t h", p=128), w=[t_dtr])
    dt = A.alloc([128, T, 16], F32)
    t_dt = Tok()
    softplus_tm(kb, A, dt[:, :, :], dtr[:, :, :], pks[:, 64:80].unsqueeze(1).to_broadcast([128, T, 16]), [128, T, 16], t_dtr, t_dt, t_pk)
    aneg = A.alloc([128, 16], F32)
    t_an = Tok()
    kb.act(aneg[:, :], pks[:, 80:96], AF.Exp, r=[t_pk], w=[t_an])
    kb.ts('dve', aneg[:, :], aneg[:, :], -1.0, None, ALU.mult, r=[t_an], w=[t_an])
    a_all = A.alloc([128, T, 16], F32)
    t_a = Tok()
    kb.tt('dve', a_all[:, :, :], dt[:, :, :], aneg[:, :].unsqueeze(1).to_broadcast([128, T, 16]), ALU.mult, r=[t_dt, t_an], w=[t_a])
    S = A.alloc([128, 1024], F32)
    Sbf = A.alloc([128, 1024], BF16)
    t_S = Tok('S')
    t_Sbf = Tok('Sbf')
    kb.memset('dve', S[:, :], 0.0, w=[t_S])
    kb.memset('dve', Sbf[:, :], 0.0, w=[t_Sbf])
    pre = A.alloc([128, 48], F32)
    t_pre = Tok()
    E3 = A.alloc([128, 48], F32)
    t_E3 = Tok()
    rhsA = A.alloc([128, 16, 128], F32)
    t_rhsA = Tok()
    ET = A.alloc([128, 16, 128], F32)
    t_ET = Tok()
    Gm = A.alloc([128, 2, 128], F32)
    t_Gm = Tok()
    M = A.alloc([128, 16, 128], BF16)
    t_M = Tok()
    xdt = A.alloc([128, 16, 64], BF16)
    t_xdt = Tok()
    xdtd = A.alloc([128, 16, 64], BF16)
    t_xdtd = Tok()
    t1 = A.alloc([128, 1024], F32)
    t_t1 = Tok()
    y = A.alloc([128, 1024], F32)
    t_y = Tok()
    zt = A.alloc([128, 1024], F32)
    t_zt = Tok()
    junk = A.alloc([128, 512], BF16)
    t_junk = Tok()
    ssq = A.alloc([128, 2], F32)
    t_ssq = Tok()
    yn = A.alloc([128, 1024], BF16)
    t_yn = Tok()
    oT = A.alloc([128, 8, 128], BF16)
    t_oT = Tok()
    Drep = pks[:, 96:1120]
    bk = P.banks
    E3s = [E3, A.alloc([128, 48], F32)]
    t_E3s = [t_E3, Tok()]
    xdtds = [xdtd, A.alloc([128, 16, 64], BF16)]
    t_xdtds = [t_xdtd, Tok()]
    yds = [A.alloc([128, 1024], F32) for _ in range(2)]
    t_yds = [Tok(), Tok()]

    def gen_pre(c):
        p = c % 2
        E3_, tE3_ = E3s[p], t_E3s[p]
        a_c = a_all[:, c, :]
        tok = slice(c * 128, (c + 1) * 128)
        b0, tb0 = bk[0]
        kb.mm(b0[:, 0:16], P.tri_f32, a_c, r=[t_a, P.t_const], w=[tb0])
        kb.mm(b0[:, 16:32], P.ones_f32, a_c, r=[t_a, P.t_const], pw=[tb0])
        for g in range(2):
            kb.mm(b0[:, 128 + g * 128:256 + g * 128], BT[:, g, tok], CT[:, g, tok], r=[t_BC], pw=[tb0])
        kb.copy('dve', pre[:, 0:16], b0[:, 0:16], r=[tb0], w=[t_pre])
        kb.copy('dve', pre[:, 32:48], b0[:, 16:32], r=[tb0], pw=[t_pre])
        kb.tt('dve', pre[:, 16:32], pre[:, 32:48], pre[:, 0:16], ALU.subtract, r=[t_pre], pw=[t_pre])
        kb.act(E3_[:, :], pre[:, :], AF.Exp, r=[t_pre], w=[tE3_])
        kb.tt('dve', Gm[:, :, :], b0[:, 128:384].rearrange("p (g l) -> p g l", g=2),
              P.tri_f32.unsqueeze(1).to_broadcast([128, 2, 128]), ALU.mult, r=[tb0, P.t_const], w=[t_Gm])
        yield
        kb.tt('dve', rhsA[:, :, :], P.tri_f32.unsqueeze(1).to_broadcast([128, 16, 128]),
              a_c.unsqueeze(2).to_broadcast([128, 16, 128]), ALU.mult, r=[t_a, P.t_const], w=[t_rhsA])
        kb.tt('pool', xdt[:, :, :], x_tm[:, c, :].rearrange("p (h d) -> p h d", h=16),
              dt[:, c, :].unsqueeze(2).to_broadcast([128, 16, 64]), ALU.mult, r=[t_xtm, t_dt], w=[t_xdt])
        kb.tt('pool', xdtds[p][:, :, :], xdt[:, :, :], E3_[:, 16:32].unsqueeze(2).to_broadcast([128, 16, 64]), ALU.mult,
              r=[t_xdt, tE3_], w=[t_xdtds[p]])
        for hb in range(4):
            bs, tbs = bk[1 + hb % 2]
            kb.mm(bs[:, :], P.lstrict_f32, rhsA[:, hb * 4:(hb + 1) * 4, :].rearrange("p a b -> p (a b)"),
                  r=[t_rhsA, P.t_const], w=[tbs])
            kb.act(ET[:, hb * 4:(hb + 1) * 4, :].rearrange("p a b -> p (a b)"), bs[:, :], AF.Exp, r=[tbs],
                   **({'w': [t_ET]} if hb == 0 else {'pw': [t_ET]}))
            if hb % 2 == 1:
                yield
        for g in range(2):
            kb.tt('dve', M[:, g * 8:(g + 1) * 8, :], ET[:, g * 8:(g + 1) * 8, :],
                  Gm[:, g:g + 1, :].to_broadcast([128, 8, 128]), ALU.mult, r=[t_ET, t_Gm],
                  **({'w': [t_M]} if g == 0 else {'pw': [t_M]}))
        yield
        for h in range(16):
            by, tby = bk[3 + h // 8]
            kb.mm(by[:, (h % 8) * 64:(h % 8 + 1) * 64], M[:, h, :], xdt[:, h, :], r=[t_M, t_xdt],
                  **({'w': [tby]} if h % 8 == 0 else {'pw': [tby]}))
        for g in range(2):
            by, tby = bk[3 + g]
            kb.copy('act', yds[p][:, g * 512:(g + 1) * 512], by[:, :], r=[tby],
                    **({'w': [t_yds[p]]} if g == 0 else {'pw': [t_yds[p]]}))
        yield

    def gen_rec(c):
        p = c % 2
        E3_, tE3_ = E3s[p], t_E3s[p]
        tok = slice(c * 128, (c + 1) * 128)
        for g in range(2):
            bo, tbo = bk[5 + g]
            kb.mm(bo[:, :], CT[:, g, tok], Sbf[:, g * 512:(g + 1) * 512], r=[t_BC, t_Sbf], w=[tbo])
        kb.dma('sp', zt[:, :], scr['sz'][c * 128:(c + 1) * 128, :], w=[t_zt])
        kb.act(zt[:, :], zt[:, :], AF.Silu, r=[t_zt], w=[t_zt])
        for g in range(2):
            hs = slice(g * 512, (g + 1) * 512)
            bo, tbo = bk[5 + g]
            kb.tt('dve', t1[:, hs].rearrange("p (h d) -> p h d", h=8), bo[:, :].rearrange("p (h d) -> p h d", h=8),
                  E3_[:, g * 8:(g + 1) * 8].unsqueeze(2).to_broadcast([128, 8, 64]), ALU.mult, r=[tbo, tE3_],
                  **({'w': [t_t1]} if g == 0 else {'pw': [t_t1]}))
        yield
        if c + 1 < T:
            for g in range(2):
                hs = slice(g * 512, (g + 1) * 512)
                bo, tbo = bk[5 + g]
                kb.mm(bo[:, :], B_tm[:, c, g * 128:(g + 1) * 128], xdtds[p][:, g * 8:(g + 1) * 8, :].rearrange("p a b -> p (a b)"),
                      r=[t_Btm, t_xdtds[p]], w=[tbo])
                kb.tt('dve', S[:, hs].rearrange("p (h d) -> p h d", h=8), S[:, hs].rearrange("p (h d) -> p h d", h=8),
                      E3_[:, 32 + g * 8:32 + (g + 1) * 8].unsqueeze(2).to_broadcast([128, 8, 64]), ALU.mult,
                      r=[tE3_, t_S], pw=[t_S])
                kb.tt('dve', S[:, hs], S[:, hs], bo[:, :], ALU.add, r=[tbo, t_S], pw=[t_S])
                kb.copy('act', Sbf[:, hs], S[:, hs], r=[t_S], **({'w': [t_Sbf]} if g == 0 else {'pw': [t_Sbf]}))
            yield
        kb.tt('dve', t1[:, :], t1[:, :], yds[p][:, :], ALU.add, r=[t_yds[p], t_t1], w=[t_t1])
        for g in range(2):
            hs = slice(g * 512, (g + 1) * 512)
            kb.tt('dve', y[:, hs], x_tm[:, c, hs], Drep[:, hs], ALU.mult, r=[t_xtm, t_pk],
                  **({'w': [t_y]} if g == 0 else {'pw': [t_y]}))
        kb.tt('dve', y[:, :], y[:, :], t1[:, :], ALU.add, r=[t_y, t_t1], w=[t_y])
        kb.tt('dve', y[:, :], y[:, :], zt[:, :], ALU.mult, r=[t_y, t_zt], w=[t_y])
        yield
        for g in range(2):
            hs = slice(g * 512, (g + 1) * 512)
            kb.act(junk[:, :], y[:, hs], AF.Square, scale=float(512 ** -0.5), accum_out=ssq[:, g:g + 1], r=[t_y],
                   **({'w': [t_junk, t_ssq]} if g == 0 else {'w': [t_junk], 'pw': [t_ssq]}))
        kb.rsqrt_eps(ssq[:, :], ssq[:, :], t_ssq, t_ssq)
        for g in range(2):
            hs = slice(g * 512, (g + 1) * 512)
            kb.act(yn[:, hs], y[:, hs], AF.Copy, scale=ssq[:, g:g + 1], r=[t_y, t_ssq],
                   **({'w': [t_yn]} if g == 0 else {'pw': [t_yn]}))
        yield
        bt, tbt = bk[7]
        btb = bt[:, :].bitcast(BF16)
        for j in range(8):
            kb.tr(btb[:, j * 128:(j + 1) * 128], yn[:, j * 128:(j + 1) * 128], P.ident[:, :], r=[t_yn, P.t_const],
                  **({'w': [tbt]} if j == 0 else {'pw': [tbt]}))
        kb.copy('act', oT[:, :, :], btb[:, 0:1024].rearrange("p (a b) -> p a b", a=8), r=[tbt], w=[t_oT])
        kb.dma('sp', scr['obT'][1024:2048, c * 128:(c + 1) * 128].rearrange("(j p) t -> p j t", p=128), oT[:, :, :], r=[t_oT])
        yield

    for _ in gen_pre(0):
        pass
    for c in range(T):
        gr = gen_rec(c)
        gp = gen_pre(c + 1) if c + 1 < T else iter(())
        alive = [True, True]
        while alive[0] or alive[1]:
            if alive[1]:
                try:
                    next(gp)
                except StopIteration:
                    alive[1] = False
            if alive[0]:
                try:
                    next(gr)
                except StopIteration:
                    alive[0] = False


def phase_gdn(kb, A, P, L, scr, pks, t_pk):
    T = L // 128
    NB = L // 512
    bk = P.banks
    base0 = A.off
    bar = A.alloc([128, T, 16], F32)
    t_bar = Tok()
    kb.dma('sp', bar[:, :, :], scr['gba'].rearrange("(t p) c -> p t c", p=128), w=[t_bar])
    beta = A.alloc([128, T, 8], F32)
    negb = A.alloc([128, T, 8], F32)
    t_beta = Tok()
    kb.act(beta[:, :, :], bar[:, :, 0:8], AF.Sigmoid, r=[t_bar], w=[t_beta])
    kb.ts('dve', negb[:, :, :], beta[:, :, :], -1.0, None, ALU.mult, r=[t_beta], pw=[t_beta])
    sp = A.alloc([128, T, 8], F32)
    t_sp = Tok()
    softplus_tm(kb, A, sp[:, :, :], bar[:, :, 8:16], pks[:, 96:104].unsqueeze(1).to_broadcast([128, T, 8]), [128, T, 8], t_bar, t_sp, t_pk)
    aneg = A.alloc([128, 8], F32)
    t_an = Tok()
    kb.act(aneg[:, :], pks[:, 104:112], AF.Exp, r=[t_pk], w=[t_an])
    kb.ts('dve', aneg[:, :], aneg[:, :], -1.0, None, ALU.mult, r=[t_an], w=[t_an])
    g_all = A.alloc([128, T, 8], F32)
    t_g = Tok()
    kb.tt('dve', g_all[:, :, :], sp[:, :, :], aneg[:, :].unsqueeze(1).to_broadcast([128, T, 8]), ALU.mult, r=[t_sp, t_an], w=[t_g])
    base1 = A.off
    for hb in range(2):
        kb.s.barrier()
        A.off = base1
        qT = A.alloc([128, 4, L], BF16)
        kT = A.alloc([128, 4, L], BF16)
        t_qk = Tok('qkT')
        k_tm = A.alloc([128, T, 4, 128], BF16)
        v_tm = A.alloc([128, T, 4, 128], BF16)
        t_kv = Tok('kv_tm')
        xin = [A.alloc([128, 3 + L], F32) for _ in range(2)]
        t_xin = [Tok(), Tok()]
        acc2 = [A.alloc([128, L], F32) for _ in range(2)]
        t_acc2 = [Tok(), Tok()]
        ks2 = [A.alloc([128, L], F32) for _ in range(2)]
        t_ks2 = [Tok(), Tok()]
        sq2 = [A.alloc([128, L], F32) for _ in range(2)]
        t_sq2 = [Tok(), Tok()]
        rn2 = [A.alloc([128, 512], F32) for _ in range(2)]
        t_rn2 = [Tok(), Tok()]
        vs2 = [A.alloc([128, L], BF16) for _ in range(2)]
        t_vs2 = [Tok(), Tok()]
        for b in range(2):
            kb.memset('dve', xin[b][:, 0:3], 0.0, w=[t_xin[b]])
        ci = 0
        def partA(kind, hh, b):
            h = hb * 4 + hh
            cc = kind * 8 + h
            wcols = [pks[:, cc * 4 + j:cc * 4 + j + 1] for j in range(4)]
            src = scr['gqkvT'][cc * 128:(cc + 1) * 128, :]
            if kind < 2:
                conv_silu_chunk(kb, xin[b], t_xin[b], acc2[b], t_acc2[b], src, L, 4, wcols, None, t_pk, ks2[b][:, :], t_ks2[b])
                kb.act(sq2[b][:, :], ks2[b][:, :], AF.Square, r=[t_ks2[b]], w=[t_sq2[b]])
            else:
                conv_silu_chunk(kb, xin[b], t_xin[b], acc2[b], t_acc2[b], src, L, 4, wcols, None, t_pk, vs2[b][:, :], t_vs2[b])

        def partB(kind, hh, b):
            ks, t_ks, sq, t_sq, vs, t_vs = ks2[b], t_ks2[b], sq2[b], t_sq2[b], vs2[b], t_vs2[b]
            if kind < 2:
                dstT = qT if kind == 0 else kT
                scale = float(128 ** -0.5) if kind == 0 else 1.0
                for tb in range(NB):
                    ps, tps = P.next_bank()
                    cs_ = slice(tb * 512, (tb + 1) * 512)
                    rn, t_rn = rn2[tb % 2], t_rn2[tb % 2]
                    kb.mm(ps[:, :], P.ones_f32, sq[:, cs_], r=[t_sq, P.t_const], w=[tps])
                    kb.act(rn[:, :], ps[:, :], AF.Ln, bias=1e-6, r=[tps], w=[t_rn])
                    kb.act(rn[:, :], rn[:, :], AF.Exp, scale=-0.5, r=[t_rn], w=[t_rn])
                    kb.stt('dve', dstT[:, hh, cs_], ks[:, cs_], scale, rn[:, :], ALU.mult, ALU.mult, r=[t_ks, t_rn], pw=[t_qk])
                if kind == 1:
                    for t0 in range(0, T, 8):
                        ps, tps = P.next_bank()
                        psb = ps[:, :].bitcast(BF16)
                        nt = min(8, T - t0)
                        for i in range(nt):
                            kb.tr(psb[:, i * 128:(i + 1) * 128], kT[:, hh, (t0 + i) * 128:(t0 + i + 1) * 128], P.ident[:, :],
                                  r=[t_qk, P.t_const], pw=[tps])
                        kb.copy('act', k_tm[:, t0:t0 + nt, hh, :], psb[:, 0:nt * 128].rearrange("p (a b) -> p a b", a=nt),
                                r=[tps], pw=[t_kv])
            else:
                for t0 in range(0, T, 8):
                    ps, tps = P.next_bank()
                    psb = ps[:, :].bitcast(BF16)
                    nt = min(8, T - t0)
                    for i in range(nt):
                        kb.tr(psb[:, i * 128:(i + 1) * 128], vs[:, (t0 + i) * 128:(t0 + i + 1) * 128], P.ident[:, :],
                              r=[t_vs, P.t_const], pw=[tps])
                    kb.copy('act', v_tm[:, t0:t0 + nt, hh, :], psb[:, 0:nt * 128].rearrange("p (a b) -> p a b", a=nt),
                            r=[tps], pw=[t_kv])

        chunks = [(kind, hh, i % 2) for i, (kind, hh) in enumerate([(k_, h_) for k_ in range(3) for h_ in range(4)])]
        partA(*chunks[0])
        for i in range(len(chunks)):
            if i + 1 < len(chunks):
                partA(*chunks[i + 1])
            partB(*chunks[i])
        S = A.alloc([128, 4, 128], F32)
        Sbf = A.alloc([128, 4, 128], BF16)
        t_S = Tok('S')
        t_Sbf = Tok('Sbf')
        kb.memset('dve', S[:, :, :], 0.0, w=[t_S])
        kb.memset('dve', Sbf[:, :, :], 0.0, w=[t_Sbf])
        gm = A.alloc([128, 2, 4], F32)
        t_gm = Tok()
        pre = A.alloc([128, 16], F32)
        t_pre = Tok()
        E = A.alloc([128, 16], F32)
        t_E = Tok()
        rhsG = A.alloc([128, 4, 128], F32)
        t_rhsG = Tok()
        DT = A.alloc([128, 4, 128], F32)
        t_DT = Tok()
        tmp = A.alloc([128, 4, 128], F32)
        t_tmp = Tok()
        tmp2 = A.alloc([128, 4, 128], F32)
        t_tmp2 = Tok()
        attnT = A.alloc([128, 4, 128], BF16)
        t_attn = Tok()
        Xs = [A.alloc([128, 4, 128], BF16) for _ in range(2)]
        Ys = [A.alloc([128, 4, 128], BF16) for _ in range(2)]
        t_X = [Tok(), Tok()]
        t_Y = [Tok(), Tok()]
        Rs = [A.alloc([128, 4, 128], BF16) for _ in range(2)]
        t_R = [Tok(), Tok()]
        u0b = A.alloc([128, 4, 128], F32)
        t_u0b = Tok()
        w0T = A.alloc([128, 4, 128], BF16)
        t_w0T = Tok()
        keg = A.alloc([128, 4, 128], BF16)
        t_keg = Tok()
        kd = A.alloc([128, 4, 128], BF16)
        t_kd = Tok()
        vn = A.alloc([128, 4, 128], BF16)
        t_vn = Tok()
        kb.memset('dve', vn[:, :, :], 0.0, w=[t_vn])
        tq = A.alloc([128, 4, 128], F32)
        t_tq = Tok()
        o = A.alloc([128, 4, 128], F32)
        t_o = Tok()
        zt = A.alloc([128, 512], F32)
        t_zt = Tok()
        junk = A.alloc([128, 128], BF16)
        t_junk = Tok()
        ssq = A.alloc([128, 4], F32)
        t_ssq = Tok()
        onb = A.alloc([128, 512], BF16)
        t_onb = Tok()
        oT = A.alloc([128, 4, 128], BF16)
        t_oT = Tok()
        hs = slice(hb * 4, hb * 4 + 4)

        def bc3(ap2, n=128):
            return ap2.unsqueeze(2).to_broadcast([ap2.shape[0], 4, n])

        def m4(ap2):
            return ap2.unsqueeze(1).to_broadcast([128, 4, 128])

        def f2(ap3):
            return ap3.rearrange("p a b -> p (a b)")

        E2 = [E, A.alloc([128, 16], F32)]
        t_E2 = [t_E, Tok()]
        attn2 = [attnT, A.alloc([128, 4, 128], BF16)]
        t_attn2 = [t_attn, Tok()]
        u0b2 = [u0b, A.alloc([128, 4, 128], F32)]
        t_u0b2 = [t_u0b, Tok()]
        w0T2 = [w0T, A.alloc([128, 4, 128], BF16)]
        t_w0T2 = [t_w0T, Tok()]
        kd2 = [kd, A.alloc([128, 4, 128], BF16)]
        t_kd2 = [t_kd, Tok()]

        def gen_pre(t):
            p = t % 2
            E_, tE_ = E2[p], t_E2[p]
            tok = slice(t * 128, (t + 1) * 128)
            g_t = g_all[:, t, hs]
            for j in range(2):
                kb.ts('dve', gm[:, j, :], g_t, P.blk_f32[:, 64 * j:64 * j + 1], None, ALU.mult, r=[t_g, P.t_const],
                      **({'w': [t_gm]} if j == 0 else {'pw': [t_gm]}))
            b0, tb0 = bk[0]
            kb.mm(b0[:, 0:4], P.u2_f32, g_t, r=[t_g, P.t_const], w=[tb0])
            kb.mm(b0[:, 4:8], P.blk_f32, g_t, r=[t_g, P.t_const], pw=[tb0])
            kb.mm(b0[:, 8:16], P.ones_f32, gm[:, :, :].rearrange("p a b -> p (a b)"), r=[t_gm, P.t_const], pw=[tb0])
            kb.copy('dve', pre[:, :], b0[:, 0:16], r=[tb0], w=[t_pre])
            kb.tt('dve', pre[:, 4:8], pre[:, 4:8], pre[:, 0:4], ALU.subtract, r=[t_pre], w=[t_pre])
            kb.act(E_[:, :], pre[:, :], AF.Exp, r=[t_pre], w=[tE_])
            yield
            kb.tt('dve', rhsG[:, :, :], m4(P.u2_f32), bc3(g_t), ALU.mult, r=[t_g, P.t_const], w=[t_rhsG])
            b1, tb1 = bk[1]
            kb.mm(b1[:, :], P.l2_f32, f2(rhsG[:, :, :]), r=[t_rhsG, P.t_const], w=[tb1])
            kb.act(f2(DT[:, :, :]), b1[:, :], AF.Exp, r=[tb1], w=[t_DT])
            yield
            b2, tb2 = bk[2]
            b3, tb3 = bk[3]
            for hh in range(4):
                kb.mm(b2[:, hh * 128:(hh + 1) * 128], kT[:, hh, tok], kT[:, hh, tok], r=[t_qk],
                      **({'w': [tb2]} if hh == 0 else {'pw': [tb2]}))
            for hh in range(4):
                kb.mm(b3[:, hh * 128:(hh + 1) * 128], kT[:, hh, tok], qT[:, hh, tok], r=[t_qk],
                      **({'w': [tb3]} if hh == 0 else {'pw': [tb3]}))
            kb.tt('dve', f2(tmp[:, :, :]), b2[:, :], f2(DT[:, :, :]), ALU.mult, r=[tb2, t_DT], w=[t_tmp])
            kb.tt('dve', tmp[:, :, :], tmp[:, :, :], m4(P.su2_f32), ALU.mult, r=[t_tmp, P.t_const], w=[t_tmp])
            kb.tt('dve', Ys[0][:, :, :], tmp[:, :, :], bc3(negb[:, t, hs]), ALU.mult, r=[t_tmp, t_beta], w=[t_Y[0]])
            yield
            kb.tt('dve', f2(tmp2[:, :, :]), b3[:, :], f2(DT[:, :, :]), ALU.mult, r=[tb3, t_DT], w=[t_tmp2])
            kb.tt('dve', attn2[p][:, :, :], tmp2[:, :, :], m4(P.u2_f32), ALU.mult, r=[t_tmp2, P.t_const], w=[t_attn2[p]])
            b4, tb4 = bk[0]
            b4b = b4[:, :].bitcast(BF16)
            for hh in range(4):
                kb.tr(b4b[:, hh * 128:(hh + 1) * 128], Ys[0][:, hh, :], P.ident[:, :], r=[t_Y[0], P.t_const],
                      **({'w': [tb4]} if hh == 0 else {'pw': [tb4]}))
            kb.copy('act', f2(Xs[0][:, :, :]), b4b[:, 0:512], r=[tb4], w=[t_X[0]])
            kb.tt('dve', Rs[0][:, :, :], Ys[0][:, :, :], m4(P.ident_f32), ALU.add, r=[t_Y[0], P.t_const], w=[t_R[0]])
            yield
            for lv in range(5):
                a, n_ = lv % 2, (lv + 1) % 2
                bx, tbx = bk[1]
                by, tby = bk[2]
                br, tbr = bk[3]
                for hh in range(4):
                    kb.mm(bx[:, hh * 128:(hh + 1) * 128], Ys[a][:, hh, :], Xs[a][:, hh, :], r=[t_X[a], t_Y[a]],
                          **({'w': [tbx]} if hh == 0 else {'pw': [tbx]}))
                kb.copy('act', f2(Xs[n_][:, :, :]), bx[:, :], r=[tbx], w=[t_X[n_]])
                if lv < 4:
                    for hh in range(4):
                        kb.mm(by[:, hh * 128:(hh + 1) * 128], Xs[a][:, hh, :], Ys[a][:, hh, :], r=[t_X[a], t_Y[a]],
                              **({'w': [tby]} if hh == 0 else {'pw': [tby]}))
                    kb.copy('dve', f2(Ys[n_][:, :, :]), by[:, :], r=[tby], w=[t_Y[n_]])
                yield
                for hh in range(4):
                    kb.mm(br[:, hh * 128:(hh + 1) * 128], Xs[n_][:, hh, :], Rs[a][:, hh, :], r=[t_X[n_], t_R[a]],
                          **({'w': [tbr]} if hh == 0 else {'pw': [tbr]}))
                kb.tt('dve', f2(Rs[n_][:, :, :]), br[:, :], f2(Rs[a][:, :, :]), ALU.add, r=[tbr, t_R[a]], w=[t_R[n_]])
                yield
            R, tR = Rs[1], t_R[1]
            kb.tt('dve', keg[:, :, :], k_tm[:, t, :, :], bc3(E_[:, 0:4]), ALU.mult, r=[t_kv, tE_], w=[t_keg])
            kb.tt('dve', kd2[p][:, :, :], k_tm[:, t, :, :], bc3(E_[:, 4:8]), ALU.mult, r=[t_kv, tE_], w=[t_kd2[p]])
            b2, tb2 = bk[0]
            b3, tb3 = bk[1]
            for hh in range(4):
                kb.mm(b2[:, hh * 128:(hh + 1) * 128], R[:, hh, :], v_tm[:, t, hh, :], r=[tR, t_kv],
                      **({'w': [tb2]} if hh == 0 else {'pw': [tb2]}))
            kb.tt('dve', u0b2[p][:, :, :], b2[:, :].rearrange("p (a b) -> p a b", a=4), bc3(beta[:, t, hs]), ALU.mult,
                  r=[tb2, t_beta], w=[t_u0b2[p]])
            yield
            for hh in range(4):
                kb.mm(b3[:, hh * 128:(hh + 1) * 128], keg[:, hh, :], R[:, hh, :], r=[tR, t_keg],
                      **({'w': [tb3]} if hh == 0 else {'pw': [tb3]}))
            kb.copy('act', f2(w0T2[p][:, :, :]), b3[:, :], r=[tb3], w=[t_w0T2[p]])
            yield

        def gen_rec(t):
            p = t % 2
            E_, tE_ = E2[p], t_E2[p]
            tok = slice(t * 128, (t + 1) * 128)
            for j in range(2):
                rows = slice(64 * j, 64 * j + 64)
                ba_, tba = bk[4]
                bq, tbq = bk[5]
                bo, tbo = bk[6]
                bs, tbs = bk[7]
                for hh in range(4):
                    kb.mm(ba_[:, hh * 128:(hh + 1) * 128], w0T2[p][:, hh, :], Sbf[:, hh, :], r=[t_w0T2[p], t_Sbf],
                          **({'w': [tba]} if hh == 0 else {'pw': [tba]}))
                for hh in range(4):
                    kb.mm(bq[:, hh * 128:(hh + 1) * 128], qT[:, hh, tok], Sbf[:, hh, :], r=[t_qk, t_Sbf],
                          **({'w': [tbq]} if hh == 0 else {'pw': [tbq]}))
                kb.tt('dve', tq[rows, :, :], ba_[rows, :].rearrange("p (a b) -> p a b", a=4), bc3(negb[rows, t, hs]), ALU.mult,
                      r=[tba, t_beta], w=[t_tq])
                kb.tt('dve', vn[rows, :, :], tq[rows, :, :], u0b2[p][rows, :, :], ALU.add, r=[t_tq, t_u0b2[p]], w=[t_vn])
                yield
                for hh in range(4):
                    kb.mm(bo[:, hh * 128:(hh + 1) * 128], attn2[p][rows, hh, :], vn[rows, hh, :], r=[t_attn2[p], t_vn],
                          **({'w': [tbo]} if hh == 0 else {'pw': [tbo]}))
                for hh in range(4):
                    kb.mm(bs[:, hh * 128:(hh + 1) * 128], kd2[p][rows, hh, :], vn[rows, hh, :], r=[t_kd2[p], t_vn],
                          **({'w': [tbs]} if hh == 0 else {'pw': [tbs]}))
                kb.tt('dve', S[:, :, :], S[:, :, :], bc3(E_[:, 8 + 4 * j:12 + 4 * j]), ALU.mult, r=[t_S, tE_], w=[t_S])
                kb.tt('dve', f2(S[:, :, :]), f2(S[:, :, :]), bs[:, :], ALU.add, r=[t_S, tbs], w=[t_S])
                kb.copy('act', Sbf[:, :, :], S[:, :, :], r=[t_S], w=[t_Sbf])
                yield
                kb.tt('dve', tq[rows, :, :], bq[rows, :].rearrange("p (a b) -> p a b", a=4), bc3(E_[rows, 0:4]), ALU.mult,
                      r=[tbq, tE_], w=[t_tq])
                kb.tt('dve', o[rows, :, :], tq[rows, :, :], bo[rows, :].rearrange("p (a b) -> p a b", a=4), ALU.add,
                      r=[t_tq, tbo], **({'w': [t_o]} if j == 0 else {'pw': [t_o]}))
                yield
            kb.dma('sp', zt[:, :], scr['gz'][tok, hb * 512:(hb + 1) * 512], w=[t_zt])
            kb.act(zt[:, :], zt[:, :], AF.Silu, r=[t_zt], w=[t_zt])
            for hh in range(4):
                kb.act(junk[:, :], o[:, hh, :], AF.Square, scale=float(128 ** -0.5), accum_out=ssq[:, hh:hh + 1], r=[t_o],
                       **({'w': [t_junk, t_ssq]} if hh == 0 else {'w': [t_junk], 'pw': [t_ssq]}))
            kb.rsqrt_eps(ssq[:, :], ssq[:, :], t_ssq, t_ssq)
            yield
            kb.tt('dve', o[:, :, :], o[:, :, :], bc3(ssq[:, 0:4]), ALU.mult, r=[t_o, t_ssq], w=[t_o])
            kb.tt('dve', onb[:, :], f2(o[:, :, :]), zt[:, :], ALU.mult, r=[t_o, t_zt], w=[t_onb])
            b1, tb1 = bk[4]
            b1b = b1[:, :].bitcast(BF16)
            for hh in range(4):
                kb.tr(b1b[:, hh * 128:(hh + 1) * 128], onb[:, hh * 128:(hh + 1) * 128], P.ident[:, :], r=[t_onb, P.t_const],
                      **({'w': [tb1]} if hh == 0 else {'pw': [tb1]}))
            kb.copy('act', f2(oT[:, :, :]), b1b[:, 0:512], r=[tb1], w=[t_oT])
            r0 = 2048 + hb * 512
            kb.dma('sp', scr['obT'][r0:r0 + 512, tok].rearrange("(j p) t -> p j t", p=128), oT[:, :, :], r=[t_oT])
            yield

        for _ in gen_pre(0):
            pass
        for t in range(T):
            gr = gen_rec(t)
            gp = gen_pre(t + 1) if t + 1 < T else iter(())
            alive = [True, True]
            while alive[0] or alive[1]:
                for k_ in range(2):
                    if alive[1]:
                        try:
                            next(gp)
                        except StopIteration:
                            alive[1] = False
                if alive[0]:
                    try:
                        next(gr)
                    except StopIteration:
                        alive[0] = False
    kb.s.barrier()
    A.off = base0


PK_NMIX, PK_DALAM, PK_NSB, PK_NFFN, PK_SSD, PK_GDN, PK_FFNC, PK_W = 0, 16, 280, 304, 320, 1472, 1600, 2048


def phase_merge(kb, A, P, L, scr, w_branch, nsb, t_pk, mT, t_mT):
    HL = 1024 if L >= 1024 else 512
    ob = A.alloc([128, 24, HL], BF16)
    t_ob = Tok('ob')
    G = GemmRes(kb, A, P.next_bank, elems=8 * 512)
    gt = [A.alloc([128, 4, HL], BF16) for _ in range(2)]
    t_gt = [Tok(), Tok()]
    acc = A.alloc([128, 4, HL], F32)
    t_acc = [Tok() for _ in range(4)]
    tmp = [A.alloc([128, 512], F32)] * 2
    t_tmp = [Tok()] * 2
    st = {'gi': 0, 'ti': 0}

    def cons(ps, tps, tl, sub, tb):
        ft, b, th = tl['ft'], tl['b'], tl['th']
        if sub == 0 and tb == 0:
            st['gi'] += 1
            gi = st['gi'] % 2
            r0 = b * 2048 + ft * 512
            kb.dma('sp', gt[gi][:, :, :], scr['gatesT'][r0:r0 + 512, th * HL:(th + 1) * HL].rearrange("(s p) t -> p s t", p=128),
                   w=[t_gt[gi]])
        gi = st['gi'] % 2
        cs = slice(tb * 512, (tb + 1) * 512)
        if b == 0:
            kb.tt('dve', acc[:, sub, cs], ps[:, :], gt[gi][:, sub, cs], ALU.mult, r=[tps, t_gt[gi]], w=[t_acc[sub]])
        else:
            st['ti'] += 1
            ti = st['ti'] % 2
            kb.tt('dve', tmp[ti][:, :], ps[:, :], gt[gi][:, sub, cs], ALU.mult, r=[tps, t_gt[gi]], w=[t_tmp[ti]])
            if b == 1:
                kb.tt('dve', acc[:, sub, cs], acc[:, sub, cs], tmp[ti][:, :], ALU.add, r=[t_tmp[ti], t_acc[sub]], w=[t_acc[sub]])
            else:
                kb.tt('dve', mT[:, ft * 4 + sub, th * HL + tb * 512:th * HL + (tb + 1) * 512], acc[:, sub, cs], tmp[ti][:, :],
                      ALU.add, r=[t_tmp[ti], t_acc[sub]], pw=[t_mT])

    def load_ob(th):
        def f():
            for b in range(3):
                kb.dma('sp', ob[:, b * 8:(b + 1) * 8, :],
                       scr['obT'][b * 1024:(b + 1) * 1024, th * HL:(th + 1) * HL].rearrange("(kc p) t -> p kc t", p=128),
                       **({'w': [t_ob]} if b == 0 else {'pw': [t_ob]}))
        return f

    tiles = []
    for th in range(L // HL):
        for ft in range(4):
            for b in range(3):
                tiles.append(dict(W=w_branch, k0=b * 1024, KC=8, c0=ft * 512, width=512, mode='fm', nscale=nsb[:, b * 8:(b + 1) * 8],
                                  t_nscale=t_pk, consume=cons, kx0=b * 8, ft=ft, b=b, th=th,
                                  cast=('act' if (ft * 3 + b) % 2 else 'pool'),
                                  pre=(load_ob(th) if (ft == 0 and b == 0) else None)))
    run_gemms(kb, G, ob, t_ob, HL, tiles)


def accum_consumer(kb, A, h_dst, t_h, wd):
    hx = [A.alloc([128, wd], F32) for _ in range(4)]
    t_hx = [Tok() for _ in range(4)]
    st = {'i': 0}

    def cons(ps, tps, tl, sub, t):
        i = st['i'] % 4
        st['i'] += 1
        j = tl['j']
        key = (t, j * wd)
        if key not in t_h:
            t_h[key] = Tok()
        rs, cs = slice(t * 128, (t + 1) * 128), slice(j * wd, (j + 1) * wd)
        kb.copy('dve' if i % 2 == 0 else 'act', hx[i][:, :], ps[:, 0:wd], r=[tps], w=[t_hx[i]])
        kb.s.op('pool', lambda e, i=i, rs=rs, cs=cs: e.dma_start(out=h_dst[rs, cs], in_=hx[i][:, :], accum_op=ALU.add),
                r=[t_hx[i]], w=[t_h[key]], dma=True)
    return cons


def phase_outproj(kb, A, P, L, w_out, mT, t_mT, h_src, h_dst):
    G = GemmRes(kb, A, P.next_bank, elems=16 * 512)
    t_h = {}
    cons = accum_consumer(kb, A, h_dst, t_h, 512)
    tiles = [dict(W=w_out, k0=0, KC=16, c0=j * 512, width=512, mode='tm', nscale=None, consume=cons, j=j, cast='act') for j in range(4)]
    run_gemms(kb, G, mT, t_mT, L, tiles)


def phase_ffn(kb, A, P, L, ffn_up, ffn_down, xT, t_xT, nffn, fc, t_pk, h):
    NCP = 8
    aT = A.alloc([128, NCP, L], BF16)
    t_aT = Tok('aT')
    G = GemmRes(kb, A, P.next_bank, elems=16 * 256)
    ub = [A.alloc([128, 2 + L], F32) for _ in range(2)]
    t_ub = [Tok(), Tok()]
    yv = [A.alloc([128, L], F32) for _ in range(2)]
    t_yv = [Tok(), Tok()]
    t_h = {}
    cons_down = accum_consumer(kb, A, h, t_h, 512)
    for b in range(2):
        kb.memset('dve', ub[b][:, 0:2], 0.0, w=[t_ub[b]])

    def cons_up(ps, tps, tl, sub, tb):
        kb.copy('act', ub[sub][:, 2 + tb * 512:2 + (tb + 1) * 512], ps[:, :], r=[tps], pw=[t_ub[sub]])
        if tb == L // 512 - 1:
            c = tl['chunk'] + (44 if sub == 1 else 0)
            w0, w1, w2, bb = [fc[:, c * 4 + j:c * 4 + j + 1] for j in range(4)]
            kb.ts('dve', yv[sub][:, :], ub[sub][:, 2:2 + L], w2, bb, ALU.mult, ALU.add, r=[t_ub[sub], t_pk], w=[t_yv[sub]])
            kb.stt('dve', yv[sub][:, :], ub[sub][:, 1:1 + L], w1, yv[sub][:, :], ALU.mult, ALU.add, r=[t_ub[sub], t_pk, t_yv[sub]], w=[t_yv[sub]])
            kb.stt('dve', yv[sub][:, :], ub[sub][:, 0:L], w0, yv[sub][:, :], ALU.mult, ALU.add, r=[t_ub[sub], t_pk, t_yv[sub]], w=[t_yv[sub]])
            if sub == 0:
                kb.act(yv[0][:, :], yv[0][:, :], AF.Silu, r=[t_yv[0]], w=[t_yv[0]])
            else:
                kb.tt('dve', aT[:, tl['cl'], :], yv[0][:, :], yv[1][:, :], ALU.mult, r=[t_yv[0], t_yv[1]], pw=[t_aT])

    tiles = []
    for c0 in range(0, 44, NCP):
        ncl = min(NCP, 44 - c0)
        for cl in range(ncl):
            ch = c0 + cl
            tiles.append(dict(W=ffn_up, k0=0, KC=16, c0=0, width=256, segs=[(ch * 128, 128), (DFF + ch * 128, 128)], mode='fm',
                              nscale=nffn, t_nscale=t_pk, consume=cons_up, chunk=ch, cl=cl))
        tiles += [dict(W=ffn_down, k0=c0 * 128, KC=ncl, c0=j * 512, width=512, mode='tm', nscale=None, consume=cons_down, j=j,
                       cast='act', xT=aT, t_xT=t_aT) for j in range(4)]
    run_gemms(kb, G, xT, t_xT, L, tiles)


def phase_final(kb, A, P, L, h, out, wrep_d):
    wrep = A.alloc([128, D], F32)
    t_w = Tok()
    kb.dma('sp', wrep[:, :], wrep_d[:, :], w=[t_w])
    ld = [A.alloc([128, D], F32) for _ in range(2)]
    t_ld = [Tok(), Tok()]
    junk = A.alloc([128, D], BF16)
    t_junk = Tok()
    ssq = [A.alloc([128, 1], F32) for _ in range(2)]
    t_ssq = [Tok(), Tok()]
    for t in range(L // 128):
        b = t % 2
        kb.dma('sp', ld[b][:, :], h[t * 128:(t + 1) * 128, :], w=[t_ld[b]])
        kb.act(junk[:, :], ld[b][:, :], AF.Square, scale=float(D ** -0.5), accum_out=ssq[b][:, :], r=[t_ld[b]], w=[t_junk, t_ssq[b]])
        kb.rsqrt_eps(ssq[b][:, :], ssq[b][:, :], t_ssq[b], t_ssq[b])
        kb.stt('dve', ld[b][:, :], ld[b][:, :], ssq[b][:, :], wrep[:, :], ALU.mult, ALU.mult, r=[t_ld[b], t_ssq[b], t_w], w=[t_ld[b]])
        kb.dma('sp', out[t * 128:(t + 1) * 128, :], ld[b][:, :], r=[t_ld[b]])


ARENA_BASE, ARENA_LIMIT = 16640, 229376


def pack_layer(l, p):
    pk = np.zeros((128, PK_W), np.float32)
    f = lambda a: np.asarray(a, np.float32)
    pk[:, PK_NMIX:PK_NMIX + 16] = f(p['norm_mix'][l]).reshape(16, 128).T
    pk[:, PK_DALAM:PK_DALAM + 256] = f(p['da_lambda'][l]).reshape(1, 256)
    pk[:, PK_DALAM + 256] = f(p['da_subln'][l])
    nsb = np.ones((24, 128), np.float32)
    nsb[8:16] = f(p['ssm_norm'][l]).reshape(8, 128)
    nsb[16:24] = f(p['gdn_norm'][l])[None, :]
    pk[:, PK_NSB:PK_NSB + 24] = nsb.T
    pk[:, PK_NFFN:PK_NFFN + 16] = f(p['norm_ffn'][l]).reshape(16, 128).T
    cw, cb = f(p['ssm_conv_w'][l]), f(p['ssm_conv_b'][l])
    for cc in range(12):
        pk[:, PK_SSD + cc * 4:PK_SSD + cc * 4 + 4] = cw[:, cc * 128:(cc + 1) * 128].T
        pk[:, PK_SSD + 48 + cc] = cb[cc * 128:(cc + 1) * 128]
    pk[:, PK_SSD + 64:PK_SSD + 80] = f(p['ssm_dt_bias'][l])[None, :]
    pk[:, PK_SSD + 80:PK_SSD + 96] = f(p['ssm_a_log'][l])[None, :]
    pk[:, PK_SSD + 96:PK_SSD + 1120] = np.repeat(f(p['ssm_d'][l]), 64)[None, :]
    gw = f(p['gdn_conv_w'][l])
    for cc in range(24):
        pk[:, PK_GDN + cc * 4:PK_GDN + cc * 4 + 4] = gw[:, cc * 128:(cc + 1) * 128].T
    pk[:, PK_GDN + 96:PK_GDN + 104] = f(p['gdn_dt_bias'][l])[None, :]
    pk[:, PK_GDN + 104:PK_GDN + 112] = f(p['gdn_a_log'][l])[None, :]
    fw, fb = f(p['ffn_conv_w'][l]), f(p['ffn_conv_b'][l])
    for c in range(88):
        pk[:, PK_FFNC + c * 4:PK_FFNC + c * 4 + 3] = fw[:, c * 128:(c + 1) * 128].T
        pk[:, PK_FFNC + c * 4 + 3] = fb[c * 128:(c + 1) * 128]
    return pk


def build_program(L, depth=2, scr_kind="Internal"):
    nc = bass.Bass("TRN2", target_bir_lowering=False)
    kb = KB(nc)
    EI = "ExternalInput"
    x = kb.dram('x', [L, D], F32, EI)
    w_in = kb.dram('w_in', [depth, D, 15904], F32, EI)
    w_branch = kb.dram('w_branch', [depth, 3072, D], F32, EI)
    w_out = kb.dram('w_out', [depth, D, D], F32, EI)
    ffn_up = kb.dram('ffn_up', [depth, D, 2 * DFF], F32, EI)
    ffn_down = kb.dram('ffn_down', [depth, DFF, D], F32, EI)
    pkd = kb.dram('pk', [depth, 128, PK_W], F32, EI)
    cd = kb.dram('consts', [128, 1152], F32, EI)
    augd = kb.dram('c_aug', [8, 2, 3, L], BF16, EI)
    wfin = kb.dram('wfin', [128, D], F32, EI)
    out = kb.dram('out', [L, D], F32, "ExternalOutput")
    h = kb.dram('h_scr', [L, D], F32, scr_kind)
    scr = make_scratch(kb, L, scr_kind)
    P = Persist(kb, None)
    A = Arena(kb, ARENA_BASE, ARENA_LIMIT)
    setup_consts(kb, A, P, cd)
    pks = A.alloc([128, PK_W], F32)
    t_pk = Tok('pk')
    par = A.alloc([128, 8], F32)
    t_par = Tok('par')
    A.base = A.off
    S = kb.s
    for a in range(0, L, 512):
        kb.dma('sp', h[a:a + 512, :], x[a:a + 512, :])
    for l in range(depth):
        lambda_init = 0.8 - 0.6 * float(np.exp(-0.3 * l))
        S.barrier()
        A.reset()
        kb.dma('sp', pks[:, :], pkd[l, :, :], w=[t_pk])
        h_src = h
        xT = A.alloc([128, 16, L], BF16)
        t_xT = Tok('xT')
        mark = A.off
        phase_norm_T(kb, A, P, h_src, xT, t_xT, L)
        S.barrier()
        A.off = mark
        phase_inproj(kb, A, P, xT, t_xT, L, w_in[l], pks[:, PK_NMIX:PK_NMIX + 16], t_pk, scr)
        S.barrier()
        A.reset()
        prep_da_params(kb, A, pks[:, PK_DALAM:PK_DALAM + 257], t_pk, par, t_par, lambda_init)
        phase_da(kb, A, P, L, scr, augd, par[:, 0:1], par[:, 1:2], t_par, lambda_init)
        S.barrier()
        A.reset()
        phase_ssd(kb, A, P, L, scr, pks[:, PK_SSD:PK_SSD + 1152], t_pk)
        S.barrier()
        A.reset()
        phase_gdn(kb, A, P, L, scr, pks[:, PK_GDN:PK_GDN + 128], t_pk)
        S.barrier()
        A.reset()
        mT = A.alloc([128, 16, L], BF16)
        t_mT = Tok('mT')
        mark = A.off
        phase_merge(kb, A, P, L, scr, w_branch[l], pks[:, PK_NSB:PK_NSB + 24], t_pk, mT, t_mT)
        S.barrier()
        A.off = mark
        phase_outproj(kb, A, P, L, w_out[l], mT, t_mT, h_src, h)
        S.barrier()
        A.reset()
        xT = A.alloc([128, 16, L], BF16)
        t_xT = Tok('xT2')
        mark = A.off
        phase_norm_T(kb, A, P, h, xT, t_xT, L)
        S.barrier()
        A.off = mark
        phase_ffn(kb, A, P, L, ffn_up[l], ffn_down[l], xT, t_xT, pks[:, PK_NFFN:PK_NFFN + 16], pks[:, PK_FFNC:PK_FFNC + 352], t_pk, h)
    S.barrier()
    A.reset()
    phase_final(kb, A, P, L, h, out, wfin)
    cnt = kb.s.finalize_and_emit()
    return nc, cnt


_CACHE = {}


def kernel(**inputs):
    p = {k: np.asarray(v) for k, v in inputs.items()}
    x = p['x']
    B, L, _ = x.shape
    depth = p['w_in'].shape[0]
    key = (L, depth)
    if key not in _CACHE:
        _CACHE[key] = build_program(L, depth)
    nc, _ = _CACHE[key]
    pk = np.stack([pack_layer(l, p) for l in range(depth)])
    consts = host_consts()
    aug = host_da_aug(L)
    wfin = np.ascontiguousarray(np.broadcast_to(p['norm_final'].astype(np.float32)[None, :], (128, D)))
    shared = dict(w_in=np.ascontiguousarray(p['w_in'], np.float32), w_branch=np.ascontiguousarray(p['w_branch'], np.float32),
                  w_out=np.ascontiguousarray(p['w_out'], np.float32), ffn_up=np.ascontiguousarray(p['ffn_up'], np.float32),
                  ffn_down=np.ascontiguousarray(p['ffn_down'], np.float32), pk=pk, consts=consts, c_aug=aug, wfin=wfin)
    in_maps = [dict(shared, x=np.ascontiguousarray(x[b], np.float32)) for b in range(B)]
    res = run_bass_kernel_spmd(nc, in_maps, core_ids=list(range(B)))
    return np.stack([np.asarray(r['out'], np.float32) for r in res.results]).astype(np.float32)
```

```python
import numpy as np
import ml_dtypes
import concourse.bass as bass
import concourse.mybir as mybir
from concourse.bass_utils import run_bass_kernel_spmd

F32 = mybir.dt.float32
BF16 = mybir.dt.bfloat16
AF = mybir.ActivationFunctionType
ALU = mybir.AluOpType
AX = mybir.AxisListType

ENGS = ['pe', 'act', 'dve', 'pool', 'sp']
NSEM_DMA = 14


class Tok:
    __slots__ = ('name', 'writers', 'readers', 'prev_readers')

    def __init__(self, name=''):
        self.name = name
        self.writers = []
        self.readers = []
        self.prev_readers = []


class _Op:
    __slots__ = ('eng', 'fn', 'waits', 'idx', 'dma', 'dma_k', 'target', 'val')

    def __init__(self, eng, fn, idx, dma):
        self.eng = eng
        self.fn = fn
        self.idx = idx
        self.dma = dma
        self.dma_k = None
        self.waits = []
        self.target = False
        self.val = None


class Sched:
    def __init__(self, nc):
        self.nc = nc
        self.ops = {e: [] for e in ENGS}
        self.ndma = {e: 0 for e in ENGS}
        self.dma_ops = {e: [] for e in ENGS}
        self.wc = {e: {} for e in ENGS}
        self.wd = {e: set() for e in ENGS}
        self.pending = {e: [] for e in ENGS}

    def barrier(self):
        evs = []
        for e in ENGS:
            comp = [o for o in self.ops[e] if not o.dma]
            if comp:
                evs.append(('c', e, comp[-1].idx))
            for o in self.dma_ops[e][-NSEM_DMA:]:
                evs.append(('d', e, o.dma_k))
        for e in ENGS:
            self.pending[e] = list(evs)

    def _add_wait(self, op, ev):
        e = op.eng
        if ev[0] == 'c':
            src, idx = ev[1], ev[2]
            if src == e:
                if e == 'pe':
                    return
                if op.dma:
                    pass
                elif op.idx - idx > 3:
                    return
            if self.wc[e].get(src, -1) >= idx:
                return
            self.wc[e][src] = idx
            op.waits.append(ev)
        else:
            if ev in self.wd[e]:
                return
            self.wd[e].add(ev)
            op.waits.append(ev)

    def op(self, eng, fn, r=(), w=(), pw=(), dma=False):
        lst = self.ops[eng]
        o = _Op(eng, fn, len(lst), dma)
        if dma:
            k = self.ndma[eng]
            self.ndma[eng] += 1
            o.dma_k = k
            if k >= NSEM_DMA:
                self._add_wait(o, ('d', eng, k - NSEM_DMA))
            ev = ('d', eng, k)
            self.dma_ops[eng].append(o)
        else:
            ev = ('c', eng, o.idx)
        if self.pending[eng]:
            for pe_ in self.pending[eng]:
                self._add_wait(o, pe_)
            self.pending[eng] = []
        for t in r:
            for we in t.writers:
                self._add_wait(o, we)
        for t in w:
            for we in t.writers:
                self._add_wait(o, we)
            for re_ in t.readers:
                self._add_wait(o, re_)
            for re_ in t.prev_readers:
                self._add_wait(o, re_)
        for t in pw:
            if t.readers:
                t.prev_readers = t.readers
                t.readers = []
                t.writers = []
            for re_ in t.prev_readers:
                self._add_wait(o, re_)
        for t in r:
            t.readers.append(ev)
            if len(t.readers) > 64:
                t.readers = _compact(t.readers)
        for t in w:
            t.writers = [ev]
            t.readers = []
            t.prev_readers = []
        for t in pw:
            t.writers.append(ev)
            if len(t.writers) > 64:
                t.writers = _compact(t.writers)
        lst.append(o)
        return o

    def finalize_and_emit(self, final_waits=()):
        nc = self.nc
        for e in ENGS:
            for o in self.ops[e]:
                for ev in o.waits:
                    if ev[0] == 'c':
                        self.ops[ev[1]][ev[2]].target = True
        fin = []
        for e in ENGS:
            comp = [o for o in self.ops[e] if not o.dma]
            if comp:
                comp[-1].target = True
                fin.append(('c', e, comp[-1].idx))
            for o in self.dma_ops[e][-NSEM_DMA:]:
                fin.append(('d', e, o.dma_k))
        for e in ENGS:
            c = 0
            for o in self.ops[e]:
                if o.dma:
                    continue
                if o.target:
                    c += 1
                o.val = c
        sems = {}
        dsems = {}
        import contextlib
        with contextlib.ExitStack() as st:
            for e in ENGS:
                sems[e] = st.enter_context(nc.semaphore('s_' + e))
                if self.ndma[e]:
                    dsems[e] = [st.enter_context(nc.semaphore('d_%s_%d' % (e, i))) for i in range(NSEM_DMA)]
            block = st.enter_context(nc.Block())

            def wait_ev(engh, ev):
                if ev[0] == 'c':
                    engh.wait_ge(sems[ev[1]], self.ops[ev[1]][ev[2]].val)
                else:
                    k = ev[2]
                    engh.wait_ge(dsems[ev[1]][k % NSEM_DMA], 16 * (k // NSEM_DMA + 1))

            def emit(e, engh):
                for o in self.ops[e]:
                    for ev in o.waits:
                        wait_ev(engh, ev)
                    ins = o.fn(engh)
                    if o.dma:
                        ins.then_inc(dsems[e][o.dma_k % NSEM_DMA], 16)
                    elif o.target:
                        ins.then_inc(sems[e], 1)
                if e == 'sp':
                    for ev in fin:
                        wait_ev(engh, ev)

            @block.tensor
            def _(h):
                emit('pe', h)

            @block.scalar
            def _(h):
                emit('act', h)

            @block.vector
            def _(h):
                emit('dve', h)

            @block.gpsimd
            def _(h):
                emit('pool', h)

            @block.sync
            def _(h):
                emit('sp', h)
        return {e: len(self.ops[e]) for e in ENGS}


def _compact(evs):
    best = {}
    out = []
    for ev in evs:
        if ev[0] == 'c':
            if best.get(ev[1], -1) < ev[2]:
                best[ev[1]] = ev[2]
        else:
            out.append(ev)
    return [('c', e, i) for e, i in best.items()] + out


class KB:
    def __init__(self, nc):
        self.nc = nc
        self.s = Sched(nc)
        self._n = 0

    def sb(self, shape, dt, name=None):
        self._n += 1
        return self.nc.alloc_sbuf_tensor(name or ('t%d' % self._n), list(shape), dt)

    def ps(self, shape, dt, name=None):
        self._n += 1
        return self.nc.alloc_psum_tensor(name or ('p%d' % self._n), list(shape), dt)

    def dram(self, name, shape, dt, kind="Internal"):
        return self.nc.dram_tensor(name, list(shape), dt, kind=kind).ap()

    def dma(self, q, out, in_, r=(), w=(), pw=()):
        return self.s.op(q, lambda e: e.dma_start(out=out, in_=in_), r=r, w=w, pw=pw, dma=True)

    def mm(self, out, lhsT, rhs, start=True, stop=True, r=(), w=(), pw=()):
        return self.s.op('pe', lambda e: e.matmul(out, lhsT, rhs, start=start, stop=stop), r=r, w=w, pw=pw)

    def tr(self, out, in_, ident, r=(), w=(), pw=()):
        return self.s.op('pe', lambda e: e.transpose(out, in_, ident), r=r, w=w, pw=pw)

    def act(self, out, in_, func, bias=None, scale=None, accum_out=None, r=(), w=(), pw=(), eng='act'):
        kw = {}
        if bias is not None:
            kw['bias'] = bias
        if scale is not None:
            kw['scale'] = scale
        if accum_out is not None:
            kw['accum_out'] = accum_out
        return self.s.op('act', lambda e: e.activation(out, in_, func, **kw), r=r, w=w, pw=pw)

    def copy(self, eng, out, in_, r=(), w=(), pw=()):
        if eng == 'act':
            return self.s.op('act', lambda e: e.copy(out, in_), r=r, w=w, pw=pw)
        return self.s.op(eng, lambda e: e.tensor_copy(out, in_), r=r, w=w, pw=pw)

    def tt(self, eng, out, in0, in1, op, r=(), w=(), pw=()):
        return self.s.op(eng, lambda e: e.tensor_tensor(out, in0, in1, op), r=r, w=w, pw=pw)

    def ts(self, eng, out, in0, s1, s2, op0, op1=None, accum_out=None, r=(), w=(), pw=()):
        def f(e):
            kw = {}
            if accum_out is not None:
                kw['accum_out'] = accum_out
            if op1 is None:
                return e.tensor_scalar(out, in0, s1, None, op0, **kw)
            return e.tensor_scalar(out, in0, s1, s2, op0, op1, **kw)
        return self.s.op(eng, f, r=r, w=w, pw=pw)

    def stt(self, eng, out, in0, scalar, in1, op0, op1, accum_out=None, r=(), w=(), pw=()):
        def f(e):
            kw = {}
            if accum_out is not None:
                kw['accum_out'] = accum_out
            return e.scalar_tensor_tensor(out, in0, scalar, in1, op0, op1, **kw)
        return self.s.op(eng, f, r=r, w=w, pw=pw)

    def rsqrt_eps(self, out, in_, t_in, t_out, eps=1e-6):
        self.s.op('act', lambda e: e.activation(out, in_, AF.Ln, bias=float(eps)), r=[t_in], w=[t_out])
        self.s.op('act', lambda e: e.activation(out, out, AF.Exp, scale=-0.5), r=[t_out], w=[t_out])

    def memset(self, eng, ap, val, r=(), w=(), pw=()):
        return self.s.op(eng, lambda e: e.memset(ap, val), r=r, w=w, pw=pw)


class Arena:
    def __init__(self, kb, base, limit):
        self.kb = kb
        self.base = base
        self.off = base
        self.limit = limit
        self.n = 0

    def reset(self):
        self.off = self.base

    def alloc(self, shape, dt, name=None):
        nb = int(np.prod(shape[1:])) * (4 if dt == F32 else 2)
        nb = (nb + 31) // 32 * 32
        self.n += 1
        h = self.kb.nc.alloc_sbuf_tensor_at(name or ('a%d' % self.n), list(shape), dt, offset=self.off)
        self.off += nb
        assert self.off <= self.limit, ('SBUF arena overflow', self.off, self.limit)
        return h


class GemmRes:
    def __init__(self, kb, arena, next_bank, elems=16 * 512):
        self.elems = elems
        self.wst = [arena.alloc([128, elems], F32) for _ in range(2)]
        self.wbf = [arena.alloc([128, elems], BF16) for _ in range(2)]
        self.t_wst = [Tok('wst%d' % i) for i in range(2)]
        self.t_wbf = [Tok('wbf%d' % i) for i in range(2)]
        self.next_bank = next_bank


def run_gemms(kb, G, xT, t_xT, L, tiles):
    n = len(tiles)

    def view(buf, KC, wd):
        return buf[:, 0:KC * wd].rearrange("p (k c) -> p k c", k=KC)

    def load(j):
        tl = tiles[j]
        KC, wd = tl['KC'], tl['width']
        assert KC * wd <= G.elems
        buf, tk = view(G.wst[j % 2], KC, wd), G.t_wst[j % 2]
        segs = tl.get('segs') or [(tl['c0'], wd)]
        step = 4
        first = True
        off = 0
        for (c0, w_) in segs:
            for a in range(0, KC, step):
                b = min(KC, a + step)
                src = tl['W'][tl['k0'] + a * 128: tl['k0'] + b * 128, c0:c0 + w_].rearrange("(kc p) c -> p kc c", p=128)
                if first:
                    kb.dma('sp', buf[:, a:b, off:off + w_], src, w=[tk])
                    first = False
                else:
                    kb.dma('sp', buf[:, a:b, off:off + w_], src, pw=[tk])
            off += w_

    def cast(j):
        tl = tiles[j]
        KC, wd = tl['KC'], tl['width']
        src, dst = view(G.wst[j % 2], KC, wd), view(G.wbf[j % 2], KC, wd)
        ce = tl.get('cast', 'pool')
        if tl.get('nscale') is not None:
            ns = tl['nscale']
            if ce == 'act':
                for kc in range(KC):
                    kb.act(dst[:, kc, :], src[:, kc, :], AF.Copy, scale=ns[:, kc:kc + 1], r=[G.t_wst[j % 2], tl['t_nscale']],
                           **({'w': [G.t_wbf[j % 2]]} if kc == 0 else {'pw': [G.t_wbf[j % 2]]}))
            else:
                kb.tt('pool', dst, src, ns[:, 0:KC].unsqueeze(2).to_broadcast([128, KC, wd]), ALU.mult,
                      r=[G.t_wst[j % 2], tl['t_nscale']], w=[G.t_wbf[j % 2]])
        else:
            kb.copy(ce, dst, src, r=[G.t_wst[j % 2]], w=[G.t_wbf[j % 2]])

    load(0)
    if n > 1:
        load(1)
    cast(0)
    for j in range(n):
        tl = tiles[j]
        KC, wd = tl['KC'], tl['width']
        if tl.get('pre') is not None:
            tl['pre']()
        if j + 1 < n:
            cast(j + 1)
        if j + 2 < n:
            load(j + 2)
        wb, twb = view(G.wbf[j % 2], KC, wd), G.t_wbf[j % 2]
        kx = tl.get('kx0', 0)
        x_, tx_ = tl.get('xT', xT), tl.get('t_xT', t_xT)
        if tl['mode'] == 'tm':
            for t in range(L // 128):
                ps, tps = G.next_bank()
                for kc in range(KC):
                    kb.mm(ps[:, 0:wd], x_[:, kx + kc, t * 128:(t + 1) * 128], wb[:, kc, 0:wd],
                          start=(kc == 0), stop=(kc == KC - 1), r=[tx_, twb], pw=[tps])
                tl['consume'](ps, tps, tl, 0, t)
        else:
            for sub in range(wd // 128):
                for tb in range(L // 512):
                    ps, tps = G.next_bank()
                    for kc in range(KC):
                        kb.mm(ps[:, 0:512], wb[:, kc, sub * 128:(sub + 1) * 128], x_[:, kx + kc, tb * 512:(tb + 1) * 512],
                              start=(kc == 0), stop=(kc == KC - 1), r=[tx_, twb], pw=[tps])
                    tl['consume'](ps, tps, tl, sub, tb)


D = 2048
DFF = 5632
SEG = dict(da_q=(0, 1024), da_k=(1024, 1024), da_v=(2048, 1024), ssm_z=(3072, 1024), ssm_xbc=(4096, 1536),
           ssm_dt=(5632, 16), gdn_qkv=(5648, 3072), gdn_z=(8720, 1024), gdn_ba=(9744, 16), gates=(9760, 6144))
EPS = 1e-6


class Evac:
    def __init__(self, kb, arena, n=4):
        self.kb = kb
        self.f = [arena.alloc([128, 512], F32) for _ in range(n)]
        self.tf = [Tok('evf%d' % i) for i in range(n)]
        self.b = [arena.alloc([128, 512], BF16) for _ in range(n)]
        self.tb = [Tok('evb%d' % i) for i in range(n)]
        self.i = 0
        self.n = n

    def get(self, dt):
        i = self.i % self.n
        self.i += 1
        if dt == F32:
            return self.f[i], self.tf[i]
        return self.b[i], self.tb[i]

    def eng(self):
        return 'act' if (self.i % 2 == 0) else 'dve'


def store_consumer(kb, EV, dst, dt, func=None, q='sp'):
    def consume(ps, tps, tl, sub, tb):
        wd = tl['width']
        st, tst = EV.get(dt)
        if tl['mode'] == 'tm':
            n = wd
            d = dst[tb * 128:(tb + 1) * 128, tl['doff']:tl['doff'] + wd]
        else:
            n = 512
            f0 = tl['doff'] + sub * 128
            d = dst[f0:f0 + 128, tb * 512:(tb + 1) * 512]
        if func is not None:
            kb.act(st[:, 0:n], ps[:, 0:n], func, r=[tps], w=[tst])
        else:
            kb.copy(EV.eng(), st[:, 0:n], ps[:, 0:n], r=[tps], w=[tst])
        kb.dma(q, d, st[:, 0:n], r=[tst])
    return consume


def phase_norm_T(kb, A, P, h_dram, xT, t_xT, L):
    ld = [A.alloc([128, D], F32) for _ in range(2)]
    t_ld = [Tok() for _ in range(2)]
    xn = [A.alloc([128, D], BF16) for _ in range(2)]
    t_xn = [Tok() for _ in range(2)]
    junk = A.alloc([128, D], BF16)
    t_junk = Tok()
    ssq = [A.alloc([128, 1], F32) for _ in range(2)]
    t_ssq = [Tok() for _ in range(2)]
    rstd = [A.alloc([128, 1], F32) for _ in range(2)]
    t_rstd = [Tok() for _ in range(2)]
    for t in range(L // 128):
        b = t % 2
        kb.dma('sp', ld[b][:, :], h_dram[t * 128:(t + 1) * 128, :], w=[t_ld[b]])
        kb.act(junk[:, :], ld[b][:, :], AF.Square, scale=float(D ** -0.5), accum_out=ssq[b][:, :], r=[t_ld[b]], w=[t_junk, t_ssq[b]])
        kb.rsqrt_eps(rstd[b][:, :], ssq[b][:, :], t_ssq[b], t_rstd[b])
        kb.ts('dve', xn[b][:, :], ld[b][:, :], rstd[b][:, :], None, ALU.mult, r=[t_ld[b], t_rstd[b]], w=[t_xn[b]])
        for g in range(2):
            ps, tps = P.next_bank()
            psb = ps[:, :].bitcast(BF16)
            for i in range(8):
                c = (g * 8 + i) * 128
                kb.tr(psb[:, i * 128:(i + 1) * 128], xn[b][:, c:c + 128], P.ident[:, :], r=[t_xn[b], P.t_const], pw=[tps])
            eng = 'dve' if g == 0 else 'act'
            kb.copy(eng, xT[:, g * 8:(g + 1) * 8, t * 128:(t + 1) * 128],
                    psb[:, 0:1024].rearrange("p (a b) -> p a b", a=8), r=[tps], pw=[t_xT])


class Persist:
    def __init__(self, kb, consts):
        self.kb = kb
        nc = kb.nc
        self.banks = []
        for i in range(8):
            self.banks.append((kb.ps([128, 512], F32, 'bank%d' % i), Tok('bank%d' % i)))
        self.bi = 0
        self.t_const = Tok('const')
        self.consts = consts

    def next_bank(self):
        b = self.banks[self.bi % 8]
        self.bi += 1
        return b


def phase_inproj(kb, A, P, xT, t_xT, L, w_in, nscale, t_nscale, scr):
    G = GemmRes(kb, A, P.next_bank)
    EV = Evac(kb, A)
    tiles = []

    def add(seg, mode, dst, dt, dbase=0, func=None):
        c0, n = SEG[seg]
        cons = store_consumer(kb, EV, dst, dt, func)
        for a in range(0, n, 512):
            wd = min(512, n - a)
            tiles.append(dict(W=w_in, k0=0, KC=16, c0=c0 + a, width=wd, mode=mode, nscale=nscale,
                              t_nscale=t_nscale, consume=cons, doff=dbase + a))
    add('da_q', 'fm', scr['qkT'], BF16, 0)
    add('da_k', 'fm', scr['qkT'], BF16, 1024)
    add('da_v', 'tm', scr['v_tm'], BF16)
    add('ssm_z', 'tm', scr['sz'], F32)
    add('ssm_xbc', 'fm', scr['xbcT'], F32)
    add('ssm_dt', 'tm', scr['sdt'], F32)
    add('gdn_qkv', 'fm', scr['gqkvT'], F32)
    add('gdn_z', 'tm', scr['gz'], F32)
    add('gdn_ba', 'tm', scr['gba'], F32)
    add('gates', 'fm', scr['gatesT'], BF16, 0, AF.Sigmoid)
    run_gemms(kb, G, xT, t_xT, L, tiles)


def make_scratch(kb, L, kind="Internal"):
    scr = {}
    scr['qkT'] = kb.dram('scr_qkT', [2048, L], BF16, kind)
    scr['v_tm'] = kb.dram('scr_v', [L, 1024], BF16, kind)
    scr['sz'] = kb.dram('scr_sz', [L, 1024], F32, kind)
    scr['xbcT'] = kb.dram('scr_xbcT', [1536, L], F32, kind)
    scr['sdt'] = kb.dram('scr_sdt', [L, 16], F32, kind)
    scr['gqkvT'] = kb.dram('scr_gqkvT', [3072, L], F32, kind)
    scr['gz'] = kb.dram('scr_gz', [L, 1024], F32, kind)
    scr['gba'] = kb.dram('scr_gba', [L, 16], F32, kind)
    scr['gatesT'] = kb.dram('scr_gatesT', [6144, L], BF16, kind)
    scr['obT'] = kb.dram('scr_obT', [3072, L], BF16, kind)
    return scr


DA_SLOPES = [2.0 ** (-(h + 1)) for h in range(8)]


def host_da_aug(L):
    t = np.arange(L)
    out = np.zeros((8, 2, 3, L), np.float32)
    for h in range(8):
        sl = DA_SLOPES[h]
        qr = t % 512
        kr = t % 128
        out[h, 0, 0] = -8.0 * sl * (2 * (qr // 2))
        out[h, 0, 1] = -8.0 * sl * (qr % 2)
        out[h, 0, 2] = 1.0
        out[h, 1, 0] = 1.0
        out[h, 1, 1] = 1.0
        out[h, 1, 2] = 8.0 * sl * kr
    return out.astype(ml_dtypes.bfloat16)


def phase_da(kb, A, P, L, scr, c_aug, lam_neg, sw, t_par, lambda_init):
    NQ = L // 512
    QT = [[A.alloc([67, L], BF16) for _ in range(2)] for _ in range(2)]
    KT = [[A.alloc([67, L], BF16) for _ in range(2)] for _ in range(2)]
    V = [A.alloc([128, L // 128, 128], BF16) for _ in range(2)]
    t_qkv = [Tok('qkv%d' % i) for i in range(2)]
    NPT = 4
    PT = [A.alloc([128, 512], BF16) for _ in range(NPT)]
    t_PT = [Tok('pt%d' % i) for i in range(NPT)]
    rl = [A.alloc([128, 512], F32) for _ in range(2)]
    t_rl = [Tok(), Tok()]
    on = [A.alloc([128, 512], F32) for _ in range(2)]
    t_on = [Tok(), Tok()]
    o = A.alloc([128, 512], F32)
    t_o = Tok()
    sq = A.alloc([128, 512], F32)
    t_sq = Tok()
    o2 = A.alloc([128, 512], F32)
    t_o2 = Tok()
    rs = A.alloc([128, 512], F32)
    t_rs = Tok()
    ob = [A.alloc([128, 512], BF16) for _ in range(2)]
    t_ob = [Tok(), Tok()]
    S_b = [P.banks[0], P.banks[1]]
    O_b = [P.banks[2], P.banks[3]]
    L_b = [P.banks[4], P.banks[5]]
    N_b = P.banks[6]
    st = {'pti': 0, 'si': 0, 'dq_evac': [], 'dq_tail': []}

    def load_head(h):
        s = h % 2
        first = True
        for i in range(2):
            for (dst, base) in ((QT[s][i], 0), (KT[s][i], 1024)):
                r0 = base + h * 128 + i * 64
                if first:
                    kb.dma('sp', dst[0:64, :], scr['qkT'][r0:r0 + 64, :], w=[t_qkv[s]])
                    first = False
                else:
                    kb.dma('sp', dst[0:64, :], scr['qkT'][r0:r0 + 64, :], pw=[t_qkv[s]])
            kb.dma('sp', QT[s][i][64:67, :], c_aug[h, 0, :, :], pw=[t_qkv[s]])
            kb.dma('sp', KT[s][i][64:67, :], c_aug[h, 1, :, :], pw=[t_qkv[s]])
        kb.dma('sp', V[s][:, :, :], scr['v_tm'][:, h * 128:(h + 1) * 128].rearrange("(t p) e -> p t e", p=128),
               pw=[t_qkv[s]])

    load_head(0)
    for h in range(8):
        s = h % 2
        if h + 1 < 8:
            load_head(h + 1)
        sl = DA_SLOPES[h]
        for j in range(NQ):
            nk = 4 * (j + 1)
            for i in range(2):
                Ob, tO = O_b[i]
                Lb, tL = L_b[i]
                pend = {}

                def emit_S(kt, j=j, i=i, s=s, sl=sl, pend=pend):
                    c = kt - 4 * j
                    c0 = 128 * c if c > 0 else 0
                    Sb, tS = S_b[st['si'] % 2]
                    st['si'] += 1
                    kb.mm(Sb[:, c0:512], KT[s][i][0:67, kt * 128:(kt + 1) * 128], QT[s][i][0:67, j * 512 + c0:(j + 1) * 512],
                          start=True, stop=(c < 0), r=[t_qkv[s]], w=[tS])
                    if c >= 0:
                        kb.mm(Sb[:, c0:c0 + 128], P.ident[:, :], P.negtri_bf[:, :], start=False, stop=True, r=[P.t_const], pw=[tS])
                    pt, tpt = PT[st['pti'] % NPT], t_PT[st['pti'] % NPT]
                    st['pti'] += 1
                    kb.act(pt[:, c0:512], Sb[:, c0:512], AF.Exp, bias=float(sl * (kt * 128 - j * 512)), scale=0.125,
                           r=[tS], w=[tpt])
                    pend[kt] = (pt, tpt, c0)

                def emit_AV(kt, nk=nk, s=s, Ob=Ob, tO=tO, Lb=Lb, tL=tL, pend=pend):
                    pt, tpt, c0 = pend.pop(kt)
                    kb.mm(Ob[:, c0:512], V[s][:, kt, :], pt[:, c0:512], start=(kt == 0), stop=(kt == nk - 1),
                          r=[tpt, t_qkv[s]], pw=[tO])
                    kb.mm(Lb[:, c0:512], P.ones_bf[:, :], pt[:, c0:512], start=(kt == 0), stop=(kt == nk - 1),
                          r=[tpt, P.t_const], pw=[tL])

                emit_S(0)
                for kt in range(nk):
                    if kt + 1 < nk:
                        emit_S(kt + 1)
                    emit_AV(kt)
                    if kt == 0:
                        for f in st['dq_evac']:
                            f()
                        st['dq_evac'] = []
                    if kt == 2:
                        for f in st['dq_tail']:
                            f()
                        st['dq_tail'] = []

                def evac(i=i, Ob=Ob, tO=tO, Lb=Lb, tL=tL):
                    kb.act(rl[i][:, :], Lb[:, :], AF.Ln, r=[tL], w=[t_rl[i]])
                    kb.act(rl[i][:, :], rl[i][:, :], AF.Exp, scale=-1.0, r=[t_rl[i]], w=[t_rl[i]])
                    kb.tt('dve', on[i][:, :], Ob[:, :], rl[i][:, :], ALU.mult, r=[tO, t_rl[i]], w=[t_on[i]])
                st['dq_evac'].append(evac)

            def tail(h=h, j=j):
                kb.stt('dve', o[:, :], on[1][:, :], lam_neg, on[0][:, :], ALU.mult, ALU.add, r=[t_on[0], t_on[1], t_par], w=[t_o])
                kb.act(sq[:, :], o[:, :], AF.Square, r=[t_o], w=[t_sq])
                Nb, tN = N_b
                kb.mm(Nb[:, :], P.ones_f32[:, :], sq[:, :], r=[t_sq, P.t_const], w=[tN])
                kb.act(rs[:, :], Nb[:, :], AF.Ln, scale=1.0 / 128, bias=float(EPS), r=[tN], w=[t_rs])
                kb.act(rs[:, :], rs[:, :], AF.Exp, scale=-0.5, r=[t_rs], w=[t_rs])
                b = (h * NQ + j) % 2
                kb.stt('dve', ob[b][:, :], o[:, :], sw, rs[:, :], ALU.mult, ALU.mult, r=[t_o, t_rs, t_par], w=[t_ob[b]])
                kb.dma('sp', scr['obT'][h * 128:(h + 1) * 128, j * 512:(j + 1) * 512], ob[b][:, :], r=[t_ob[b]])
            st['dq_tail'].append(tail)
    for f in st['dq_evac'] + st['dq_tail']:
        f()


def host_consts():
    c = np.zeros((128, 1152), np.float32)
    i = np.arange(128)
    c[:, 0:128] = np.eye(128)
    c[:, 128:256] = (i[:, None] <= i[None, :])
    c[:, 256:384] = 1.0
    c[:, 384:512] = (i[:, None] > i[None, :])
    blk = (i[:, None] // 64) == (i[None, :] // 64)
    c[:, 512:640] = (i[:, None] <= i[None, :]) & blk
    c[:, 640:768] = (i[:, None] > i[None, :]) & blk
    c[:, 768:896] = (i[:, None] < i[None, :]) & blk
    c[:, 896:1024] = blk
    c[:, 1024:1152] = -30000.0 * (i[:, None] > i[None, :])
    return c


def setup_consts(kb, A, P, cd):
    cst = A.alloc([128, 1152], F32)
    kb.dma('sp', cst[:, :], cd[:, :], w=[P.t_const])
    P.cst = cst
    P.ident = A.alloc([128, 128], BF16)
    P.tri_bf = A.alloc([128, 128], BF16)
    P.ones_bf = A.alloc([128, 128], BF16)
    kb.copy('dve', P.ident[:, :], cst[:, 0:128], r=[P.t_const], pw=[P.t_const])
    kb.copy('dve', P.tri_bf[:, :], cst[:, 128:256], r=[P.t_const], pw=[P.t_const])
    kb.copy('dve', P.ones_bf[:, :], cst[:, 256:384], r=[P.t_const], pw=[P.t_const])
    P.negtri_bf = A.alloc([128, 128], BF16)
    kb.copy('dve', P.negtri_bf[:, :], cst[:, 1024:1152], r=[P.t_const], pw=[P.t_const])
    P.ident_f32 = cst[:, 0:128]
    P.tri_f32 = cst[:, 128:256]
    P.ones_f32 = cst[:, 256:384]
    P.lstrict_f32 = cst[:, 384:512]
    P.u2_f32 = cst[:, 512:640]
    P.l2_f32 = cst[:, 640:768]
    P.su2_f32 = cst[:, 768:896]
    P.blk_f32 = cst[:, 896:1024]


def prep_da_params(kb, A, pks, t_pk, par, t_par, lambda_init):
    tmp = A.alloc([128, 64], F32)
    s12 = A.alloc([128, 2], F32)
    t_tmp = Tok()
    for i in range(2):
        kb.tt('dve', tmp[:, :], pks[:, i * 128:i * 128 + 64], pks[:, i * 128 + 64:i * 128 + 128], ALU.mult, r=[t_pk], w=[t_tmp])
        kb.s.op('dve', lambda e, i=i: e.reduce_sum(s12[:, i:i + 1], tmp[:, :], axis=AX.X), r=[t_tmp], pw=[t_par])
    kb.act(s12[:, :], s12[:, :], AF.Exp, r=[t_par], w=[t_par])
    kb.tt('dve', par[:, 0:1], s12[:, 1:2], s12[:, 0:1], ALU.subtract, r=[t_par], pw=[t_par])
    kb.ts('dve', par[:, 0:1], par[:, 0:1], -float(lambda_init), None, ALU.add, r=[t_par], pw=[t_par])
    kb.ts('dve', par[:, 1:2], pks[:, 256:257], float(1.0 - lambda_init), None, ALU.mult, r=[t_pk, t_par], pw=[t_par])


def conv_silu_chunk(kb, xin, t_xin, acc, t_acc, src_rows, L, K, wcols, bcol, t_par, out_ap, t_out, out_w=True):
    pad = K - 1
    kb.dma('sp', xin[:, pad:pad + L], src_rows, pw=[t_xin])
    if bcol is not None:
        kb.ts('dve', acc[:, 0:L], xin[:, pad:pad + L], wcols[K - 1], bcol, ALU.mult, ALU.add, r=[t_xin, t_par], w=[t_acc])
    else:
        kb.ts('dve', acc[:, 0:L], xin[:, pad:pad + L], wcols[K - 1], None, ALU.mult, r=[t_xin, t_par], w=[t_acc])
    for j in range(K - 1):
        kb.stt('dve', acc[:, 0:L], xin[:, j:j + L], wcols[j], acc[:, 0:L], ALU.mult, ALU.add, r=[t_xin, t_par, t_acc], w=[t_acc])
    if out_w:
        kb.act(out_ap, acc[:, 0:L], AF.Silu, r=[t_acc], w=[t_out])
    else:
        kb.act(out_ap, acc[:, 0:L], AF.Silu, r=[t_acc], pw=[t_out])


def softplus_tm(kb, A, out, in_, bias_bc, shape, t_in, t_out, t_par):
    xb = A.alloc(shape, F32)
    ab = A.alloc(shape, F32)
    t_x = Tok()
    t_a = Tok()
    sl = tuple([slice(None)] * len(shape))
    kb.tt('dve', xb[sl], in_, bias_bc, ALU.add, r=[t_in, t_par], w=[t_x])
    kb.ts('dve', ab[sl], xb[sl], -1.0, None, ALU.mult, r=[t_x], w=[t_a])
    kb.tt('dve', ab[sl], ab[sl], xb[sl], ALU.max, r=[t_x, t_a], w=[t_a])
    kb.act(ab[sl], ab[sl], AF.Exp, scale=-1.0, r=[t_a], w=[t_a])
    kb.act(ab[sl], ab[sl], AF.Ln, bias=1.0, r=[t_a], w=[t_a])
    kb.ts('dve', xb[sl], xb[sl], 0.0, None, ALU.max, r=[t_x], w=[t_x])
    kb.tt('dve', out, xb[sl], ab[sl], ALU.add, r=[t_x, t_a], w=[t_out])


def phase_ssd(kb, A, P, L, scr, pks, t_pk):
    T = L // 128
    x_tm = A.alloc([128, T, 1024], BF16)
    t_xtm = Tok('x_tm')
    BT = A.alloc([128, 2, L], BF16)
    CT = A.alloc([128, 2, L], BF16)
    t_BC = Tok('BCT')
    B_tm = A.alloc([128, T, 256], BF16)
    t_Btm = Tok('B_tm')
    xin = [A.alloc([128, 3 + L], F32) for _ in range(2)]
    t_xin = [Tok(), Tok()]
    acc = A.alloc([128, L], F32)
    t_acc = Tok()
    xs = [A.alloc([128, L], BF16) for _ in range(2)]
    t_xs = [Tok(), Tok()]
    for b in range(2):
        kb.memset('dve', xin[b][:, 0:3], 0.0, w=[t_xin[b]])
    acc2 = [acc, A.alloc([128, L], F32)]
    t_acc2 = [t_acc, Tok()]

    def partA(cc):
        b = cc % 2
        wcols = [pks[:, cc * 4 + j:cc * 4 + j + 1] for j in range(4)]
        bcol = pks[:, 48 + cc:49 + cc]
        src = scr['xbcT'][cc * 128:(cc + 1) * 128, :]
        if cc < 8:
            conv_silu_chunk(kb, xin[b], t_xin[b], acc2[b], t_acc2[b], src, L, 4, wcols, bcol, t_pk, xs[b][:, :], t_xs[b])
        elif cc < 10:
            conv_silu_chunk(kb, xin[b], t_xin[b], acc2[b], t_acc2[b], src, L, 4, wcols, bcol, t_pk, BT[:, cc - 8, :], t_BC, out_w=False)
        else:
            conv_silu_chunk(kb, xin[b], t_xin[b], acc2[b], t_acc2[b], src, L, 4, wcols, bcol, t_pk, CT[:, cc - 10, :], t_BC, out_w=False)

    def partB(cc):
        b = cc % 2
        if cc < 8:
            for t0 in range(0, T, 8):
                ps, tps = P.next_bank()
                psb = ps[:, :].bitcast(BF16)
                nt = min(8, T - t0)
                for i in range(nt):
                    kb.tr(psb[:, i * 128:(i + 1) * 128], xs[b][:, (t0 + i) * 128:(t0 + i + 1) * 128], P.ident[:, :],
                          r=[t_xs[b], P.t_const], pw=[tps])
                kb.copy('act' if (t0 // 8) % 2 else 'dve', x_tm[:, t0:t0 + nt, cc * 128:(cc + 1) * 128],
                        psb[:, 0:nt * 128].rearrange("p (a b) -> p a b", a=nt), r=[tps], pw=[t_xtm])
        elif cc < 10:
            g = cc - 8
            for t0 in range(0, T, 8):
                ps, tps = P.next_bank()
                psb = ps[:, :].bitcast(BF16)
                nt = min(8, T - t0)
                for i in range(nt):
                    kb.tr(psb[:, i * 128:(i + 1) * 128], BT[:, g, (t0 + i) * 128:(t0 + i + 1) * 128], P.ident[:, :],
                          r=[t_BC, P.t_const], pw=[tps])
                kb.copy('dve', B_tm[:, t0:t0 + nt, g * 128:(g + 1) * 128],
                        psb[:, 0:nt * 128].rearrange("p (a b) -> p a b", a=nt), r=[tps], pw=[t_Btm])

    partA(0)
    for cc in range(12):
        if cc + 1 < 12:
            partA(cc + 1)
        partB(cc)
    dtr = A.alloc([128, T, 16], F32)
    t_dtr = Tok()
    kb.dma('sp', dtr[:, :, :], scr['sdt'].rearrange("(t p) h -> p t h", p=128), w=[t_dtr])
    dt = A.alloc([128, T, 16], F32)
    t_dt = Tok()
    softplus_tm(kb, A, dt[:, :, :], dtr[:, :, :], pks[:, 64:80].unsqueeze(1).to_broadcast([128, T, 16]), [128, T, 16], t_dtr, t_dt, t_pk)
    aneg = A.alloc([128, 16], F32)
    t_an = Tok()
    kb.act(aneg[:, :], pks[:, 80:96], AF.Exp, r=[t_pk], w=[t_an])
    kb.ts('dve', aneg[:, :], aneg[:, :], -1.0, None, ALU.mult, r=[t_an], w=[t_an])
    a_all = A.alloc([128, T, 16], F32)
    t_a = Tok()
    kb.tt('dve', a_all[:, :, :], dt[:, :, :], aneg[:, :].unsqueeze(1).to_broadcast([128, T, 16]), ALU.mult, r=[t_dt, t_an], w=[t_a])
    S = A.alloc([128, 1024], F32)
    Sbf = A.alloc([128, 1024], BF16)
    t_S = Tok('S')
    t_Sbf = Tok('Sbf')
    kb.memset('dve', S[:, :], 0.0, w=[t_S])
    kb.memset('dve', Sbf[:, :], 0.0, w=[t_Sbf])
    pre = A.alloc([128, 48], F32)
    t_pre = Tok()
    E3 = A.alloc([128, 48], F32)
    t_E3 = Tok()
    rhsA = A.alloc([128, 16, 128], F32)
    t_rhsA = Tok()
    ET = A.alloc([128, 16, 128], F32)
    t_ET = Tok()
    Gm = A.alloc([128, 2, 128], F32)
    t_Gm = Tok()
    M = A.alloc([128, 16, 128], BF16)
    t_M = Tok()
    xdt = A.alloc([128, 16, 64], BF16)
    t_xdt = Tok()
    xdtd = A.alloc([128, 16, 64], BF16)
    t_xdtd = Tok()
    t1 = A.alloc([128, 1024], F32)
    t_t1 = Tok()
    y = A.alloc([128, 1024], F32)
    t_y = Tok()
    zt = A.alloc([128, 1024], F32)
    t_zt = Tok()
    junk = A.alloc([128, 512], BF16)
    t_junk = Tok()
    ssq = A.alloc([128, 2], F32)
    t_ssq = Tok()
    yn = A.alloc([128, 1024], BF16)
    t_yn = Tok()
    oT = A.alloc([128, 8, 128], BF16)
    t_oT = Tok()
    Drep = pks[:, 96:1120]
    bk = P.banks
    E3s = [E3, A.alloc([128, 48], F32)]
    t_E3s = [t_E3, Tok()]
    xdtds = [xdtd, A.alloc([128, 16, 64], BF16)]
    t_xdtds = [t_xdtd, Tok()]
    yds = [A.alloc([128, 1024], F32) for _ in range(2)]
    t_yds = [Tok(), Tok()]

    def gen_pre(c):
        p = c % 2
        E3_, tE3_ = E3s[p], t_E3s[p]
        a_c = a_all[:, c, :]
        tok = slice(c * 128, (c + 1) * 128)
        b0, tb0 = bk[0]
        kb.mm(b0[:, 0:16], P.tri_f32, a_c, r=[t_a, P.t_const], w=[tb0])
        kb.mm(b0[:, 16:32], P.ones_f32, a_c, r=[t_a, P.t_const], pw=[tb0])
        for g in range(2):
            kb.mm(b0[:, 128 + g * 128:256 + g * 128], BT[:, g, tok], CT[:, g, tok], r=[t_BC], pw=[tb0])
        kb.copy('dve', pre[:, 0:16], b0[:, 0:16], r=[tb0], w=[t_pre])
        kb.copy('dve', pre[:, 32:48], b0[:, 16:32], r=[tb0], pw=[t_pre])
        kb.tt('dve', pre[:, 16:32], pre[:, 32:48], pre[:, 0:16], ALU.subtract, r=[t_pre], pw=[t_pre])
        kb.act(E3_[:, :], pre[:, :], AF.Exp, r=[t_pre], w=[tE3_])
        kb.tt('dve', Gm[:, :, :], b0[:, 128:384].rearrange("p (g l) -> p g l", g=2),
              P.tri_f32.unsqueeze(1).to_broadcast([128, 2, 128]), ALU.mult, r=[tb0, P.t_const], w=[t_Gm])
        yield
        kb.tt('dve', rhsA[:, :, :], P.tri_f32.unsqueeze(1).to_broadcast([128, 16, 128]),
              a_c.unsqueeze(2).to_broadcast([128, 16, 128]), ALU.mult, r=[t_a, P.t_const], w=[t_rhsA])
        kb.tt('pool', xdt[:, :, :], x_tm[:, c, :].rearrange("p (h d) -> p h d", h=16),
              dt[:, c, :].unsqueeze(2).to_broadcast([128, 16, 64]), ALU.mult, r=[t_xtm, t_dt], w=[t_xdt])
        kb.tt('pool', xdtds[p][:, :, :], xdt[:, :, :], E3_[:, 16:32].unsqueeze(2).to_broadcast([128, 16, 64]), ALU.mult,
              r=[t_xdt, tE3_], w=[t_xdtds[p]])
        for hb in range(4):
            bs, tbs = bk[1 + hb % 2]
            kb.mm(bs[:, :], P.lstrict_f32, rhsA[:, hb * 4:(hb + 1) * 4, :].rearrange("p a b -> p (a b)"),
                  r=[t_rhsA, P.t_const], w=[tbs])
            kb.act(ET[:, hb * 4:(hb + 1) * 4, :].rearrange("p a b -> p (a b)"), bs[:, :], AF.Exp, r=[tbs],
                   **({'w': [t_ET]} if hb == 0 else {'pw': [t_ET]}))
            if hb % 2 == 1:
                yield
        for g in range(2):
            kb.tt('dve', M[:, g * 8:(g + 1) * 8, :], ET[:, g * 8:(g + 1) * 8, :],
                  Gm[:, g:g + 1, :].to_broadcast([128, 8, 128]), ALU.mult, r=[t_ET, t_Gm],
                  **({'w': [t_M]} if g == 0 else {'pw': [t_M]}))
        yield
        for h in range(16):
            by, tby = bk[3 + h // 8]
            kb.mm(by[:, (h % 8) * 64:(h % 8 + 1) * 64], M[:, h, :], xdt[:, h, :], r=[t_M, t_xdt],
                  **({'w': [tby]} if h % 8 == 0 else {'pw': [tby]}))
        for g in range(2):
            by, tby = bk[3 + g]
            kb.copy('act', yds[p][:, g * 512:(g + 1) * 512], by[:, :], r=[tby],
                    **({'w': [t_yds[p]]} if g == 0 else {'pw': [t_yds[p]]}))
        yield

    def gen_rec(c):
        p = c % 2
        E3_, tE3_ = E3s[p], t_E3s[p]
        tok = slice(c * 128, (c + 1) * 128)
        for g in range(2):
            bo, tbo = bk[5 + g]
            kb.mm(bo[:, :], CT[:, g, tok], Sbf[:, g * 512:(g + 1) * 512], r=[t_BC, t_Sbf], w=[tbo])
        kb.dma('sp', zt[:, :], scr['sz'][c * 128:(c + 1) * 128, :], w=[t_zt])
        kb.act(zt[:, :], zt[:, :], AF.Silu, r=[t_zt], w=[t_zt])
        for g in range(2):
            hs = slice(g * 512, (g + 1) * 512)
            bo, tbo = bk[5 + g]
            kb.tt('dve', t1[:, hs].rearrange("p (h d) -> p h d", h=8), bo[:, :].rearrange("p (h d) -> p h d", h=8),
                  E3_[:, g * 8:(g + 1) * 8].unsqueeze(2).to_broadcast([128, 8, 64]), ALU.mult, r=[tbo, tE3_],
                  **({'w': [t_t1]} if g == 0 else {'pw': [t_t1]}))
        yield
        if c + 1 < T:
            for g in range(2):
                hs = slice(g * 512, (g + 1) * 512)
                bo, tbo = bk[5 + g]
                kb.mm(bo[:, :], B_tm[:, c, g * 128:(g + 1) * 128], xdtds[p][:, g * 8:(g + 1) * 8, :].rearrange("p a b -> p (a b)"),
                      r=[t_Btm, t_xdtds[p]], w=[tbo])
                kb.tt('dve', S[:, hs].rearrange("p (h d) -> p h d", h=8), S[:, hs].rearrange("p (h d) -> p h d", h=8),
                      E3_[:, 32 + g * 8:32 + (g + 1) * 8].unsqueeze(2).to_broadcast([128, 8, 64]), ALU.mult,
                      r=[tE3_, t_S], pw=[t_S])
                kb.tt('dve', S[:, hs], S[:, hs], bo[:, :], ALU.add, r=[tbo, t_S], pw=[t_S])
                kb.copy('act', Sbf[:, hs], S[:, hs], r=[t_S], **({'w': [t_Sbf]} if g == 0 else {'pw': [t_Sbf]}))
            yield
        kb.tt('dve', t1[:, :], t1[:, :], yds[p][:, :], ALU.add, r=[t_yds[p], t_t1], w=[t_t1])
        for g in range(2):
            hs = slice(g * 512, (g + 1) * 512)
            kb.tt('dve', y[:, hs], x_tm[:, c, hs], Drep[:, hs], ALU.mult, r=[t_xtm, t_pk],
                  **({'w': [t_y]} if g == 0 else {'pw': [t_y]}))
        kb.tt('dve', y[:, :], y[:, :], t1[:, :], ALU.add, r=[t_y, t_t1], w=[t_y])
        kb.tt('dve', y[:, :], y[:, :], zt[:, :], ALU.mult, r=[t_y, t_zt], w=[t_y])
        yield
        for g in range(2):
            hs = slice(g * 512, (g + 1) * 512)
            kb.act(junk[:, :], y[:, hs], AF.Square, scale=float(512 ** -0.5), accum_out=ssq[:, g:g + 1], r=[t_y],
                   **({'w': [t_junk, t_ssq]} if g == 0 else {'w': [t_junk], 'pw': [t_ssq]}))
        kb.rsqrt_eps(ssq[:, :], ssq[:, :], t_ssq, t_ssq)
        for g in range(2):
            hs = slice(g * 512, (g + 1) * 512)
            kb.act(yn[:, hs], y[:, hs], AF.Copy, scale=ssq[:, g:g + 1], r=[t_y, t_ssq],
                   **({'w': [t_yn]} if g == 0 else {'pw': [t_yn]}))
        yield
        bt, tbt = bk[7]
        btb = bt[:, :].bitcast(BF16)
        for j in range(8):
            kb.tr(btb[:, j * 128:(j + 1) * 128], yn[:, j * 128:(j + 1) * 128], P.ident[:, :], r=[t_yn, P.t_const],
                  **({'w': [tbt]} if j == 0 else {'pw': [tbt]}))
        kb.copy('act', oT[:, :, :], btb[:, 0:1024].rearrange("p (a b) -> p a b", a=8), r=[tbt], w=[t_oT])
        kb.dma('sp', scr['obT'][1024:2048, c * 128:(c + 1) * 128].rearrange("(j p) t -> p j t", p=128), oT[:, :, :], r=[t_oT])
        yield

    for _ in gen_pre(0):
        pass
    for c in range(T):
        gr = gen_rec(c)
        gp = gen_pre(c + 1) if c + 1 < T else iter(())
        alive = [True, True]
        while alive[0] or alive[1]:
            if alive[1]:
                try:
                    next(gp)
                except StopIteration:
                    alive[1] = False
            if alive[0]:
                try:
                    next(gr)
                except StopIteration:
                    alive[0] = False


def phase_gdn(kb, A, P, L, scr, pks, t_pk):
    T = L // 128
    NB = L // 512
    bk = P.banks
    base0 = A.off
    bar = A.alloc([128, T, 16], F32)
    t_bar = Tok()
    kb.dma('sp', bar[:, :, :], scr['gba'].rearrange("(t p) c -> p t c", p=128), w=[t_bar])
    beta = A.alloc([128, T, 8], F32)
    negb = A.alloc([128, T, 8], F32)
    t_beta = Tok()
    kb.act(beta[:, :, :], bar[:, :, 0:8], AF.Sigmoid, r=[t_bar], w=[t_beta])
    kb.ts('dve', negb[:, :, :], beta[:, :, :], -1.0, None, ALU.mult, r=[t_beta], pw=[t_beta])
    sp = A.alloc([128, T, 8], F32)
    t_sp = Tok()
    softplus_tm(kb, A, sp[:, :, :], bar[:, :, 8:16], pks[:, 96:104].unsqueeze(1).to_broadcast([128, T, 8]), [128, T, 8], t_bar, t_sp, t_pk)
    aneg = A.alloc([128, 8], F32)
    t_an = Tok()
    kb.act(aneg[:, :], pks[:, 104:112], AF.Exp, r=[t_pk], w=[t_an])
    kb.ts('dve', aneg[:, :], aneg[:, :], -1.0, None, ALU.mult, r=[t_an], w=[t_an])
    g_all = A.alloc([128, T, 8], F32)
    t_g = Tok()
    kb.tt('dve', g_all[:, :, :], sp[:, :, :], aneg[:, :].unsqueeze(1).to_broadcast([128, T, 8]), ALU.mult, r=[t_sp, t_an], w=[t_g])
    base1 = A.off
    for hb in range(2):
        kb.s.barrier()
        A.off = base1
        qT = A.alloc([128, 4, L], BF16)
        kT = A.alloc([128, 4, L], BF16)
        t_qk = Tok('qkT')
        k_tm = A.alloc([128, T, 4, 128], BF16)
        v_tm = A.alloc([128, T, 4, 128], BF16)
        t_kv = Tok('kv_tm')
        xin = [A.alloc([128, 3 + L], F32) for _ in range(2)]
        t_xin = [Tok(), Tok()]
        acc2 = [A.alloc([128, L], F32) for _ in range(2)]
        t_acc2 = [Tok(), Tok()]
        ks2 = [A.alloc([128, L], F32) for _ in range(2)]
        t_ks2 = [Tok(), Tok()]
        sq2 = [A.alloc([128, L], F32) for _ in range(2)]
        t_sq2 = [Tok(), Tok()]
        rn2 = [A.alloc([128, 512], F32) for _ in range(2)]
        t_rn2 = [Tok(), Tok()]
        vs2 = [A.alloc([128, L], BF16) for _ in range(2)]
        t_vs2 = [Tok(), Tok()]
        for b in range(2):
            kb.memset('dve', xin[b][:, 0:3], 0.0, w=[t_xin[b]])
        ci = 0
        def partA(kind, hh, b):
            h = hb * 4 + hh
            cc = kind * 8 + h
            wcols = [pks[:, cc * 4 + j:cc * 4 + j + 1] for j in range(4)]
            src = scr['gqkvT'][cc * 128:(cc + 1) * 128, :]
            if kind < 2:
                conv_silu_chunk(kb, xin[b], t_xin[b], acc2[b], t_acc2[b], src, L, 4, wcols, None, t_pk, ks2[b][:, :], t_ks2[b])
                kb.act(sq2[b][:, :], ks2[b][:, :], AF.Square, r=[t_ks2[b]], w=[t_sq2[b]])
            else:
                conv_silu_chunk(kb, xin[b], t_xin[b], acc2[b], t_acc2[b], src, L, 4, wcols, None, t_pk, vs2[b][:, :], t_vs2[b])

        def partB(kind, hh, b):
            ks, t_ks, sq, t_sq, vs, t_vs = ks2[b], t_ks2[b], sq2[b], t_sq2[b], vs2[b], t_vs2[b]
            if kind < 2:
                dstT = qT if kind == 0 else kT
                scale = float(128 ** -0.5) if kind == 0 else 1.0
                for tb in range(NB):
                    ps, tps = P.next_bank()
                    cs_ = slice(tb * 512, (tb + 1) * 512)
                    rn, t_rn = rn2[tb % 2], t_rn2[tb % 2]
                    kb.mm(ps[:, :], P.ones_f32, sq[:, cs_], r=[t_sq, P.t_const], w=[tps])
                    kb.act(rn[:, :], ps[:, :], AF.Ln, bias=1e-6, r=[tps], w=[t_rn])
                    kb.act(rn[:, :], rn[:, :], AF.Exp, scale=-0.5, r=[t_rn], w=[t_rn])
                    kb.stt('dve', dstT[:, hh, cs_], ks[:, cs_], scale, rn[:, :], ALU.mult, ALU.mult, r=[t_ks, t_rn], pw=[t_qk])
                if kind == 1:
                    for t0 in range(0, T, 8):
                        ps, tps = P.next_bank()
                        psb = ps[:, :].bitcast(BF16)
                        nt = min(8, T - t0)
                        for i in range(nt):
                            kb.tr(psb[:, i * 128:(i + 1) * 128], kT[:, hh, (t0 + i) * 128:(t0 + i + 1) * 128], P.ident[:, :],
                                  r=[t_qk, P.t_const], pw=[tps])
                        kb.copy('act', k_tm[:, t0:t0 + nt, hh, :], psb[:, 0:nt * 128].rearrange("p (a b) -> p a b", a=nt),
                                r=[tps], pw=[t_kv])
            else:
                for t0 in range(0, T, 8):
                    ps, tps = P.next_bank()
                    psb = ps[:, :].bitcast(BF16)
                    nt = min(8, T - t0)
                    for i in range(nt):
                        kb.tr(psb[:, i * 128:(i + 1) * 128], vs[:, (t0 + i) * 128:(t0 + i + 1) * 128], P.ident[:, :],
                              r=[t_vs, P.t_const], pw=[tps])
                    kb.copy('act', v_tm[:, t0:t0 + nt, hh, :], psb[:, 0:nt * 128].rearrange("p (a b) -> p a b", a=nt),
                            r=[tps], pw=[t_kv])

        chunks = [(kind, hh, i % 2) for i, (kind, hh) in enumerate([(k_, h_) for k_ in range(3) for h_ in range(4)])]
        partA(*chunks[0])
        for i in range(len(chunks)):
            if i + 1 < len(chunks):
                partA(*chunks[i + 1])
            partB(*chunks[i])
        S = A.alloc([128, 4, 128], F32)
        Sbf = A.alloc([128, 4, 128], BF16)
        t_S = Tok('S')
        t_Sbf = Tok('Sbf')
        kb.memset('dve', S[:, :, :], 0.0, w=[t_S])
        kb.memset('dve', Sbf[:, :, :], 0.0, w=[t_Sbf])
        gm = A.alloc([128, 2, 4], F32)
        t_gm = Tok()
        pre = A.alloc([128, 16], F32)
        t_pre = Tok()
        E = A.alloc([128, 16], F32)
        t_E = Tok()
        rhsG = A.alloc([128, 4, 128], F32)
        t_rhsG = Tok()
        DT = A.alloc([128, 4, 128], F32)
        t_DT = Tok()
        tmp = A.alloc([128, 4, 128], F32)
        t_tmp = Tok()
        tmp2 = A.alloc([128, 4, 128], F32)
        t_tmp2 = Tok()
        attnT = A.alloc([128, 4, 128], BF16)
        t_attn = Tok()
        Xs = [A.alloc([128, 4, 128], BF16) for _ in range(2)]
        Ys = [A.alloc([128, 4, 128], BF16) for _ in range(2)]
        t_X = [Tok(), Tok()]
        t_Y = [Tok(), Tok()]
        Rs = [A.alloc([128, 4, 128], BF16) for _ in range(2)]
        t_R = [Tok(), Tok()]
        u0b = A.alloc([128, 4, 128], F32)
        t_u0b = Tok()
        w0T = A.alloc([128, 4, 128], BF16)
        t_w0T = Tok()
        keg = A.alloc([128, 4, 128], BF16)
        t_keg = Tok()
        kd = A.alloc([128, 4, 128], BF16)
        t_kd = Tok()
        vn = A.alloc([128, 4, 128], BF16)
        t_vn = Tok()
        kb.memset('dve', vn[:, :, :], 0.0, w=[t_vn])
        tq = A.alloc([128, 4, 128], F32)
        t_tq = Tok()
        o = A.alloc([128, 4, 128], F32)
        t_o = Tok()
        zt = A.alloc([128, 512], F32)
        t_zt = Tok()
        junk = A.alloc([128, 128], BF16)
        t_junk = Tok()
        ssq = A.alloc([128, 4], F32)
        t_ssq = Tok()
        onb = A.alloc([128, 512], BF16)
        t_onb = Tok()
        oT = A.alloc([128, 4, 128], BF16)
        t_oT = Tok()
        hs = slice(hb * 4, hb * 4 + 4)

        def bc3(ap2, n=128):
            return ap2.unsqueeze(2).to_broadcast([ap2.shape[0], 4, n])

        def m4(ap2):
            return ap2.unsqueeze(1).to_broadcast([128, 4, 128])

        def f2(ap3):
            return ap3.rearrange("p a b -> p (a b)")

        E2 = [E, A.alloc([128, 16], F32)]
        t_E2 = [t_E, Tok()]
        attn2 = [attnT, A.alloc([128, 4, 128], BF16)]
        t_attn2 = [t_attn, Tok()]
        u0b2 = [u0b, A.alloc([128, 4, 128], F32)]
        t_u0b2 = [t_u0b, Tok()]
        w0T2 = [w0T, A.alloc([128, 4, 128], BF16)]
        t_w0T2 = [t_w0T, Tok()]
        kd2 = [kd, A.alloc([128, 4, 128], BF16)]
        t_kd2 = [t_kd, Tok()]

        def gen_pre(t):
            p = t % 2
            E_, tE_ = E2[p], t_E2[p]
            tok = slice(t * 128, (t + 1) * 128)
            g_t = g_all[:, t, hs]
            for j in range(2):
                kb.ts('dve', gm[:, j, :], g_t, P.blk_f32[:, 64 * j:64 * j + 1], None, ALU.mult, r=[t_g, P.t_const],
                      **({'w': [t_gm]} if j == 0 else {'pw': [t_gm]}))
            b0, tb0 = bk[0]
            kb.mm(b0[:, 0:4], P.u2_f32, g_t, r=[t_g, P.t_const], w=[tb0])
            kb.mm(b0[:, 4:8], P.blk_f32, g_t, r=[t_g, P.t_const], pw=[tb0])
            kb.mm(b0[:, 8:16], P.ones_f32, gm[:, :, :].rearrange("p a b -> p (a b)"), r=[t_gm, P.t_const], pw=[tb0])
            kb.copy('dve', pre[:, :], b0[:, 0:16], r=[tb0], w=[t_pre])
            kb.tt('dve', pre[:, 4:8], pre[:, 4:8], pre[:, 0:4], ALU.subtract, r=[t_pre], w=[t_pre])
            kb.act(E_[:, :], pre[:, :], AF.Exp, r=[t_pre], w=[tE_])
            yield
            kb.tt('dve', rhsG[:, :, :], m4(P.u2_f32), bc3(g_t), ALU.mult, r=[t_g, P.t_const], w=[t_rhsG])
            b1, tb1 = bk[1]
            kb.mm(b1[:, :], P.l2_f32, f2(rhsG[:, :, :]), r=[t_rhsG, P.t_const], w=[tb1])
            kb.act(f2(DT[:, :, :]), b1[:, :], AF.Exp, r=[tb1], w=[t_DT])
            yield
            b2, tb2 = bk[2]
            b3, tb3 = bk[3]
            for hh in range(4):
                kb.mm(b2[:, hh * 128:(hh + 1) * 128], kT[:, hh, tok], kT[:, hh, tok], r=[t_qk],
                      **({'w': [tb2]} if hh == 0 else {'pw': [tb2]}))
            for hh in range(4):
                kb.mm(b3[:, hh * 128:(hh + 1) * 128], kT[:, hh, tok], qT[:, hh, tok], r=[t_qk],
                      **({'w': [tb3]} if hh == 0 else {'pw': [tb3]}))
            kb.tt('dve', f2(tmp[:, :, :]), b2[:, :], f2(DT[:, :, :]), ALU.mult, r=[tb2, t_DT], w=[t_tmp])
            kb.tt('dve', tmp[:, :, :], tmp[:, :, :], m4(P.su2_f32), ALU.mult, r=[t_tmp, P.t_const], w=[t_tmp])
            kb.tt('dve', Ys[0][:, :, :], tmp[:, :, :], bc3(negb[:, t, hs]), ALU.mult, r=[t_tmp, t_beta], w=[t_Y[0]])
            yield
            kb.tt('dve', f2(tmp2[:, :, :]), b3[:, :], f2(DT[:, :, :]), ALU.mult, r=[tb3, t_DT], w=[t_tmp2])
            kb.tt('dve', attn2[p][:, :, :], tmp2[:, :, :], m4(P.u2_f32), ALU.mult, r=[t_tmp2, P.t_const], w=[t_attn2[p]])
            b4, tb4 = bk[0]
            b4b = b4[:, :].bitcast(BF16)
            for hh in range(4):
                kb.tr(b4b[:, hh * 128:(hh + 1) * 128], Ys[0][:, hh, :], P.ident[:, :], r=[t_Y[0], P.t_const],
                      **({'w': [tb4]} if hh == 0 else {'pw': [tb4]}))
            kb.copy('act', f2(Xs[0][:, :, :]), b4b[:, 0:512], r=[tb4], w=[t_X[0]])
            kb.tt('dve', Rs[0][:, :, :], Ys[0][:, :, :], m4(P.ident_f32), ALU.add, r=[t_Y[0], P.t_const], w=[t_R[0]])
            yield
            for lv in range(5):
                a, n_ = lv % 2, (lv + 1) % 2
                bx, tbx = bk[1]
                by, tby = bk[2]
                br, tbr = bk[3]
                for hh in range(4):
                    kb.mm(bx[:, hh * 128:(hh + 1) * 128], Ys[a][:, hh, :], Xs[a][:, hh, :], r=[t_X[a], t_Y[a]],
                          **({'w': [tbx]} if hh == 0 else {'pw': [tbx]}))
                kb.copy('act', f2(Xs[n_][:, :, :]), bx[:, :], r=[tbx], w=[t_X[n_]])
                if lv < 4:
                    for hh in range(4):
                        kb.mm(by[:, hh * 128:(hh + 1) * 128], Xs[a][:, hh, :], Ys[a][:, hh, :], r=[t_X[a], t_Y[a]],
                              **({'w': [tby]} if hh == 0 else {'pw': [tby]}))
                    kb.copy('dve', f2(Ys[n_][:, :, :]), by[:, :], r=[tby], w=[t_Y[n_]])
                yield
                for hh in range(4):
                    kb.mm(br[:, hh * 128:(hh + 1) * 128], Xs[n_][:, hh, :], Rs[a][:, hh, :], r=[t_X[n_], t_R[a]],
                          **({'w': [tbr]} if hh == 0 else {'pw': [tbr]}))
                kb.tt('dve', f2(Rs[n_][:, :, :]), br[:, :], f2(Rs[a][:, :, :]), ALU.add, r=[tbr, t_R[a]], w=[t_R[n_]])
                yield
            R, tR = Rs[1], t_R[1]
            kb.tt('dve', keg[:, :, :], k_tm[:, t, :, :], bc3(E_[:, 0:4]), ALU.mult, r=[t_kv, tE_], w=[t_keg])
            kb.tt('dve', kd2[p][:, :, :], k_tm[:, t, :, :], bc3(E_[:, 4:8]), ALU.mult, r=[t_kv, tE_], w=[t_kd2[p]])
            b2, tb2 = bk[0]
            b3, tb3 = bk[1]
            for hh in range(4):
                kb.mm(b2[:, hh * 128:(hh + 1) * 128], R[:, hh, :], v_tm[:, t, hh, :], r=[tR, t_kv],
                      **({'w': [tb2]} if hh == 0 else {'pw': [tb2]}))
            kb.tt('dve', u0b2[p][:, :, :], b2[:, :].rearrange("p (a b) -> p a b", a=4), bc3(beta[:, t, hs]), ALU.mult,
                  r=[tb2, t_beta], w=[t_u0b2[p]])
            yield
            for hh in range(4):
                kb.mm(b3[:, hh * 128:(hh + 1) * 128], keg[:, hh, :], R[:, hh, :], r=[tR, t_keg],
                      **({'w': [tb3]} if hh == 0 else {'pw': [tb3]}))
            kb.copy('act', f2(w0T2[p][:, :, :]), b3[:, :], r=[tb3], w=[t_w0T2[p]])
            yield

        def gen_rec(t):
            p = t % 2
            E_, tE_ = E2[p], t_E2[p]
            tok = slice(t * 128, (t + 1) * 128)
            for j in range(2):
                rows = slice(64 * j, 64 * j + 64)
                ba_, tba = bk[4]
                bq, tbq = bk[5]
                bo, tbo = bk[6]
                bs, tbs = bk[7]
                for hh in range(4):
                    kb.mm(ba_[:, hh * 128:(hh + 1) * 128], w0T2[p][:, hh, :], Sbf[:, hh, :], r=[t_w0T2[p], t_Sbf],
                          **({'w': [tba]} if hh == 0 else {'pw': [tba]}))
                for hh in range(4):
                    kb.mm(bq[:, hh * 128:(hh + 1) * 128], qT[:, hh, tok], Sbf[:, hh, :], r=[t_qk, t_Sbf],
                          **({'w': [tbq]} if hh == 0 else {'pw': [tbq]}))
                kb.tt('dve', tq[rows, :, :], ba_[rows, :].rearrange("p (a b) -> p a b", a=4), bc3(negb[rows, t, hs]), ALU.mult,
                      r=[tba, t_beta], w=[t_tq])
                kb.tt('dve', vn[rows, :, :], tq[rows, :, :], u0b2[p][rows, :, :], ALU.add, r=[t_tq, t_u0b2[p]], w=[t_vn])
                yield
                for hh in range(4):
                    kb.mm(bo[:, hh * 128:(hh + 1) * 128], attn2[p][rows, hh, :], vn[rows, hh, :], r=[t_attn2[p], t_vn],
                          **({'w': [tbo]} if hh == 0 else {'pw': [tbo]}))
                for hh in range(4):
                    kb.mm(bs[:, hh * 128:(hh + 1) * 128], kd2[p][rows, hh, :], vn[rows, hh, :], r=[t_kd2[p], t_vn],
                          **({'w': [tbs]} if hh == 0 else {'pw': [tbs]}))
                kb.tt('dve', S[:, :, :], S[:, :, :], bc3(E_[:, 8 + 4 * j:12 + 4 * j]), ALU.mult, r=[t_S, tE_], w=[t_S])
                kb.tt('dve', f2(S[:, :, :]), f2(S[:, :, :]), bs[:, :], ALU.add, r=[t_S, tbs], w=[t_S])
                kb.copy('act', Sbf[:, :, :], S[:, :, :], r=[t_S], w=[t_Sbf])
                yield
                kb.tt('dve', tq[rows, :, :], bq[rows, :].rearrange("p (a b) -> p a b", a=4), bc3(E_[rows, 0:4]), ALU.mult,
                      r=[tbq, tE_], w=[t_tq])
                kb.tt('dve', o[rows, :, :], tq[rows, :, :], bo[rows, :].rearrange("p (a b) -> p a b", a=4), ALU.add,
                      r=[t_tq, tbo], **({'w': [t_o]} if j == 0 else {'pw': [t_o]}))
                yield
            kb.dma('sp', zt[:, :], scr['gz'][tok, hb * 512:(hb + 1) * 512], w=[t_zt])
            kb.act(zt[:, :], zt[:, :], AF.Silu, r=[t_zt], w=[t_zt])
            for hh in range(4):
                kb.act(junk[:, :], o[:, hh, :], AF.Square, scale=float(128 ** -0.5), accum_out=ssq[:, hh:hh + 1], r=[t_o],
                       **({'w': [t_junk, t_ssq]} if hh == 0 else {'w': [t_junk], 'pw': [t_ssq]}))
            kb.rsqrt_eps(ssq[:, :], ssq[:, :], t_ssq, t_ssq)
            yield
            kb.tt('dve', o[:, :, :], o[:, :, :], bc3(ssq[:, 0:4]), ALU.mult, r=[t_o, t_ssq], w=[t_o])
            kb.tt('dve', onb[:, :], f2(o[:, :, :]), zt[:, :], ALU.mult, r=[t_o, t_zt], w=[t_onb])
            b1, tb1 = bk[4]
            b1b = b1[:, :].bitcast(BF16)
            for hh in range(4):
                kb.tr(b1b[:, hh * 128:(hh + 1) * 128], onb[:, hh * 128:(hh + 1) * 128], P.ident[:, :], r=[t_onb, P.t_const],
                      **({'w': [tb1]} if hh == 0 else {'pw': [tb1]}))
            kb.copy('act', f2(oT[:, :, :]), b1b[:, 0:512], r=[tb1], w=[t_oT])
            r0 = 2048 + hb * 512
            kb.dma('sp', scr['obT'][r0:r0 + 512, tok].rearrange("(j p) t -> p j t", p=128), oT[:, :, :], r=[t_oT])
            yield

        for _ in gen_pre(0):
            pass
        for t in range(T):
            gr = gen_rec(t)
            gp = gen_pre(t + 1) if t + 1 < T else iter(())
            alive = [True, True]
            while alive[0] or alive[1]:
                for k_ in range(2):
                    if alive[1]:
                        try:
                            next(gp)
                        except StopIteration:
                            alive[1] = False
                if alive[0]:
                    try:
                        next(gr)
                    except StopIteration:
                        alive[0] = False
    kb.s.barrier()
    A.off = base0


PK_NMIX, PK_DALAM, PK_NSB, PK_NFFN, PK_SSD, PK_GDN, PK_FFNC, PK_W = 0, 16, 280, 304, 320, 1472, 1600, 2048


def phase_merge(kb, A, P, L, scr, w_branch, nsb, t_pk, mT, t_mT):
    HL = 1024 if L >= 1024 else 512
    ob = A.alloc([128, 24, HL], BF16)
    t_ob = Tok('ob')
    G = GemmRes(kb, A, P.next_bank, elems=8 * 512)
    gt = [A.alloc([128, 4, HL], BF16) for _ in range(2)]
    t_gt = [Tok(), Tok()]
    acc = A.alloc([128, 4, HL], F32)
    t_acc = [Tok() for _ in range(4)]
    tmp = [A.alloc([128, 512], F32)] * 2
    t_tmp = [Tok()] * 2
    st = {'gi': 0, 'ti': 0}

    def cons(ps, tps, tl, sub, tb):
        ft, b, th = tl['ft'], tl['b'], tl['th']
        if sub == 0 and tb == 0:
            st['gi'] += 1
            gi = st['gi'] % 2
            r0 = b * 2048 + ft * 512
            kb.dma('sp', gt[gi][:, :, :], scr['gatesT'][r0:r0 + 512, th * HL:(th + 1) * HL].rearrange("(s p) t -> p s t", p=128),
                   w=[t_gt[gi]])
        gi = st['gi'] % 2
        cs = slice(tb * 512, (tb + 1) * 512)
        if b == 0:
            kb.tt('dve', acc[:, sub, cs], ps[:, :], gt[gi][:, sub, cs], ALU.mult, r=[tps, t_gt[gi]], w=[t_acc[sub]])
        else:
            st['ti'] += 1
            ti = st['ti'] % 2
            kb.tt('dve', tmp[ti][:, :], ps[:, :], gt[gi][:, sub, cs], ALU.mult, r=[tps, t_gt[gi]], w=[t_tmp[ti]])
            if b == 1:
                kb.tt('dve', acc[:, sub, cs], acc[:, sub, cs], tmp[ti][:, :], ALU.add, r=[t_tmp[ti], t_acc[sub]], w=[t_acc[sub]])
            else:
                kb.tt('dve', mT[:, ft * 4 + sub, th * HL + tb * 512:th * HL + (tb + 1) * 512], acc[:, sub, cs], tmp[ti][:, :],
                      ALU.add, r=[t_tmp[ti], t_acc[sub]], pw=[t_mT])

    def load_ob(th):
        def f():
            for b in range(3):
                kb.dma('sp', ob[:, b * 8:(b + 1) * 8, :],
                       scr['obT'][b * 1024:(b + 1) * 1024, th * HL:(th + 1) * HL].rearrange("(kc p) t -> p kc t", p=128),
                       **({'w': [t_ob]} if b == 0 else {'pw': [t_ob]}))
        return f

    tiles = []
    for th in range(L // HL):
        for ft in range(4):
            for b in range(3):
                tiles.append(dict(W=w_branch, k0=b * 1024, KC=8, c0=ft * 512, width=512, mode='fm', nscale=nsb[:, b * 8:(b + 1) * 8],
                                  t_nscale=t_pk, consume=cons, kx0=b * 8, ft=ft, b=b, th=th,
                                  cast=('act' if (ft * 3 + b) % 2 else 'pool'),
                                  pre=(load_ob(th) if (ft == 0 and b == 0) else None)))
    run_gemms(kb, G, ob, t_ob, HL, tiles)


def accum_consumer(kb, A, h_dst, t_h, wd):
    hx = [A.alloc([128, wd], F32) for _ in range(8)]
    t_hx = [Tok() for _ in range(8)]
    st = {'i': 0}

    def cons(ps, tps, tl, sub, t):
        i = st['i'] % 8
        st['i'] += 1
        j = tl['j']
        key = (t, j * wd)
        if key not in t_h:
            t_h[key] = Tok()
        rs, cs = slice(t * 128, (t + 1) * 128), slice(j * wd, (j + 1) * wd)
        kb.copy('dve' if i % 2 == 0 else 'act', hx[i][:, :], ps[:, 0:wd], r=[tps], w=[t_hx[i]])
        kb.s.op('pool', lambda e, i=i, rs=rs, cs=cs: e.dma_start(out=h_dst[rs, cs], in_=hx[i][:, :], accum_op=ALU.add),
                r=[t_hx[i]], w=[t_h[key]], dma=True)
    return cons


def phase_outproj(kb, A, P, L, w_out, mT, t_mT, h_src, h_dst):
    G = GemmRes(kb, A, P.next_bank, elems=16 * 512)
    t_h = {}
    cons = accum_consumer(kb, A, h_dst, t_h, 512)
    tiles = [dict(W=w_out, k0=0, KC=16, c0=j * 512, width=512, mode='tm', nscale=None, consume=cons, j=j, cast='act') for j in range(4)]
    run_gemms(kb, G, mT, t_mT, L, tiles)


def phase_ffn(kb, A, P, L, ffn_up, ffn_down, xT, t_xT, nffn, fc, t_pk, h):
    NCP = 8
    aT = A.alloc([128, NCP, L], BF16)
    t_aT = Tok('aT')
    G = GemmRes(kb, A, P.next_bank, elems=16 * 256)
    ub = [A.alloc([128, 2 + L], F32) for _ in range(2)]
    t_ub = [Tok(), Tok()]
    yv = [A.alloc([128, L], F32) for _ in range(2)]
    t_yv = [Tok(), Tok()]
    t_h = {}
    cons_down = accum_consumer(kb, A, h, t_h, 512)
    for b in range(2):
        kb.memset('dve', ub[b][:, 0:2], 0.0, w=[t_ub[b]])

    def cons_up(ps, tps, tl, sub, tb):
        kb.copy('act', ub[sub][:, 2 + tb * 512:2 + (tb + 1) * 512], ps[:, :], r=[tps], pw=[t_ub[sub]])
        if tb == L // 512 - 1:
            c = tl['chunk'] + (44 if sub == 1 else 0)
            w0, w1, w2, bb = [fc[:, c * 4 + j:c * 4 + j + 1] for j in range(4)]
            kb.ts('dve', yv[sub][:, :], ub[sub][:, 2:2 + L], w2, bb, ALU.mult, ALU.add, r=[t_ub[sub], t_pk], w=[t_yv[sub]])
            kb.stt('dve', yv[sub][:, :], ub[sub][:, 1:1 + L], w1, yv[sub][:, :], ALU.mult, ALU.add, r=[t_ub[sub], t_pk, t_yv[sub]], w=[t_yv[sub]])
            kb.stt('dve', yv[sub][:, :], ub[sub][:, 0:L], w0, yv[sub][:, :], ALU.mult, ALU.add, r=[t_ub[sub], t_pk, t_yv[sub]], w=[t_yv[sub]])
            if sub == 0:
                kb.act(yv[0][:, :], yv[0][:, :], AF.Silu, r=[t_yv[0]], w=[t_yv[0]])
            else:
                kb.tt('dve', aT[:, tl['cl'], :], yv[0][:, :], yv[1][:, :], ALU.mult, r=[t_yv[0], t_yv[1]], pw=[t_aT])

    tiles = []
    for c0 in range(0, 44, NCP):
        ncl = min(NCP, 44 - c0)
        for cl in range(ncl):
            ch = c0 + cl
            tiles.append(dict(W=ffn_up, k0=0, KC=16, c0=0, width=256, segs=[(ch * 128, 128), (DFF + ch * 128, 128)], mode='fm',
                              nscale=nffn, t_nscale=t_pk, consume=cons_up, chunk=ch, cl=cl))
        tiles += [dict(W=ffn_down, k0=c0 * 128, KC=ncl, c0=j * 512, width=512, mode='tm', nscale=None, consume=cons_down, j=j,
                       cast='act', xT=aT, t_xT=t_aT) for j in range(4)]
    run_gemms(kb, G, xT, t_xT, L, tiles)


def phase_final(kb, A, P, L, h, out, wrep_d):
    wrep = A.alloc([128, D], F32)
    t_w = Tok()
    kb.dma('sp', wrep[:, :], wrep_d[:, :], w=[t_w])
    ld = [A.alloc([128, D], F32) for _ in range(2)]
    t_ld = [Tok(), Tok()]
    junk = A.alloc([128, D], BF16)
    t_junk = Tok()
    ssq = [A.alloc([128, 1], F32) for _ in range(2)]
    t_ssq = [Tok(), Tok()]
    for t in range(L // 128):
        b = t % 2
        kb.dma('sp', ld[b][:, :], h[t * 128:(t + 1) * 128, :], w=[t_ld[b]])
        kb.act(junk[:, :], ld[b][:, :], AF.Square, scale=float(D ** -0.5), accum_out=ssq[b][:, :], r=[t_ld[b]], w=[t_junk, t_ssq[b]])
        kb.rsqrt_eps(ssq[b][:, :], ssq[b][:, :], t_ssq[b], t_ssq[b])
        kb.stt('dve', ld[b][:, :], ld[b][:, :], ssq[b][:, :], wrep[:, :], ALU.mult, ALU.mult, r=[t_ld[b], t_ssq[b], t_w], w=[t_ld[b]])
        kb.dma('sp', out[t * 128:(t + 1) * 128, :], ld[b][:, :], r=[t_ld[b]])


ARENA_BASE, ARENA_LIMIT = 16640, 229376


def pack_layer(l, p):
    pk = np.zeros((128, PK_W), np.float32)
    f = lambda a: np.asarray(a, np.float32)
    pk[:, PK_NMIX:PK_NMIX + 16] = f(p['norm_mix'][l]).reshape(16, 128).T
    pk[:, PK_DALAM:PK_DALAM + 256] = f(p['da_lambda'][l]).reshape(1, 256)
    pk[:, PK_DALAM + 256] = f(p['da_subln'][l])
    nsb = np.ones((24, 128), np.float32)
    nsb[8:16] = f(p['ssm_norm'][l]).reshape(8, 128)
    nsb[16:24] = f(p['gdn_norm'][l])[None, :]
    pk[:, PK_NSB:PK_NSB + 24] = nsb.T
    pk[:, PK_NFFN:PK_NFFN + 16] = f(p['norm_ffn'][l]).reshape(16, 128).T
    cw, cb = f(p['ssm_conv_w'][l]), f(p['ssm_conv_b'][l])
    for cc in range(12):
        pk[:, PK_SSD + cc * 4:PK_SSD + cc * 4 + 4] = cw[:, cc * 128:(cc + 1) * 128].T
        pk[:, PK_SSD + 48 + cc] = cb[cc * 128:(cc + 1) * 128]
    pk[:, PK_SSD + 64:PK_SSD + 80] = f(p['ssm_dt_bias'][l])[None, :]
    pk[:, PK_SSD + 80:PK_SSD + 96] = f(p['ssm_a_log'][l])[None, :]
    pk[:, PK_SSD + 96:PK_SSD + 1120] = np.repeat(f(p['ssm_d'][l]), 64)[None, :]
    gw = f(p['gdn_conv_w'][l])
    for cc in range(24):
        pk[:, PK_GDN + cc * 4:PK_GDN + cc * 4 + 4] = gw[:, cc * 128:(cc + 1) * 128].T
    pk[:, PK_GDN + 96:PK_GDN + 104] = f(p['gdn_dt_bias'][l])[None, :]
    pk[:, PK_GDN + 104:PK_GDN + 112] = f(p['gdn_a_log'][l])[None, :]
    fw, fb = f(p['ffn_conv_w'][l]), f(p['ffn_conv_b'][l])
    for c in range(88):
        pk[:, PK_FFNC + c * 4:PK_FFNC + c * 4 + 3] = fw[:, c * 128:(c + 1) * 128].T
        pk[:, PK_FFNC + c * 4 + 3] = fb[c * 128:(c + 1) * 128]
    return pk


def build_program(L, depth=2, scr_kind="Internal"):
    nc = bass.Bass("TRN2", target_bir_lowering=False)
    kb = KB(nc)
    EI = "ExternalInput"
    x = kb.dram('x', [L, D], F32, EI)
    w_in = kb.dram('w_in', [depth, D, 15904], F32, EI)
    w_branch = kb.dram('w_branch', [depth, 3072, D], F32, EI)
    w_out = kb.dram('w_out', [depth, D, D], F32, EI)
    ffn_up = kb.dram('ffn_up', [depth, D, 2 * DFF], F32, EI)
    ffn_down = kb.dram('ffn_down', [depth, DFF, D], F32, EI)
    pkd = kb.dram('pk', [depth, 128, PK_W], F32, EI)
    cd = kb.dram('consts', [128, 1152], F32, EI)
    augd = kb.dram('c_aug', [8, 2, 3, L], BF16, EI)
    wfin = kb.dram('wfin', [128, D], F32, EI)
    out = kb.dram('out', [L, D], F32, "ExternalOutput")
    h = kb.dram('h_scr', [L, D], F32, scr_kind)
    scr = make_scratch(kb, L, scr_kind)
    P = Persist(kb, None)
    A = Arena(kb, ARENA_BASE, ARENA_LIMIT)
    setup_consts(kb, A, P, cd)
    pks = A.alloc([128, PK_W], F32)
    t_pk = Tok('pk')
    par = A.alloc([128, 8], F32)
    t_par = Tok('par')
    A.base = A.off
    S = kb.s
    for a in range(0, L, 512):
        kb.dma('sp', h[a:a + 512, :], x[a:a + 512, :])
    for l in range(depth):
        lambda_init = 0.8 - 0.6 * float(np.exp(-0.3 * l))
        S.barrier()
        A.reset()
        kb.dma('sp', pks[:, :], pkd[l, :, :], w=[t_pk])
        h_src = h
        xT = A.alloc([128, 16, L], BF16)
        t_xT = Tok('xT')
        mark = A.off
        phase_norm_T(kb, A, P, h_src, xT, t_xT, L)
        S.barrier()
        A.off = mark
        phase_inproj(kb, A, P, xT, t_xT, L, w_in[l], pks[:, PK_NMIX:PK_NMIX + 16], t_pk, scr)
        S.barrier()
        A.reset()
        prep_da_params(kb, A, pks[:, PK_DALAM:PK_DALAM + 257], t_pk, par, t_par, lambda_init)
        phase_da(kb, A, P, L, scr, augd, par[:, 0:1], par[:, 1:2], t_par, lambda_init)
        S.barrier()
        A.reset()
        phase_ssd(kb, A, P, L, scr, pks[:, PK_SSD:PK_SSD + 1152], t_pk)
        S.barrier()
        A.reset()
        phase_gdn(kb, A, P, L, scr, pks[:, PK_GDN:PK_GDN + 128], t_pk)
        S.barrier()
        A.reset()
        mT = A.alloc([128, 16, L], BF16)
        t_mT = Tok('mT')
        mark = A.off
        phase_merge(kb, A, P, L, scr, w_branch[l], pks[:, PK_NSB:PK_NSB + 24], t_pk, mT, t_mT)
        S.barrier()
        A.off = mark
        phase_outproj(kb, A, P, L, w_out[l], mT, t_mT, h_src, h)
        S.barrier()
        A.reset()
        xT = A.alloc([128, 16, L], BF16)
        t_xT = Tok('xT2')
        mark = A.off
        phase_norm_T(kb, A, P, h, xT, t_xT, L)
        S.barrier()
        A.off = mark
        phase_ffn(kb, A, P, L, ffn_up[l], ffn_down[l], xT, t_xT, pks[:, PK_NFFN:PK_NFFN + 16], pks[:, PK_FFNC:PK_FFNC + 352], t_pk, h)
    S.barrier()
    A.reset()
    phase_final(kb, A, P, L, h, out, wfin)
    cnt = kb.s.finalize_and_emit()
    return nc, cnt


_CACHE = {}


def kernel(**inputs):
    p = {k: np.asarray(v) for k, v in inputs.items()}
    x = p['x']
    B, L, _ = x.shape
    depth = p['w_in'].shape[0]
    key = (L, depth)
    if key not in _CACHE:
        _CACHE[key] = build_program(L, depth)
    nc, _ = _CACHE[key]
    pk = np.stack([pack_layer(l, p) for l in range(depth)])
    consts = host_consts()
    aug = host_da_aug(L)
    wfin = np.ascontiguousarray(np.broadcast_to(p['norm_final'].astype(np.float32)[None, :], (128, D)))
    shared = dict(w_in=np.ascontiguousarray(p['w_in'], np.float32), w_branch=np.ascontiguousarray(p['w_branch'], np.float32),
                  w_out=np.ascontiguousarray(p['w_out'], np.float32), ffn_up=np.ascontiguousarray(p['ffn_up'], np.float32),
                  ffn_down=np.ascontiguousarray(p['ffn_down'], np.float32), pk=pk, consts=consts, c_aug=aug, wfin=wfin)
    in_maps = [dict(shared, x=np.ascontiguousarray(x[b], np.float32)) for b in range(B)]
    res = run_bass_kernel_spmd(nc, in_maps, core_ids=list(range(B)))
    return np.stack([np.asarray(r['out'], np.float32) for r in res.results]).astype(np.float32)
```
